# Optimizing a Trainium2 kernel written in Bass

```python
import jax, jax.numpy as jnp
from jax import lax
import numpy as np

D_MODEL = 1024
BATCH = 8
SEQ = 2048
DEPTH = 2

HEAD_DIM = 64
NA_HEADS = 8
GQA_Q_HEADS = 8
GQA_KV_HEADS = 2
GRID_W = 64
NA_WIN_ROWS = 8
NA_WIN_COLS = 16
SWA_WINDOW = 128
SWA_BLOCK = 128
ROPE_THETA = 10000.0
PEER_HEADS = 8
PEER_N_KEYS = 128
PEER_N_EXPERTS = PEER_N_KEYS * PEER_N_KEYS
PEER_TOPK = 16
PEER_KEY_DIM = 256
PEER_CHUNK = 128
N_BRANCH = 2
EPS = 1e-6

NA_WIDTH = NA_HEADS * HEAD_DIM
GQA_Q_WIDTH = GQA_Q_HEADS * HEAD_DIM
GQA_KV_WIDTH = GQA_KV_HEADS * HEAD_DIM
IN_SPLITS = (NA_WIDTH, 2 * NA_WIDTH, 3 * NA_WIDTH,
             3 * NA_WIDTH + GQA_Q_WIDTH,
             3 * NA_WIDTH + GQA_Q_WIDTH + GQA_KV_WIDTH,
             3 * NA_WIDTH + GQA_Q_WIDTH + 2 * GQA_KV_WIDTH)
IN_COLS = IN_SPLITS[-1] + N_BRANCH * D_MODEL

kernel_name = "hybrid_natten_swa_peer_encoder"


def rms_norm(x, g):
    xf = x.astype(jnp.float32)
    y = xf * lax.rsqrt(jnp.mean(xf * xf, axis=-1, keepdims=True) + EPS)
    return (y * g.astype(jnp.float32)).astype(x.dtype)


def rotary(x, pos):
    half = x.shape[-1] // 2
    inv = ROPE_THETA ** (-jnp.arange(half, dtype=jnp.float32) / half)
    ang = pos.astype(jnp.float32)[:, None] * inv[None, :]
    cos = jnp.cos(ang)[None, :, None, :]
    sin = jnp.sin(ang)[None, :, None, :]
    xf = x.astype(jnp.float32)
    x1, x2 = xf[..., :half], xf[..., half:]
    out = jnp.concatenate([x1 * cos - x2 * sin, x1 * sin + x2 * cos], axis=-1)
    return out.astype(x.dtype)


def neighbourhood_attention(q, k, v, rpb):
    B, S, H, dh = q.shape
    rows = S // GRID_W
    kr = min(NA_WIN_ROWS, rows)
    r = jnp.arange(rows)
    rs = jnp.clip(r - kr // 2, 0, rows - kr)
    key_rows = rs[:, None] + jnp.arange(kr)[None, :]
    c = jnp.arange(GRID_W)
    cs = jnp.clip(c - NA_WIN_COLS // 2, 0, GRID_W - NA_WIN_COLS)
    qg = q.reshape(B, rows, GRID_W, H, dh)
    kg = k.reshape(B, rows, GRID_W, H, dh)[:, key_rows]
    vg = v.reshape(B, rows, GRID_W, H, dh)[:, key_rows]
    s = jnp.einsum('brqhd,brakhd->brhqak', qg, kg).astype(jnp.float32) * (dh ** -0.5)
    dr_idx = key_rows - r[:, None] + (NA_WIN_ROWS - 1)
    dc = c[None, :] - c[:, None]
    dc_idx = jnp.clip(dc + NA_WIN_COLS - 1, 0, 2 * NA_WIN_COLS - 2)
    in_win = (c[None, :] >= cs[:, None]) & (c[None, :] < cs[:, None] + NA_WIN_COLS)
    bias = rpb.astype(jnp.float32)[:, dr_idx[:, None, :, None], dc_idx[None, :, None, :]]
    bias = jnp.where(in_win[:, None, :], jnp.transpose(bias, (1, 0, 2, 3, 4)), -jnp.inf)
    p = jax.nn.softmax(s + bias[None], axis=(-2, -1))
    o = jnp.einsum('brhqak,brakhd->brqhd', p.astype(v.dtype), vg)
    return o.reshape(B, S, H * dh)


def sliding_window_gqa(q, k, v, sink):
    B, S, Hq, dh = q.shape
    Hkv = k.shape[2]
    G = Hq // Hkv
    nb = S // SWA_BLOCK
    pad = ((0, 0), (SWA_BLOCK, SWA_BLOCK), (0, 0), (0, 0))
    kp = jnp.pad(k, pad).reshape(B, nb + 2, SWA_BLOCK, Hkv, dh)
    vp = jnp.pad(v, pad).reshape(B, nb + 2, SWA_BLOCK, Hkv, dh)
    kw = jnp.concatenate([kp[:, :-2], kp[:, 1:-1], kp[:, 2:]], axis=2)
    vw = jnp.concatenate([vp[:, :-2], vp[:, 1:-1], vp[:, 2:]], axis=2)
    qb = q.reshape(B, nb, SWA_BLOCK, Hkv, G, dh)
    s = jnp.einsum('bnqkgd,bnckd->bnkgqc', qb, kw).astype(jnp.float32) * (dh ** -0.5)
    blk = jnp.arange(nb)[:, None]
    qpos = blk * SWA_BLOCK + jnp.arange(SWA_BLOCK)[None, :]
    kpos = (blk - 1) * SWA_BLOCK + jnp.arange(3 * SWA_BLOCK)[None, :]
    valid = ((kpos >= 0) & (kpos < S))[:, None, :] & \
        (jnp.abs(qpos[:, :, None] - kpos[:, None, :]) <= SWA_WINDOW)
    s = jnp.where(valid[None, :, None, None], s, -jnp.inf)
    sink_l = jnp.broadcast_to(sink.astype(jnp.float32).reshape(Hkv, G)[None, None, :, :, None, None],
                              s.shape[:-1] + (1,))
    p = jax.nn.softmax(jnp.concatenate([s, sink_l], axis=-1), axis=-1)[..., :-1]
    o = jnp.einsum('bnkgqc,bnckd->bnqkgd', p.astype(v.dtype), vw)
    return o.reshape(B, S, Hq * dh)


def token_mixer(h, w_in, gate_bias, qk_norm, na_rpb, swa_sink, w_branch_na, w_branch_swa, w_out, pos):
    B, S, _ = h.shape
    proj = h @ w_in
    qa, ka, va, qb, kb, vb, gl = jnp.split(proj, IN_SPLITS, axis=-1)
    heads = lambda t, n: t.reshape(B, S, n, HEAD_DIM)
    qa = rms_norm(heads(qa, NA_HEADS), qk_norm[0])
    ka = rms_norm(heads(ka, NA_HEADS), qk_norm[1])
    va = heads(va, NA_HEADS)
    qb = rotary(rms_norm(heads(qb, GQA_Q_HEADS), qk_norm[2]), pos)
    kb = rotary(rms_norm(heads(kb, GQA_KV_HEADS), qk_norm[3]), pos)
    vb = heads(vb, GQA_KV_HEADS)
    oa = neighbourhood_attention(qa, ka, va, na_rpb)
    ob = sliding_window_gqa(qb, kb, vb, swa_sink)
    gates = jax.nn.sigmoid((gl.reshape(B, S, N_BRANCH, D_MODEL) + gate_bias).astype(jnp.float32)).astype(h.dtype)
    merged = gates[:, :, 0] * (oa @ w_branch_na) + gates[:, :, 1] * (ob @ w_branch_swa)
    return merged @ w_out


def peer_ffn(x, w_query, sub_keys, expert_down, expert_up):
    B, S, D = x.shape
    T = B * S
    half = PEER_KEY_DIM // 2

    def chunk(xc):
        q = (xc @ w_query).reshape(PEER_CHUNK, PEER_HEADS, 2, half)
        s = jnp.einsum('thpk,pnk->thpn', q, sub_keys).astype(jnp.float32)
        sv, si = lax.top_k(s, PEER_TOPK)
        cand = (sv[:, :, 0, :, None] + sv[:, :, 1, None, :]).reshape(PEER_CHUNK, PEER_HEADS, PEER_TOPK * PEER_TOPK)
        cidx = (si[:, :, 0, :, None] * PEER_N_KEYS + si[:, :, 1, None, :]).reshape(PEER_CHUNK, PEER_HEADS, PEER_TOPK * PEER_TOPK)
        top_s, sel = lax.top_k(cand, PEER_TOPK)
        eidx = jnp.take_along_axis(cidx, sel, axis=-1)
        g = jax.nn.softmax(top_s, axis=-1)
        u = expert_down[eidx]
        vv = expert_up[eidx]
        act = jax.nn.gelu(jnp.einsum('td,thkd->thk', xc, u).astype(jnp.float32), approximate=False)
        return jnp.einsum('thk,thkd->td', (g * act).astype(x.dtype), vv)

    out = lax.map(chunk, x.reshape(T // PEER_CHUNK, PEER_CHUNK, D))
    return out.reshape(B, S, D)


def setup_inputs(seed: int = 0) -> dict:
    key = jax.random.key(seed)
    ks = jax.random.split(key, 16)
    f32 = jnp.float32

    def nrm(k, shape, scale):
        return jax.random.normal(k, shape, f32) * scale

    return {
        "x": nrm(ks[0], (BATCH, SEQ, D_MODEL), 1.0),
        "norm_mix": 1.0 + nrm(ks[1], (DEPTH, D_MODEL), 0.02),
        "w_in": nrm(ks[2], (DEPTH, D_MODEL, IN_COLS), D_MODEL ** -0.5),
        "gate_bias": nrm(ks[3], (DEPTH, N_BRANCH, D_MODEL), 0.02),
        "qk_norm": 1.0 + nrm(ks[4], (DEPTH, 4, HEAD_DIM), 0.02),
        "na_rpb": nrm(ks[5], (DEPTH, NA_HEADS, 2 * NA_WIN_ROWS - 1, 2 * NA_WIN_COLS - 1), 0.1),
        "swa_sink": nrm(ks[6], (DEPTH, GQA_Q_HEADS), 0.5),
        "w_branch_na": nrm(ks[7], (DEPTH, NA_WIDTH, D_MODEL), NA_WIDTH ** -0.5),
        "w_branch_swa": nrm(ks[8], (DEPTH, GQA_Q_WIDTH, D_MODEL), GQA_Q_WIDTH ** -0.5),
        "w_out": nrm(ks[9], (DEPTH, D_MODEL, D_MODEL), D_MODEL ** -0.5),
        "norm_ffn": 1.0 + nrm(ks[10], (DEPTH, D_MODEL), 0.02),
        "peer_query": nrm(ks[11], (DEPTH, D_MODEL, PEER_HEADS * PEER_KEY_DIM), D_MODEL ** -0.5),
        "peer_sub_keys": nrm(ks[12], (DEPTH, 2, PEER_N_KEYS, PEER_KEY_DIM // 2), (PEER_KEY_DIM // 2) ** -0.5),
        "peer_down": nrm(ks[13], (DEPTH, PEER_N_EXPERTS, D_MODEL), D_MODEL ** -0.5),
        "peer_up": nrm(ks[14], (DEPTH, PEER_N_EXPERTS, D_MODEL), (PEER_HEADS * PEER_TOPK) ** -0.5),
    }


def reference(x, norm_mix, w_in, gate_bias, qk_norm, na_rpb, swa_sink, w_branch_na, w_branch_swa,
              w_out, norm_ffn, peer_query, peer_sub_keys, peer_down, peer_up):
    S = x.shape[1]
    pos = jnp.arange(S, dtype=jnp.int32)
    for l in range(DEPTH):
        x = x + token_mixer(rms_norm(x, norm_mix[l]), w_in[l], gate_bias[l], qk_norm[l], na_rpb[l],
                            swa_sink[l], w_branch_na[l], w_branch_swa[l], w_out[l], pos)
        x = x + peer_ffn(rms_norm(x, norm_ffn[l]), peer_query[l], peer_sub_keys[l], peer_down[l], peer_up[l])
    return x
```

```python
import numpy as np
import concourse.bass as bass
import concourse.mybir as mybir
from concourse.bass_utils import run_bass_kernel_spmd
from contextlib import ExitStack

F32 = mybir.dt.float32
BF16 = mybir.dt.bfloat16
I32 = mybir.dt.int32
U32 = mybir.dt.uint32
ALU = mybir.AluOpType
AF = mybir.ActivationFunctionType
AX = mybir.AxisListType

D = 1024
SEQ = 2048
NT = 16
DEPTH = 2
INC = 4352
NEXP = 16384
EPS = 1e-6
NEG = -30000.0


class T:
    def __init__(self, t, name):
        self.t = t
        self.name = name
        self.writers = []
        self.readers = []
        self.group = None
        self.sem = None
        self.cnt = 0

    def __getitem__(self, idx):
        return self.t[idx]


class Op:
    __slots__ = ("eng", "fn", "deps", "dma", "sem", "val", "needed")

    def __init__(self, eng, fn, dma):
        self.eng = eng
        self.fn = fn
        self.dma = dma
        self.deps = []
        self.sem = None
        self.val = 0
        self.needed = False


class Sched:
    ENGS = ["pe", "act", "dve", "pool", "sp"]

    def __init__(self, nc, es, same_engine_sync=True):
        self.nc = nc
        self.es = es
        self.prog = {e: [] for e in self.ENGS}
        self.esem = {e: es.enter_context(nc.semaphore("sem_" + e)) for e in ["pe", "act", "dve", "pool"]}
        self.ses = same_engine_sync
        self.pending_dma = []
        self.semcache = {}

    def sb(self, name, shape, dt):
        return T(self.es.enter_context(self.nc.sbuf_tensor(name, list(shape), dt)), name)

    def sb_at(self, name, shape, dt, offset):
        return T(self.nc.alloc_sbuf_tensor_at(name, list(shape), dt, offset=offset), name)

    def ps(self, name, shape, dt):
        return T(self.es.enter_context(self.nc.psum_tensor(name, list(shape), dt)), name)

    def dram(self, name, shape, dt, kind="Internal"):
        return T(self.nc.dram_tensor(name, list(shape), dt, kind=kind), name)

    def token(self, t, name):
        return T(t, name)

    def add(self, eng, fn, reads=(), writes=(), dma=False, group=None, semkey=None, nobar=False):
        op = Op(eng, fn, dma)
        deps = []
        for t in reads:
            deps.extend(t.writers)
        for t in writes:
            if group is not None and t.group == group:
                continue
            deps.extend(t.writers)
            deps.extend(t.readers)
        seen = set()
        for d in deps:
            if d is op or id(d) in seen:
                continue
            seen.add(id(d))
            op.deps.append(d)
        for t in writes:
            if group is not None and t.group == group:
                t.writers.append(op)
            else:
                t.writers = [op]
                t.readers = []
                t.group = group
        wset = set(id(t) for t in writes)
        for t in reads:
            if id(t) not in wset:
                t.readers.append(op)
        if dma:
            t0 = writes[0]
            if t0.sem is None:
                key = semkey if semkey is not None else ("t", id(t0))
                if key not in self.semcache:
                    self.semcache[key] = [self.es.enter_context(self.nc.semaphore("dsem%d" % len(self.semcache))), 0]
                t0.sem = self.semcache[key]
            t0.sem[1] += 16
            op.sem = t0.sem[0]
            op.val = t0.sem[1]
            if not nobar:
                self.pending_dma.append(op)
        self.prog[eng].append(op)
        return op

    def dma(self, out_t, out_ap, in_t, in_ap, eng="sp", group=None, extra_reads=(), semkey=None, nobar=False):
        return self.add(eng, lambda e: e.dma_start(out=out_ap, in_=in_ap), reads=[in_t] + list(extra_reads),
                        writes=[out_t], dma=True, group=group, semkey=semkey, nobar=nobar)

    def barrier(self):
        lasts = []
        for e in ["pe", "act", "dve", "pool"]:
            for op in reversed(self.prog[e]):
                if not op.dma:
                    lasts.append(op)
                    break
        pend = self.pending_dma
        self.pending_dma = []
        tok = T(None, "bar")
        for e in self.ENGS:
            op = Op(e, None, False)
            op.deps = [d for d in lasts if d.eng != e] + list(pend)
            self.prog[e].append(op)

    def emit(self, final_wait_tiles=()):
        nc = self.nc
        for e in self.ENGS:
            for op in self.prog[e]:
                for d in op.deps:
                    if not d.dma:
                        if d.eng == op.eng and (not self.ses or d.eng == "pe"):
                            continue
                        d.needed = True
        for e in ["pe", "act", "dve", "pool"]:
            c = 0
            for op in self.prog[e]:
                if op.dma or op.fn is None:
                    continue
                if op.needed:
                    c += 1
                    op.sem = self.esem[e]
                    op.val = c
        prog = self.prog
        ses = self.ses
        final_ops = []
        for t in final_wait_tiles:
            final_ops.extend(t.writers)

        with nc.Block() as block:

            def make(ename):
                def body(eng):
                    known = {}
                    for op in prog[ename]:
                        need = {}
                        for d in op.deps:
                            if not d.dma and d.eng == ename and (not ses or ename == "pe"):
                                continue
                            if d.sem is None:
                                continue
                            k = id(d.sem)
                            if k not in need or need[k][1] < d.val:
                                need[k] = (d.sem, d.val)
                        for k, (sm, v) in need.items():
                            if known.get(k, 0) >= v:
                                continue
                            eng.wait_ge(sm, v)
                            known[k] = v
                        if op.fn is None:
                            continue
                        ins = op.fn(eng)
                        if op.dma:
                            ins.then_inc(op.sem, 16)
                        elif op.needed:
                            ins.then_inc(op.sem, 1)
                    if ename == "sp":
                        need = {}
                        for d in final_ops:
                            k = id(d.sem)
                            if k not in need or need[k][1] < d.val:
                                need[k] = (d.sem, d.val)
                        for k, (sm, v) in need.items():
                            if known.get(k, 0) >= v:
                                continue
                            eng.wait_ge(sm, v)
                            known[k] = v

                return body

            block.tensor(make("pe"))
            block.scalar(make("act"))
            block.vector(make("dve"))
            block.gpsimd(make("pool"))
            block.sync(make("sp"))


class Ring:
    def __init__(self, tiles):
        self.tiles = tiles
        self.i = 0

    def next(self):
        t = self.tiles[self.i % len(self.tiles)]
        self.i += 1
        return t


def build_program(n_layers=DEPTH, do_attn=True, do_peer=True, dbg=False, peer_idx_only=False, peer_stage=3):
    nc = bass.Bass("TRN2", target_bir_lowering=False)
    es = ExitStack()
    with es:
        S = Sched(nc, es)
        EI = "ExternalInput"
        x_in = S.dram("x", [SEQ, D], F32, EI)
        norm_mix = S.dram("norm_mix", [DEPTH, D], F32, EI)
        norm_ffn = S.dram("norm_ffn", [DEPTH, D], F32, EI)
        gate_bias = S.dram("gate_bias", [DEPTH, 2 * D], F32, EI)
        qk_norm = S.dram("qk_norm", [DEPTH, 256], F32, EI)
        swa_sink = S.dram("swa_sink", [DEPTH, 8], F32, EI)
        w_in = S.dram("w_in", [DEPTH, D, INC], F32, EI)
        w_bna = S.dram("w_bna", [DEPTH, 512, D], F32, EI)
        w_bsw = S.dram("w_bsw", [DEPTH, 512, D], F32, EI)
        w_out = S.dram("w_out", [DEPTH, D, D], F32, EI)
        nab = S.dram("nab", [DEPTH, 5, 8, 128, 640], F32, EI)
        rope_d = S.dram("rope", [SEQ, 64], F32, EI)
        swam_d = S.dram("swam", [128, 384], F32, EI)
        wqT = S.dram("wqT", [DEPTH, 2048, D], F32, EI)
        keysT = S.dram("keysT", [DEPTH, 2, 128, 128], F32, EI)
        ptab = S.dram("ptab", [DEPTH * NEXP, 2 * D], F32, EI)
        ptab16 = S.dram("ptab16", [DEPTH * NEXP, 2 * D], BF16, "Internal")
        out_d = S.dram("out", [SEQ, D], F32, "ExternalOutput")
        gates_d = S.dram("gates_scr", [SEQ, 2 * D], F32, "Internal")
        out_tok = [S.token(out_d.t, "out%d" % i) for i in range(NT)]
        gates_tok = [S.token(gates_d.t, "gts%d" % i) for i in range(NT)]

        ARENA = 126976
        ARENA2 = 36864
        es.enter_context(nc.sbuf_tensor("arena", [128, ARENA // 4], F32))
        es.enter_context(nc.sbuf_tensor("arena2", [128, ARENA2 // 4], F32))
        ARENA_BASE = int(nc.lookup_mloc("arena").addr)
        ARENA2_BASE = int(nc.lookup_mloc("arena2").addr)
        assert ARENA_BASE % 32 == 0 and ARENA2_BASE % 32 == 0

        def AT(name, shape, dt, offset):
            return S.sb_at(name, shape, dt, ARENA_BASE + offset)

        cur2 = [0]

        def A2(name, shape, dt):
            n = int(np.prod(shape[1:])) * (2 if dt == BF16 else 4)
            n = (n + 31) // 32 * 32
            t = S.sb_at(name, shape, dt, ARENA2_BASE + cur2[0])
            cur2[0] += n
            assert cur2[0] <= ARENA2, (name, cur2[0])
            return t

        K = 1024
        qTna = AT("qTna", [128, 4, SEQ], BF16, 0)
        kTna = AT("kTna", [128, 4, SEQ], BF16, 16 * K)
        vna = AT("vna", [128, NT, 512], BF16, 32 * K)
        qTsw = AT("qTsw", [128, 4, SEQ], BF16, 48 * K)
        kTsw = AT("kTsw", [128, 2, SEQ], BF16, 64 * K)
        vsw = AT("vsw", [128, NT, 128], BF16, 72 * K)
        xnT = AT("xnT", [128, 8, SEQ], BF16, 76 * K)
        wst = [AT("wst%d" % i, [128, 8, 512], BF16, 108 * K + i * 8 * K) for i in range(2)]
        Wna = AT("Wna", [128, 4, D], BF16, 76 * K)
        Wsw = AT("Wsw", [128, 4, D], BF16, 84 * K)
        Wout = AT("Wout", [128, 8, D], BF16, 92 * K)
        gt = [AT("gt%d" % i, [128, 2 * D], F32, 108 * K + i * 8 * K) for i in range(2)]
        Wk = AT("Wk", [128, 8, 2048], BF16, 0)
        GG = 4
        uv_r = Ring([AT("UV%d" % i, [128, GG, 2 * D], BF16, 32 * K + i * 16 * K) for i in range(4)])
        s_sb = AT("s_sb", [128, 2048], F32, 96 * K)
        ut_r = Ring([AT("UT%d" % i, [128, D], BF16, 114 * K + i * 2 * K) for i in range(5)])
        POOL_DOTS = False
        PE_DOTS = False
        TT_DOTS = True
        prod_r = Ring([AT("prod%d" % i, [128, D], F32, 104 * K + i * 4 * K) for i in range(5)])
        wq_t = [AT("wq_t%d" % i, [128, D], F32, 104 * K + i * 4 * K) for i in range(2)]
        kT_t = AT("kT_t", [128, 2, 128], F32, 112 * K)

        ident_f = S.sb("ident_f", [128, 128], F32)
        ident = S.sb("ident", [128, 128], BF16)
        epsb = S.sb("epsb", [128, 1], F32)
        iota16i = S.sb("iota16i", [128, 16], I32)
        iota16 = S.sb("iota16", [128, 16], F32)
        gvec = S.sb("gvec", [128, D], F32)
        gq = S.sb("gq", [128, 256], F32)
        gb = S.sb("gb", [128, 2 * D], F32)
        sinkb = S.sb("sinkb", [128, 8], F32)
        nsinkb = S.sb("nsinkb", [128, 8], F32)
        nB = S.sb("nB", [128, 2], F32)
        bnd = S.sb("bnd", [128, 4], F32)
        esc = S.sb("esc", [128, 8], F32)
        onesb = S.sb("onesb", [128, 1], BF16)
        rope = S.sb("rope_sb", [128, NT, 64], F32)
        swam = S.sb("swam_sb", [128, 384], F32)
        xt_r = Ring([S.sb("xt%d" % i, [128, D], F32) for i in range(2)])
        xn_r = Ring([S.sb("xn%d" % i, [128, D], BF16) for i in range(2)])
        junk = S.sb("junk", [128, D], BF16)
        junk2 = S.sb("junk2", [128, D], BF16)
        st_r = Ring([S.sb("st%d" % i, [128, 4], F32) for i in range(4)])
        cur2[0] = 0
        pj_r = Ring([A2("pj%d" % i, [128, 512], F32) for i in range(4)])
        sq_t = A2("sq_t", [128, 512], F32)
        hs_r = Ring([S.sb("hs%d" % i, [128, 16], F32) for i in range(4)])
        qn_r = Ring([A2("qn%d" % i, [128, 512], BF16) for i in range(4)])
        rt_r = Ring([A2("rt%d" % i, [128, 256], F32) for i in range(4)])
        gl_r = Ring([A2("gl%d" % i, [128, 512], F32) for i in range(4)])
        cur2[0] = 0
        bias_r = Ring([A2("nb%d" % i, [128, 640], F32) for i in range(4)])
        lg_r = Ring([A2("lg%d" % i, [128, 640], F32) for i in range(2)])
        P_r = Ring([A2("P%d" % i, [128, 640], BF16) for i in range(2)])
        PT_r = Ring([A2("PT%d" % i, [128, 640], BF16) for i in range(2)])
        ms_r = Ring([S.sb("ms%d" % i, [128, 4], F32) for i in range(4)])
        rsum_r = Ring([S.sb("rsum%d" % i, [128, 24], F32) for i in range(2)])
        osb = A2("osb", [128, 512], BF16)
        oT_r = Ring([A2("oT%d" % i, [128, 4, 128], BF16) for i in range(4)])
        m1 = A2("m1", [128, 512], F32)
        m2 = A2("m2", [128, 512], F32)
        mg = A2("mg", [128, D], BF16)
        mgT = A2("mgT", [128, 8, 128], BF16)
        xo_r = Ring([S.sb("xo%d" % i, [128, D], F32) for i in range(2)])
        cur2[0] = 0
        hf_b = [A2("hf%d" % i, [128, D], F32) for i in range(2)]
        xp = [xt_r.tiles[0], xt_r.tiles[1], A2("xp2", [128, D], F32)]
        eidi_b = [A2("eidi%d" % i, [128, 128], I32) for i in range(3)]
        esm_b = [A2("esmb%d" % i, [128, 128], F32) for i in range(3)]
        d_ring = Ring([A2("dts%d" % i, [128, 4], F32) for i in range(8)])
        w_ring = Ring([A2("wac%d" % i, [128, 4], F32) for i in range(8)])
        dgg_r = Ring([A2("dgg%d" % i, [128, 8, 128], BF16) for i in range(3)])
        pos = A2("pos", [128, 16], U32)
        pab = A2("pab", [128, 32], U32)
        pabf = A2("pabf", [128, 32], F32)
        i12 = A2("i12", [128, 32], F32)
        hT_b = [A2("hT%d" % i, [128, 8, 128], BF16) for i in range(2)]
        sv_t = [A2("sv%d" % i, [128, 16], F32) for i in range(16)]
        si_t = [A2("si%d" % i, [128, 16], U32) for i in range(16)]
        sif_t = [A2("sif%d" % i, [128, 16], F32) for i in range(16)]
        wk_t = [A2("wk128_%d" % i, [128, 128], F32) for i in range(2)]
        cand = A2("cand", [128, 256], F32)
        cand2 = A2("cand2", [128, 256], F32)
        jk256 = A2("jk256", [128, 256], F32)
        tops_t = [A2("tops%d" % i, [128, 16], F32) for i in range(8)]
        esm = A2("esm", [128, 8, 16], F32)
        zs = A2("zs", [128, 8], F32)
        eidf = A2("eidf", [128, 128], F32)

        PSA = S.ps("PSA", [128, 1024], F32)
        gram_r = None
        PSB = S.ps("PSB", [128, 1024], F32)
        PST = [S.ps("PST%d" % i, [128, 1024], BF16) for i in range(2)]
        PSO = S.ps("PSO", [128, 512], F32)
        PSX = S.ps("PSX", [128, 512], F32)
        pst_r = Ring(PST)
        gram_r = Ring([S.token(PSB.t, "gram0"), S.token(PSB.t, "gram1")])
        acc_r = Ring([PSA, PSB, PSO, PSX])

        S.add("pool", lambda e: e.memset(ident_f[:], 0.0), writes=[ident_f])
        S.add("pool", lambda e: e.affine_select(out=ident_f[:], in_=ident_f[:], pattern=[[-1, 128]],
                                                compare_op=ALU.not_equal, fill=1.0, base=0, channel_multiplier=1),
              reads=[ident_f], writes=[ident_f])
        S.add("dve", lambda e: e.tensor_copy(out=ident[:], in_=ident_f[:]), reads=[ident_f], writes=[ident])
        S.add("pool", lambda e: e.memset(epsb[:], EPS), writes=[epsb])
        S.add("pool", lambda e: e.memset(onesb[:], 1.0), writes=[onesb])
        S.add("pool", lambda e: e.iota(out=iota16i[:], pattern=[[1, 16]], base=0, channel_multiplier=0), writes=[iota16i])
        S.add("dve", lambda e: e.tensor_copy(out=iota16[:], in_=iota16i[:]), reads=[iota16i], writes=[iota16])
        S.dma(rope, rope[:], rope_d, rope_d.t.ap().rearrange("(t p) c -> p t c", p=128))
        S.dma(swam, swam[:], swam_d, swam_d[:, :])

        CONV_ROWS = 256
        NCH = NEXP // CONV_ROWS
        conv_todo = list(range(DEPTH * NCH))
        ptab16_tok = [S.token(ptab16.t, "ptab16_l%d" % i) for i in range(DEPTH)]

        def conv_some(k, after=None):
            for _ in range(k):
                if not conv_todo:
                    return
                c = conv_todo.pop(0)
                S.dma(ptab16_tok[c // NCH], ptab16[c * CONV_ROWS:(c + 1) * CONV_ROWS, :], ptab, ptab[c * CONV_ROWS:(c + 1) * CONV_ROWS, :],
                      eng="pool", group="conv", nobar=True, extra_reads=([after] if after is not None else []))

        def conv_flush(upto_layer):
            while conv_todo and conv_todo[0] // NCH <= upto_layer:
                conv_some(1)

        def transposes(src_t, src_aps, dst_t, dst_ap, n, eng="act"):
            pt = pst_r.next()

            def fn(e):
                for i in range(n):
                    ins = e.transpose(out=pt[:, i * 128:(i + 1) * 128], in_=src_aps[i], identity=ident[:])
                return ins

            S.add("pe", fn, reads=[src_t, ident], writes=[pt])
            src = pt[:, 0:n * 128]
            if len(dst_ap.shape) == 3:
                src = src.rearrange("p (c t) -> p c t", t=128)
            if eng == "act":
                S.add("act", lambda e: e.activation(out=dst_ap, in_=src, func=AF.Copy), reads=[pt], writes=[dst_t])
            else:
                S.add("dve", lambda e: e.tensor_copy(out=dst_ap, in_=src), reads=[pt], writes=[dst_t])

        def rmsnorm_tile(xt, gv, out_t, out_ap, out2_t=None, out2_ap=None):
            st = st_r.next()
            S.add("act", lambda e: e.activation(out=junk[:], in_=xt[:], func=AF.Square, accum_out=st[:, 0:1]),
                  reads=[xt], writes=[junk, st])
            S.add("act", lambda e: e.activation(out=st[:, 1:2], in_=st[:, 0:1], func=AF.Sqrt, bias=epsb[:, 0:1],
                                                scale=1.0 / D), reads=[st, epsb], writes=[st])
            S.add("dve", lambda e: e.reciprocal(out=st[:, 2:3], in_=st[:, 1:2]), reads=[st], writes=[st])
            S.add("dve", lambda e: e.scalar_tensor_tensor(out=out_ap, in0=xt[:], scalar=st[:, 2:3], in1=gv[:],
                                                          op0=ALU.mult, op1=ALU.mult),
                  reads=[xt, st, gv], writes=[out_t])
            if out2_t is not None:
                S.add("dve", lambda e: e.tensor_copy(out=out2_ap, in_=out_ap), reads=[out_t], writes=[out2_t])

        def headnorm(pj, nh, goff, out_t, out_ap3):
            hs = hs_r.next()
            n = nh * 64
            pj3 = pj[:, 0:n].rearrange("p (h d) -> p h d", d=64)
            S.add("dve", lambda e: e.tensor_tensor(out=sq_t[:, 0:n], in0=pj[:, 0:n], in1=pj[:, 0:n], op=ALU.mult),
                  reads=[pj], writes=[sq_t])
            S.add("dve", lambda e: e.tensor_reduce(out=hs[:, 0:nh], in_=sq_t[:, 0:n].rearrange("p (h d) -> p h d", d=64),
                                                   axis=AX.X, op=ALU.add), reads=[sq_t], writes=[hs])
            S.add("act", lambda e: e.activation(out=hs[:, 8:8 + nh], in_=hs[:, 0:nh], func=AF.Sqrt, bias=epsb[:, 0:1],
                                                scale=1.0 / 64), reads=[hs, epsb], writes=[hs])
            S.add("dve", lambda e: e.reciprocal(out=hs[:, 0:nh], in_=hs[:, 8:8 + nh]), reads=[hs], writes=[hs])
            S.add("dve", lambda e: e.tensor_tensor(out=pj3, in0=pj3, in1=hs[:, 0:nh].unsqueeze(2).to_broadcast([128, nh, 64]),
                                                   op=ALU.mult), reads=[pj, hs], writes=[pj])
            S.add("dve", lambda e: e.tensor_tensor(out=out_ap3, in0=pj3,
                                                   in1=gq[:, goff:goff + 64].unsqueeze(1).to_broadcast([128, nh, 64]),
                                                   op=ALU.mult), reads=[pj, gq], writes=[out_t])

        def rotary(pj, nh, tt, out_t, out_ap3):
            n = nh * 64
            pj3 = pj[:, 0:n].rearrange("p (h d) -> p h d", d=64)
            x1 = pj3[:, :, 0:32]
            x2 = pj3[:, :, 32:64]
            cs = rope[:, tt, 0:32].unsqueeze(1).to_broadcast([128, nh, 32])
            sn = rope[:, tt, 32:64].unsqueeze(1).to_broadcast([128, nh, 32])
            r = [rt_r.next() for _ in range(4)]
            rv = [t[:, 0:nh * 32].rearrange("p (h d) -> p h d", d=32) for t in r]
            S.add("dve", lambda e: e.tensor_tensor(out=rv[0], in0=x1, in1=cs, op=ALU.mult), reads=[pj, rope], writes=[r[0]])
            S.add("dve", lambda e: e.tensor_tensor(out=rv[1], in0=x2, in1=sn, op=ALU.mult), reads=[pj, rope], writes=[r[1]])
            S.add("dve", lambda e: e.tensor_tensor(out=rv[2], in0=x1, in1=sn, op=ALU.mult), reads=[pj, rope], writes=[r[2]])
            S.add("dve", lambda e: e.tensor_tensor(out=rv[3], in0=x2, in1=cs, op=ALU.mult), reads=[pj, rope], writes=[r[3]])
            S.add("dve", lambda e: e.tensor_tensor(out=out_ap3[:, :, 0:32], in0=rv[0], in1=rv[1], op=ALU.subtract),
                  reads=[r[0], r[1]], writes=[out_t])
            S.add("dve", lambda e: e.tensor_tensor(out=out_ap3[:, :, 32:64], in0=rv[2], in1=rv[3], op=ALU.add),
                  reads=[r[2], r[3], out_t], writes=[out_t])

        def attn_head(qT_t, qT_ap, kT_t, kT_ap, nk, v_t, v_aps, bias_t, bias_ap, sink_ap, sc, o_ap, rsum, hcol, grp):
            nkk = nk * 128

            def mm(e):
                ins = e.matmul(sc[:, 0:min(512, nkk)], lhsT=qT_ap, rhs=kT_ap[:, 0:min(512, nkk)], start=True, stop=True)
                if nkk > 512:
                    ins = e.matmul(sc[:, 512:nkk], lhsT=qT_ap, rhs=kT_ap[:, 512:nkk], start=True, stop=True)
                return ins

            S.add("pe", mm, reads=[qT_t, kT_t], writes=[sc])
            lg = lg_r.next()
            S.add("dve", lambda e: e.scalar_tensor_tensor(out=lg[:, 0:nkk], in0=sc[:, 0:nkk], scalar=0.125, in1=bias_ap,
                                                          op0=ALU.mult, op1=ALU.add), reads=[sc, bias_t], writes=[lg])
            ms = ms_r.next()
            S.add("dve", lambda e: e.tensor_reduce(out=ms[:, 0:1], in_=lg[:, 0:nkk], axis=AX.X, op=ALU.max),
                  reads=[lg], writes=[ms])
            if sink_ap is not None:
                S.add("dve", lambda e: e.tensor_tensor(out=ms[:, 0:1], in0=ms[:, 0:1], in1=sink_ap, op=ALU.max),
                      reads=[ms, sinkb], writes=[ms])
            S.add("dve", lambda e: e.tensor_scalar(out=ms[:, 1:2], in0=ms[:, 0:1], scalar1=-1.0, scalar2=None, op0=ALU.mult),
                  reads=[ms], writes=[ms])
            P = P_r.next()
            S.add("act", lambda e: e.activation(out=P[:, 0:nkk], in_=lg[:, 0:nkk], func=AF.Exp, bias=ms[:, 1:2], scale=1.0,
                                                accum_out=rsum[:, hcol:hcol + 1]), reads=[lg, ms], writes=[P, rsum])
            if sink_ap is not None:
                S.add("act", lambda e: e.activation(out=ms[:, 2:3], in_=sink_ap, func=AF.Exp, bias=ms[:, 1:2], scale=1.0),
                      reads=[ms, sinkb], writes=[ms])
                S.add("dve", lambda e: e.tensor_tensor(out=rsum[:, hcol:hcol + 1], in0=rsum[:, hcol:hcol + 1], in1=ms[:, 2:3],
                                                       op=ALU.add), reads=[rsum, ms], writes=[rsum])
            PT = PT_r.next()
            transposes(P, [P[:, j * 128:(j + 1) * 128] for j in range(nk)], PT, PT[:, 0:nkk], nk, eng="act")

            def pv(e):
                for j in range(nk):
                    ins = e.matmul(o_ap, lhsT=PT[:, j * 128:(j + 1) * 128], rhs=v_aps[j], start=(j == 0), stop=(j == nk - 1))
                return ins

            S.add("pe", pv, reads=[PT, v_t], writes=[PSO], group=grp)

        for l in range(n_layers):
            xsrc_tok = [x_in] * NT if l == 0 else out_tok
            xsrc = x_in if l == 0 else out_d
            if do_attn:
                S.dma(gvec, gvec[:], norm_mix, norm_mix[l:l + 1, :].to_broadcast([128, D]))
                S.dma(gq, gq[:], qk_norm, qk_norm[l:l + 1, :].to_broadcast([128, 256]))
                S.dma(gb, gb[:], gate_bias, gate_bias[l:l + 1, :].to_broadcast([128, 2 * D]))
                S.dma(sinkb, sinkb[:], swa_sink, swa_sink[l:l + 1, :].to_broadcast([128, 8]))
                S.add("dve", lambda e: e.tensor_reduce(out=bnd[:, 0:4], in_=gq[:, 0:256].rearrange("p (r d) -> p r d", d=64), axis=AX.X,
                                                       op=ALU.max, apply_absolute_value=True), reads=[gq], writes=[bnd])
                S.add("dve", lambda e: e.tensor_tensor(out=nB[:].rearrange("p (a b) -> p a b", b=1),
                                                       in0=bnd[:, 0:4].rearrange("p (a b) -> p a b", b=2)[:, :, 0:1],
                                                       in1=bnd[:, 0:4].rearrange("p (a b) -> p a b", b=2)[:, :, 1:2], op=ALU.mult),
                      reads=[bnd], writes=[nB])
                S.add("dve", lambda e: e.tensor_scalar(out=nB[:], in0=nB[:], scalar1=-8.0, scalar2=None, op0=ALU.mult),
                      reads=[nB], writes=[nB])
                S.add("act", lambda e: e.activation(out=esc[:, 0:8], in_=sinkb[:, 0:8], func=AF.Exp, bias=nB[:, 1:2], scale=1.0),
                      reads=[sinkb, nB], writes=[esc])
                for tt in range(NT):
                    xt = xt_r.next()
                    S.dma(xt, xt[:], xsrc_tok[tt], xsrc[tt * 128:(tt + 1) * 128, :])
                    xn = xn_r.next()
                    rmsnorm_tile(xt, gvec, xn, xn[:])
                    transposes(xn, [xn[:, c * 128:(c + 1) * 128] for c in range(8)], xnT,
                               xnT[:, :, tt * 128:(tt + 1) * 128], 8, eng="act")
                groups = [(0, 512, "naq"), (512, 512, "nak"), (1024, 512, "nav"), (1536, 512, "swq"), (2048, 256, "swkv"),
                          (2304, 512, "g"), (2816, 512, "g"), (3328, 512, "g"), (3840, 512, "g")]
                deferred = []
                ep_q = []

                def wload(gi_):
                    c0_, n_, _k = groups[gi_]
                    w_ = wst[gi_ % 2]
                    S.dma(w_, w_[:, :, 0:n_], w_in, w_in[l, :, c0_:c0_ + n_].rearrange("(c p) n -> p c n", p=128), eng="pool")
                wload(0)
                for gi, (c0, n, kind) in enumerate(groups):
                    w = wst[gi % 2]
                    if gi + 1 < len(groups):
                        wload(gi + 1)

                    for tt in range(NT):
                        acc = acc_r.next()

                        def mm(e, acc=acc, w=w, tt=tt, n=n):
                            for c in range(8):
                                ins = e.matmul(acc[:, 0:n], lhsT=xnT[:, c, tt * 128:(tt + 1) * 128], rhs=w[:, c, 0:n],
                                               start=(c == 0), stop=(c == 7))
                            return ins

                        ptok = T(None, "ptok")
                        S.add("pe", mm, reads=[xnT, w], writes=[acc, ptok])
                        if (gi * NT + tt) % 4 == 0 and conv_todo and conv_todo[0] < NCH:
                            conv_some(1, after=ptok)
                        while deferred:
                            deferred.pop(0)()
                        ep_prev = ep_q[:]
                        del ep_q[:]
                        if kind == "nav":
                            S.add("act", lambda e, acc=acc, tt=tt: e.activation(out=vna[:, tt, :], in_=acc[:, 0:512], func=AF.Copy),
                                  reads=[acc], writes=[vna])
                            for f_ in ep_prev:
                                f_()
                            continue
                        if kind == "g":
                            goff = c0 - 2304
                            gl = gl_r.next()
                            S.add("dve", lambda e, acc=acc, gl=gl, goff=goff: e.tensor_tensor(
                                out=gl[:], in0=acc[:, 0:512], in1=gb[:, goff:goff + 512], op=ALU.add),
                                reads=[acc, gb], writes=[gl])
                            for f_ in ep_prev:
                                f_()
                            S.add("act", lambda e, gl=gl: e.activation(out=gl[:], in_=gl[:], func=AF.Sigmoid),
                                  reads=[gl], writes=[gl])
                            S.dma(gates_tok[tt], gates_d[tt * 128:(tt + 1) * 128, goff:goff + 512], gl, gl[:],
                                  group=("g", l, tt))
                            continue
                        pj = pj_r.next()
                        S.add("act", lambda e, acc=acc, pj=pj, n=n: e.activation(out=pj[:, 0:n], in_=acc[:, 0:n], func=AF.Copy),
                              reads=[acc], writes=[pj])
                        for f_ in ep_prev:
                            f_()

                        def ep(kind=kind, pj=pj, tt=tt):
                            if kind in ("naq", "nak"):
                                qn = qn_r.next()
                                headnorm(pj, 8, 0 if kind == "naq" else 64, qn, qn[:].rearrange("p (h d) -> p h d", d=64))
                                dst = qTna if kind == "naq" else kTna
                                deferred.append(lambda qn=qn, dst=dst, tt=tt: transposes(
                                    qn, [qn[:, c * 128:(c + 1) * 128] for c in range(4)], dst,
                                    dst[:, :, tt * 128:(tt + 1) * 128], 4, eng="act"))
                            elif kind == "swq":
                                qn = qn_r.next()
                                headnorm(pj, 8, 128, pj, pj[:].rearrange("p (h d) -> p h d", d=64))
                                rotary(pj, 8, tt, qn, qn[:].rearrange("p (h d) -> p h d", d=64))
                                deferred.append(lambda qn=qn, tt=tt: transposes(
                                    qn, [qn[:, c * 128:(c + 1) * 128] for c in range(4)], qTsw,
                                    qTsw[:, :, tt * 128:(tt + 1) * 128], 4, eng="act"))
                            elif kind == "swkv":
                                qn = qn_r.next()
                                S.add("act", lambda e, pj=pj, tt=tt: e.activation(out=vsw[:, tt, :], in_=pj[:, 128:256], func=AF.Copy),
                                      reads=[pj], writes=[vsw])
                                headnorm(pj, 2, 192, pj, pj[:, 0:128].rearrange("p (h d) -> p h d", d=64))
                                rotary(pj, 2, tt, qn, qn[:, 0:128].rearrange("p (h d) -> p h d", d=64))
                                S.add("act", lambda e, qn=qn: e.activation(out=qn[:, 128:192], in_=qn[:, 64:128], func=AF.Copy),
                                      reads=[qn], writes=[qn])
                                S.add("act", lambda e, qn=qn: e.activation(out=qn[:, 192:256], in_=qn[:, 0:64], func=AF.Copy),
                                      reads=[qn], writes=[qn])
                                deferred.append(lambda qn=qn, tt=tt: transposes(
                                    qn, [qn[:, 0:128], qn[:, 128:256]], kTsw, kTsw[:, :, tt * 128:(tt + 1) * 128], 2, eng="act"))
                        ep_q.append(ep)
                for f_ in ep_q[:]:
                    f_()
                del ep_q[:]
                while deferred:
                    deferred.pop(0)()
                S.barrier()
                if dbg and l == 0:
                    for nm, tl in [("xnT", xnT), ("qTna", qTna), ("kTna", kTna), ("vna", vna), ("qTsw", qTsw), ("kTsw", kTsw), ("vsw", vsw)]:
                        shp = list(tl.t.shape)
                        dd = S.dram("dbg_" + nm, shp, F32, "ExternalOutput")
                        S.dma(dd, dd.t.ap(), tl, tl.t.ap(), eng="pool")
                    S.barrier()
                S.dma(Wna, Wna[:], w_bna, w_bna[l].rearrange("(c p) n -> p c n", p=128), eng="pool")
                S.dma(Wsw, Wsw[:], w_bsw, w_bsw[l].rearrange("(c p) n -> p c n", p=128), eng="pool")
                S.dma(Wout, Wout[:, 0:4, :], w_out, w_out[l, 0:512, :].rearrange("(c p) n -> p c n", p=128), eng="pool",
                      group=("wout", l))
                S.dma(Wout, Wout[:, 4:8, :], w_out, w_out[l, 512:1024, :].rearrange("(c p) n -> p c n", p=128), eng="pool",
                      group=("wout", l))
                sc_r = Ring([PSA, PSB])
                sums_tok = [S.token(PSA.t, "sums0"), S.token(PSB.t, "sums1")]
                PSY = PST[1]
                psy = PST[1][:, :].bitcast(F32)
                pst_r.tiles = [PST[0]]
                jobs = [(tt, br, h) for tt in range(NT) for br in range(2) for h in range(8)]
                NJ = len(jobs)
                jst = {}
                tst = {}
                later = []

                def stA(j, l=l):
                    tt, br, h = jobs[j]
                    if br == 0 and h == 0:
                        g_t = gt[tt % 2]
                        S.dma(g_t, g_t[:], gates_tok[tt], gates_d[tt * 128:(tt + 1) * 128, :])
                        xt = xt_r.next()
                        S.dma(xt, xt[:], xsrc_tok[tt], xsrc[tt * 128:(tt + 1) * 128, :])
                        tst[tt] = dict(g_t=g_t, xt=xt, oT=[None, None])
                    if h == 0:
                        tst[tt]["rsum%d" % br] = rsum_r.next()
                    sc = sc_r.next()
                    pr, hp = h // 2, (h % 2) * 64
                    d = dict(sc=sc, rsum=tst[tt]["rsum%d" % br])
                    if br == 0:
                        ws = min(max(tt - 2, 0), 11)
                        ty = {0: 0, 1: 1, 14: 3, 15: 4}.get(tt, 2)
                        nk = 5
                        bt = bias_r.next()
                        S.dma(bt, bt[:], nab, nab[l, ty, h, :, :])
                        qT_t, qT_ap = qTna, qTna[hp:hp + 64, pr, tt * 128:(tt + 1) * 128]
                        kT_t, kT_ap = kTna, kTna[hp:hp + 64, pr, ws * 128:(ws + 5) * 128]
                        d.update(nk=5, v_t=vna, v_aps=[vna[:, ws + jj, h * 64:(h + 1) * 64] for jj in range(5)],
                                 bias_t=bt, bias_ap=bt[:, 0:640], sink=None, negb=nB[:, 0:1])
                    else:
                        lo = max(tt - 1, 0)
                        hi = min(tt + 1, 15)
                        nk = hi - lo + 1
                        kvh = h // 4
                        slot = 0 if (hp // 64) == kvh else 1
                        moff = 128 if tt == 0 else 0
                        qT_t, qT_ap = qTsw, qTsw[hp:hp + 64, pr, tt * 128:(tt + 1) * 128]
                        kT_t, kT_ap = kTsw, kTsw[hp:hp + 64, slot, lo * 128:(hi + 1) * 128]
                        d.update(nk=nk, v_t=vsw, v_aps=[vsw[:, lo + jj, kvh * 64:(kvh + 1) * 64] for jj in range(nk)],
                                 bias_t=swam, bias_ap=swam[:, moff:moff + nk * 128], sink=sinkb[:, h:h + 1], negb=nB[:, 1:2])
                    nkk = nk * 128

                    def mm(e):
                        for kt in range(nk):
                            ins = e.matmul(sc[:, kt * 128:(kt + 1) * 128], lhsT=kT_ap[:, kt * 128:(kt + 1) * 128], rhs=qT_ap,
                                           start=True, stop=True)
                        return ins
                    ptok = T(None, "ptok")
                    S.add("pe", mm, reads=[qT_t, kT_t], writes=[sc, ptok])
                    if h == 0 and conv_todo and conv_todo[0] < NCH:
                        conv_some(1, after=ptok)
                    jst[j] = d

                def stB(j):
                    tt, br, h = jobs[j]
                    d = jst[j]
                    sc, rsum, sink_ap = d["sc"], d["rsum"], d["sink"]
                    nkk = d["nk"] * 128
                    lg = lg_r.next()
                    S.add("dve", lambda e: e.scalar_tensor_tensor(out=lg[:, 0:nkk], in0=sc[:, 0:nkk], scalar=0.125, in1=d["bias_ap"],
                                                                  op0=ALU.mult, op1=ALU.add), reads=[sc, d["bias_t"]], writes=[lg])
                    P = P_r.next()
                    S.add("act", lambda e: e.activation(out=P[:, 0:nkk], in_=lg[:, 0:nkk], func=AF.Exp, bias=d["negb"], scale=1.0),
                          reads=[lg, nB], writes=[P])
                    d["PT"] = P

                def stD(j, l=l):
                    tt, br, h = jobs[j]
                    d = jst.pop(j)
                    PT, nk, v_aps = d["PT"], d["nk"], d["v_aps"]

                    sc = d["sc"]
                    stok = sums_tok[0] if sc is PSA else sums_tok[1]

                    def pv(e):
                        for jj in range(nk):
                            e.matmul(PSO[:, h * 64:(h + 1) * 64], lhsT=PT[:, jj * 128:(jj + 1) * 128], rhs=v_aps[jj],
                                     start=(jj == 0), stop=(jj == nk - 1))
                        for jj in range(nk):
                            ins = e.matmul(sc[:, 640 + h:641 + h], lhsT=PT[:, jj * 128:(jj + 1) * 128], rhs=onesb[:, 0:1],
                                           start=(jj == 0), stop=(jj == nk - 1))
                        return ins
                    S.add("pe", pv, reads=[PT, d["v_t"], onesb], writes=[PSO, stok], group=("pso", l, tt, br))
                    assert (sc is PSA) == (h % 2 == 0)
                    if h != 7:
                        return
                    rsum = d["rsum"]
                    ts_ = tst[tt]
                    steps = []

                    def e1():
                        r2 = rsum[:, 0:8].rearrange("p (a b) -> p a b", b=2)
                        S.add("dve", lambda e: e.tensor_copy(out=r2[:, :, 0], in_=PSA[:, 640:648].rearrange("p (a b) -> p a b", b=2)[:, :, 0]),
                              reads=[sums_tok[0]], writes=[rsum])
                        S.add("dve", lambda e: e.tensor_copy(out=r2[:, :, 1], in_=PSB[:, 640:648].rearrange("p (a b) -> p a b", b=2)[:, :, 1]),
                              reads=[sums_tok[1], rsum], writes=[rsum])
                        if br == 1:
                            S.add("dve", lambda e: e.tensor_tensor(out=rsum[:, 0:8], in0=rsum[:, 0:8], in1=esc[:, 0:8], op=ALU.add),
                                  reads=[rsum, esc], writes=[rsum])
                        S.add("dve", lambda e: e.reciprocal(out=rsum[:, 8:16], in_=rsum[:, 0:8]), reads=[rsum], writes=[rsum])
                        S.add("dve", lambda e: e.tensor_tensor(
                            out=osb[:].rearrange("p (h d) -> p h d", d=64), in0=PSO[:].rearrange("p (h d) -> p h d", d=64),
                            in1=rsum[:, 8:16].unsqueeze(2).to_broadcast([128, 8, 64]), op=ALU.mult),
                            reads=[PSO, rsum], writes=[osb])

                    def e2():
                        oT = oT_r.next()
                        transposes(osb, [osb[:, c * 128:(c + 1) * 128] for c in range(4)], oT, oT[:], 4, eng="act")
                        ts_["oT"][br] = oT
                    e1()
                    steps += [e2]
                    if br == 1:
                        g_t, xt = ts_["g_t"], ts_["xt"]

                        def m_mm(n):
                            def mmb(e):
                                for (W, oTb, acc) in ((Wna, ts_["oT"][0], PSX[:, :]), (Wsw, ts_["oT"][1], psy)):
                                    for c in range(4):
                                        ins = e.matmul(acc, lhsT=oTb[:, c, :], rhs=W[:, c, n * 512:(n + 1) * 512],
                                                       start=(c == 0), stop=(c == 3))
                                return ins
                            S.add("pe", mmb, reads=[Wna, Wsw, ts_["oT"][0], ts_["oT"][1]], writes=[PSX, PSY])

                        def m_dve(n):
                            S.add("dve", lambda e: e.tensor_tensor(out=m1[:], in0=PSX[:], in1=g_t[:, n * 512:(n + 1) * 512], op=ALU.mult),
                                  reads=[PSX, g_t], writes=[m1])
                            S.add("dve", lambda e: e.tensor_tensor(out=m2[:], in0=psy, in1=g_t[:, D + n * 512:D + (n + 1) * 512], op=ALU.mult),
                                  reads=[PSY, g_t], writes=[m2])
                            S.add("dve", lambda e: e.tensor_tensor(out=mg[:, n * 512:(n + 1) * 512], in0=m1[:], in1=m2[:], op=ALU.add),
                                  reads=[m1, m2, mg], writes=[mg])

                        def m_tr():
                            transposes(mg, [mg[:, c * 128:(c + 1) * 128] for c in range(8)], mgT, mgT[:], 8, eng="act")
                            ts_["xo"] = xo_r.next()

                        def o_mm(n):
                            acc_t, acc = (PSX, PSX[:, :]) if n == 0 else (PSY, psy)

                            def mmo(e):
                                for c in range(8):
                                    ins = e.matmul(acc, lhsT=mgT[:, c, :], rhs=Wout[:, c, n * 512:(n + 1) * 512],
                                                   start=(c == 0), stop=(c == 7))
                                return ins
                            S.add("pe", mmo, reads=[Wout, mgT], writes=[acc_t])

                        def o_dve(n):
                            acc_t, acc = (PSX, PSX[:, :]) if n == 0 else (PSY, psy)
                            xo = ts_["xo"]
                            S.add("dve", lambda e: e.tensor_tensor(out=xo[:, n * 512:(n + 1) * 512], in0=acc,
                                                                   in1=xt[:, n * 512:(n + 1) * 512], op=ALU.add),
                                  reads=[acc_t, xt, xo], writes=[xo])

                        def fin():
                            S.dma(out_tok[tt], out_d[tt * 128:(tt + 1) * 128, :], ts_["xo"], ts_["xo"][:])
                            del tst[tt]
                        nop = None
                        steps += [lambda: m_mm(0), lambda: m_dve(0), lambda: m_mm(1), lambda: m_dve(1), m_tr,
                                  lambda: o_mm(0), lambda: o_mm(1), lambda: o_dve(0), lambda: (o_dve(1), fin())]
                    for i_, fn_ in enumerate(steps):
                        if fn_ is not None:
                            later.append((j + 1 + i_, fn_))
                    later.sort(key=lambda t_: t_[0])

                stA(0)
                stA(1)
                stB(0)
                for j in range(NJ):
                    if j + 2 < NJ:
                        stA(j + 2)
                    if j + 1 < NJ:
                        stB(j + 1)
                    stD(j)
                    while later and later[0][0] <= j:
                        later.pop(0)[1]()
                while later:
                    later.pop(0)[1]()
                pst_r.tiles = PST
                S.barrier()
            elif l == 0:
                for tt in range(NT):
                    xt = xt_r.next()
                    S.dma(xt, xt[:], x_in, x_in[tt * 128:(tt + 1) * 128, :])
                    S.dma(out_tok[tt], out_d[tt * 128:(tt + 1) * 128, :], xt, xt[:])
                S.barrier()
            if do_peer:
                S.dma(gvec, gvec[:], norm_ffn, norm_ffn[l:l + 1, :].to_broadcast([128, D]))
                S.dma(kT_t, kT_t[:], keysT, keysT[l].rearrange("p c n -> c p n"))
                for hp in range(16):
                    wq = wq_t[hp % 2]
                    S.dma(wq, wq[:], wqT, wqT[l, hp * 128:(hp + 1) * 128, :])
                    for half in range(2):
                        acc = acc_r.next()

                        def mmk(e, acc=acc, wq=wq, hp=hp, half=half):
                            for j in range(4):
                                dch = half * 4 + j
                                ins = e.matmul(acc[:, j * 128:(j + 1) * 128], lhsT=wq[:, dch * 128:(dch + 1) * 128],
                                               rhs=kT_t[:, hp % 2, :], start=True, stop=True)
                            return ins
                        S.add("pe", mmk, reads=[wq, kT_t], writes=[acc])
                        S.add("act", lambda e, acc=acc, hp=hp, half=half: e.activation(
                            out=Wk[:, half * 4:half * 4 + 4, hp * 128:(hp + 1) * 128],
                            in_=acc[:, 0:512].rearrange("p (c n) -> p c n", n=128), func=AF.Copy), reads=[acc], writes=[Wk])
                NG = 128 // GG
                hb_of = {}

                def s1_chunks(tt, l=l):
                    ch = []
                    xt = xp[tt % 3]
                    hfb = hf_b[tt % 2]

                    def a():
                        S.dma(xt, xt[:], out_tok[tt], out_d[tt * 128:(tt + 1) * 128, :])
                        xn = xn_r.next()
                        hb_of[tt] = xn
                        rmsnorm_tile(xt, gvec, hfb, hfb[:], xn, xn[:])
                        hT = hT_b[tt % 2]
                        transposes(xn, [xn[:, c * 128:(c + 1) * 128] for c in range(8)], hT, hT[:], 8, eng="act")
                        for half in range(2):
                            for q2 in range(2):
                                q4 = half * 2 + q2
                                col = q2 * 512

                                def mms(e, col=col, q4=q4, hT=hT):
                                    for c in range(8):
                                        ins = e.matmul(PSA[:, col:col + 512], lhsT=hT[:, c, :], rhs=Wk[:, c, q4 * 512:(q4 + 1) * 512],
                                                       start=(c == 0), stop=(c == 7))
                                    return ins
                                S.add("pe", mms, reads=[hT, Wk], writes=[PSA], group=("s", l, tt, half))
                            S.add("act", lambda e, half=half: e.activation(out=s_sb[:, half * 1024:(half + 1) * 1024], in_=PSA[:], func=AF.Copy),
                                  reads=[PSA], writes=[s_sb], group=("ssb", l, tt))
                    ch.append(a)

                    def oplist():
                        ops = []

                        def A_(*args, **kw):
                            ops.append(lambda: S.add(*args, **kw))
                        return ops, A_

                    def b(hp):
                        ops, A_ = oplist()
                        sl = s_sb[:, hp * 128:(hp + 1) * 128]
                        svt, sit, sft, wk = sv_t[hp], si_t[hp], sif_t[hp], wk_t[hp % 2]
                        A_("dve", lambda e: e.max(out=svt[:, 0:8], in_=sl), reads=[s_sb], writes=[svt])
                        A_("dve", lambda e: e.max_index(out=sit[:, 0:8], in_max=svt[:, 0:8], in_values=sl),
                           reads=[s_sb, svt], writes=[sit])
                        A_("dve", lambda e: e.match_replace(out=wk[:], in_to_replace=svt[:, 0:8], in_values=sl,
                                                            imm_value=-1e30), reads=[s_sb, svt], writes=[wk])
                        A_("dve", lambda e: e.max(out=svt[:, 8:16], in_=wk[:]), reads=[wk, svt], writes=[svt])
                        A_("dve", lambda e: e.max_index(out=sit[:, 8:16], in_max=svt[:, 8:16], in_values=wk[:]),
                           reads=[wk, svt, sit], writes=[sit])
                        A_("dve", lambda e: e.tensor_copy(out=sft[:], in_=sit[:]), reads=[sit], writes=[sft])
                        return ops

                    def c(h):
                        ops, A_ = oplist()
                        tp = tops_t[h]
                        c3 = cand[:].rearrange("p (a b) -> p a b", b=16)
                        A_("dve", lambda e: e.tensor_tensor(
                            out=c3, in0=sv_t[2 * h][:, :].unsqueeze(2).to_broadcast([128, 16, 16]),
                            in1=sv_t[2 * h + 1][:, :].unsqueeze(1).to_broadcast([128, 16, 16]), op=ALU.add),
                            reads=[sv_t[2 * h], sv_t[2 * h + 1]], writes=[cand])
                        A_("dve", lambda e: e.max(out=tp[:, 0:8], in_=cand[:]), reads=[cand], writes=[tp])
                        A_("dve", lambda e: e.max_index(out=pos[:, 0:8], in_max=tp[:, 0:8], in_values=cand[:]),
                           reads=[cand, tp], writes=[pos])
                        A_("dve", lambda e: e.match_replace(out=cand2[:], in_to_replace=tp[:, 0:8], in_values=cand[:],
                                                            imm_value=-1e30), reads=[cand, tp], writes=[cand2])
                        A_("dve", lambda e: e.max(out=tp[:, 8:16], in_=cand2[:]), reads=[cand2, tp], writes=[tp])
                        A_("dve", lambda e: e.max_index(out=pos[:, 8:16], in_max=tp[:, 8:16], in_values=cand2[:]),
                           reads=[cand2, tp, pos], writes=[pos])
                        A_("dve", lambda e: e.tensor_single_scalar(out=pab[:, 0:16], in_=pos[:], scalar=4, op=ALU.logical_shift_right),
                           reads=[pos], writes=[pab])
                        A_("dve", lambda e: e.tensor_single_scalar(out=pab[:, 16:32], in_=pos[:], scalar=15, op=ALU.bitwise_and),
                           reads=[pos, pab], writes=[pab])
                        A_("dve", lambda e: e.tensor_copy(out=pabf[:], in_=pab[:]), reads=[pab], writes=[pabf])
                        for w_, (srcrow, col0) in enumerate(((2 * h, 0), (2 * h + 1, 16))):
                            oh = jk256[:].rearrange("p (k a) -> p k a", a=16)
                            A_("dve", lambda e, col0=col0, oh=oh: e.tensor_tensor(
                                out=oh, in0=iota16[:].unsqueeze(1).to_broadcast([128, 16, 16]),
                                in1=pabf[:, col0:col0 + 16].unsqueeze(2).to_broadcast([128, 16, 16]), op=ALU.is_equal),
                                reads=[iota16, pabf], writes=[jk256])
                            A_("dve", lambda e, srcrow=srcrow, oh=oh: e.tensor_tensor(
                                out=oh, in0=oh, in1=sif_t[srcrow][:, :].unsqueeze(1).to_broadcast([128, 16, 16]), op=ALU.mult),
                                reads=[jk256, sif_t[srcrow]], writes=[jk256])
                            A_("dve", lambda e, w_=w_, oh=oh: e.tensor_reduce(out=i12[:, w_ * 16:(w_ + 1) * 16], in_=oh, axis=AX.X, op=ALU.add),
                               reads=[jk256, i12], writes=[i12])
                        A_("dve", lambda e: e.scalar_tensor_tensor(out=eidf[:, h * 16:(h + 1) * 16], in0=i12[:, 0:16], scalar=128.0,
                                                                   in1=i12[:, 16:32], op0=ALU.mult, op1=ALU.add),
                           reads=[i12, eidf], writes=[eidf])
                        ms = ms_r.next()
                        A_("dve", lambda e: e.tensor_scalar(out=ms[:, 0:1], in0=tp[:, 0:1], scalar1=-1.0, scalar2=None,
                                                            op0=ALU.mult), reads=[tp], writes=[ms])
                        A_("act", lambda e: e.activation(out=esm[:, h, :], in_=tp[:, :], func=AF.Exp, bias=ms[:, 0:1],
                                                         scale=1.0, accum_out=zs[:, h:h + 1]),
                           reads=[tp, ms], writes=[esm, zs])
                        return ops

                    def rr(*lists):
                        out_, lists = [], [list(x) for x in lists]
                        while any(lists):
                            for x in lists:
                                if x:
                                    out_.append(x.pop(0))
                        return out_

                    ch += rr(b(0), b(1))
                    for h in range(8):
                        if h < 7:
                            ch += rr(c(h), b(2 * h + 2), b(2 * h + 3))
                        else:
                            ch += c(h)

                    def d():
                        eb = esm_b[tt % 3]
                        ei = eidi_b[tt % 3]
                        S.add("dve", lambda e: e.reciprocal(out=zs[:], in_=zs[:]), reads=[zs], writes=[zs])
                        S.add("dve", lambda e: e.tensor_tensor(out=eb[:].rearrange("p (h k) -> p h k", k=16), in0=esm[:],
                                                               in1=zs[:].unsqueeze(2).to_broadcast([128, 8, 16]),
                                                               op=ALU.mult), reads=[esm, zs], writes=[eb])
                        S.add("dve", lambda e: e.tensor_scalar(out=eidf[:], in0=eidf[:], scalar1=float(NEXP - 1), scalar2=float(l * NEXP),
                                                               op0=ALU.min, op1=ALU.add), reads=[eidf], writes=[eidf])
                        S.add("dve", lambda e: e.tensor_copy(out=ei[:], in_=eidf[:]), reads=[eidf], writes=[ei])
                    ch.append(d)
                    return ch

                for cfn in s1_chunks(0):
                    cfn()
                conv_flush(l)
                NGRP = 128 // GG
                NG_ALL = NT * NGRP
                uv_of = {}

                def stG(G, l=l):
                    tt, g = divmod(G, NGRP)
                    UV = uv_r.next()
                    uv_of[G] = UV
                    ei = eidi_b[tt % 3]
                    for j in range(GG):
                        hk = g * GG + j
                        S.add("pool", lambda e, UV=UV, j=j, hk=hk, ei=ei: e.indirect_dma_start(
                            out=UV[:, j, :], out_offset=None, in_=ptab16[:, :],
                            in_offset=bass.IndirectOffsetOnAxis(ap=ei[:, hk:hk + 1], axis=0)),
                            reads=[ei, ptab16_tok[l]], writes=[UV], dma=True, group=("UV", l, G))

                def stDots(G, l=l):
                    tt, g = divmod(G, NGRP)
                    UV = uv_of[G]
                    hfb = hf_b[tt % 2]
                    dt_ = d_ring.next()
                    wa = w_ring.next()
                    uv_of[G] = (UV, wa)
                    if PE_DOTS:
                        hT = hT_b[tt % 2]
                        gram = gram_r.next()
                        gcol = 0 if gram.name == "gram0" else 512
                        UTs = [ut_r.next() for _ in range(GG)]

                        def Tj(j):
                            transposes(UV, [UV[:, j, c * 128:(c + 1) * 128] for c in range(8)], UTs[j], UTs[j][:], 8, eng="act")

                        def MMj(j):
                            def mmg(e):
                                for c in range(8):
                                    ins = e.matmul(PSB[:, gcol + j * 128:gcol + (j + 1) * 128], lhsT=hT[:, c, :],
                                                   rhs=UTs[j][:, c * 128:(c + 1) * 128], start=(c == 0), stop=(c == 7))
                                return ins
                            S.add("pe", mmg, reads=[hT, UTs[j]], writes=[gram], group=("gram", l, G))
                        Tj(0); Tj(1); MMj(0); Tj(2); MMj(1); Tj(3); MMj(2); MMj(3)
                        g3 = PSB[:, gcol:gcol + 512].rearrange("p (j t) -> p j t", t=128)
                        S.add("dve", lambda e: e.tensor_tensor(out=gmul[:].rearrange("p (j t) -> p j t", t=128), in0=g3,
                                                               in1=ident_f[:].unsqueeze(1).to_broadcast([128, GG, 128]), op=ALU.mult),
                              reads=[gram, ident_f], writes=[gmul])
                        S.add("dve", lambda e: e.tensor_reduce(out=dt_[:, 0:GG], in_=gmul[:].rearrange("p (j t) -> p j t", t=128),
                                                               axis=AX.X, op=ALU.add), reads=[gmul], writes=[dt_])
                        S.add("act", lambda e: e.activation(out=wa[:, 0:GG], in_=dt_[:, 0:GG], func=AF.Gelu),
                              reads=[dt_], writes=[wa])
                        return
                    if TT_DOTS:
                        hb = hb_of[tt]
                        for j in range(GG):
                            pr_ = prod_r.next()
                            S.add("dve", lambda e, UV=UV, j=j, pr_=pr_: e.tensor_tensor(out=pr_[:], in0=UV[:, j, 0:D], in1=hb[:], op=ALU.mult),
                                  reads=[UV, hb], writes=[pr_])
                            S.add("act", lambda e, j=j, pr_=pr_: e.activation(out=pr_[:], in_=pr_[:], func=AF.Copy, accum_out=dt_[:, j:j + 1]),
                                  reads=[pr_], writes=[dt_], group=("dt", l, G))
                            for _ in range(2):
                                if pend_ops:
                                    pend_ops.pop(0)()
                        S.add("act", lambda e: e.activation(out=wa[:, 0:GG], in_=dt_[:, 0:GG], func=AF.Gelu),
                              reads=[dt_], writes=[wa])
                        return
                    for j in range(GG):
                        if j == GG - 1 and POOL_DOTS:
                            pr_ = prod_r.next()
                            S.add("pool", lambda e, UV=UV, j=j, pr_=pr_: e.tensor_tensor(out=pr_[:], in0=UV[:, j, 0:D], in1=hfb[:], op=ALU.mult),
                                  reads=[UV, hfb], writes=[pr_])
                            S.add("act", lambda e, j=j, pr_=pr_: e.activation(out=junk2[:], in_=pr_[:], func=AF.Copy, accum_out=dt_[:, j:j + 1]),
                                  reads=[pr_], writes=[dt_], group=("dt", l, G))
                            continue
                        S.add("dve", lambda e, UV=UV, j=j: e.scalar_tensor_tensor(
                            out=junk[:], in0=UV[:, j, 0:D], scalar=1.0, in1=hfb[:], op0=ALU.mult, op1=ALU.mult,
                            accum_out=dt_[:, j:j + 1]), reads=[UV, hfb], writes=[dt_], group=("dt", l, G))
                    S.add("act", lambda e: e.activation(out=wa[:, 0:GG], in_=dt_[:, 0:GG], func=AF.Gelu),
                          reads=[dt_], writes=[wa])

                def stMM(G, l=l):
                    tt, g = divmod(G, NGRP)
                    UV, wa = uv_of.pop(G)
                    eb = esm_b[tt % 3]
                    dg = dgg_r.next()
                    S.add("dve", lambda e: e.tensor_tensor(out=wa[:, 0:GG], in0=wa[:, 0:GG],
                                                           in1=eb[:, g * GG:(g + 1) * GG], op=ALU.mult), reads=[wa, eb], writes=[wa])
                    S.add("dve", lambda e: e.tensor_tensor(
                        out=dg[:, 0:GG, :], in0=ident_f[:].unsqueeze(1).to_broadcast([128, GG, 128]),
                        in1=wa[:, 0:GG].unsqueeze(2).to_broadcast([128, GG, 128]), op=ALU.mult),
                        reads=[ident_f, wa], writes=[dg])

                    def mmv(e):
                        for j in range(GG):
                            hk = g * GG + j
                            e.matmul(PSO[:, 0:512], lhsT=dg[:, j, :], rhs=UV[:, j, D:D + 512], start=(hk == 0), stop=(hk == 127))
                            ins = e.matmul(PSX[:, 0:512], lhsT=dg[:, j, :], rhs=UV[:, j, D + 512:2 * D], start=(hk == 0), stop=(hk == 127))
                        return ins
                    gtok = T(None, "gtok")
                    S.add("pe", mmv, reads=[dg, UV], writes=[PSO, PSX, gtok], group=("pv", l, tt))
                    if G % 6 == 0 and conv_todo:
                        conv_some(1, after=gtok)
                    if g == NGRP - 1:
                        xt = xp[tt % 3]
                        xo = xo_r.next()
                        S.add("dve", lambda e: e.tensor_tensor(out=xo[:, 0:512], in0=PSO[:], in1=xt[:, 0:512], op=ALU.add),
                              reads=[PSO, xt], writes=[xo])
                        S.add("dve", lambda e: e.tensor_tensor(out=xo[:, 512:1024], in0=PSX[:], in1=xt[:, 512:1024], op=ALU.add),
                              reads=[PSX, xt, xo], writes=[xo])
                        S.dma(out_tok[tt], out_d[tt * 128:(tt + 1) * 128, :], xo, xo[:])

                LOOK = 3
                pend_ops = []
                for G in range(min(LOOK, NG_ALL)):
                    stG(G)
                stDots(0)
                for G in range(NG_ALL):
                    tt, g = divmod(G, NGRP)
                    if g == 0 and tt + 1 < NT:
                        assert not pend_ops
                        pend_ops.extend(s1_chunks(tt + 1))
                    if G + LOOK < NG_ALL:
                        stG(G + LOOK)
                    if G + 1 < NG_ALL:
                        stDots(G + 1)
                    stMM(G)
                    for _ in range(2):
                        if pend_ops:
                            pend_ops.pop(0)()
                    if g == 27:
                        while pend_ops:
                            pend_ops.pop(0)()
                S.barrier()
        S.emit(final_wait_tiles=out_tok)
    return nc


def _na_bias_tables(rpb):
    L = rpb.shape[0]
    out = np.empty((L, 5, 8, 128, 640), np.float32)
    q = np.arange(128)
    kk = np.arange(640)
    for ty, (tt, ws) in enumerate([(0, 0), (1, 0), (2, 0), (14, 11), (15, 11)]):
        qr = 2 * tt + q // 64
        qc = q % 64
        kr = 2 * ws + kk // 64
        kc = kk % 64
        rs = np.clip(qr - 4, 0, 24)
        cs = np.clip(qc - 8, 0, 48)
        vr = (kr[None, :] >= rs[:, None]) & (kr[None, :] < rs[:, None] + 8)
        vc = (kc[None, :] >= cs[:, None]) & (kc[None, :] < cs[:, None] + 16)
        valid = vr & vc
        dr = np.clip(kr[None, :] - qr[:, None] + 7, 0, 14)
        dc = np.clip(kc[None, :] - qc[:, None] + 15, 0, 30)
        g = rpb[:, :, dr, dc]
        out[:, ty] = np.where(valid[None, None], g, np.float32(NEG))
    out = out.reshape(L, 5, 8, 128, 5, 128).transpose(0, 1, 2, 5, 4, 3)
    return np.ascontiguousarray(out).reshape(L, 5, 8, 128, 640)


def _const_tables():
    half = 32
    inv = (10000.0 ** (-np.arange(half, dtype=np.float32) / half)).astype(np.float32)
    ang = np.arange(SEQ, dtype=np.float32)[:, None] * inv[None, :]
    rope = np.concatenate([np.cos(ang), np.sin(ang)], axis=1).astype(np.float32)
    q = np.arange(128)[:, None]
    c = np.arange(384)[None, :]
    swam = np.where(np.abs(q + 128 - c) <= 128, 0.0, NEG).astype(np.float32)
    swam = np.ascontiguousarray(swam.reshape(128, 3, 128).transpose(2, 1, 0)).reshape(128, 384)
    return rope, swam


_NC_CACHE = {}


def kernel(x, norm_mix, w_in, gate_bias, qk_norm, na_rpb, swa_sink, w_branch_na, w_branch_swa,
           w_out, norm_ffn, peer_query, peer_sub_keys, peer_down, peer_up):
    f = lambda a: np.ascontiguousarray(np.asarray(a, dtype=np.float32))
    x = f(x)
    rope, swam = _const_tables()
    shared = {
        "norm_mix": f(norm_mix), "norm_ffn": f(norm_ffn),
        "gate_bias": f(gate_bias).reshape(DEPTH, 2 * D),
        "qk_norm": f(qk_norm).reshape(DEPTH, 256),
        "swa_sink": f(swa_sink),
        "w_in": f(w_in), "w_bna": f(w_branch_na), "w_bsw": f(w_branch_swa), "w_out": f(w_out),
        "nab": _na_bias_tables(f(na_rpb)),
        "rope": rope, "swam": swam,
        "wqT": np.ascontiguousarray(np.transpose(f(peer_query), (0, 2, 1))),
        "keysT": np.ascontiguousarray(np.transpose(f(peer_sub_keys), (0, 1, 3, 2))),
        "ptab": np.concatenate([f(peer_down).reshape(DEPTH * NEXP, D), f(peer_up).reshape(DEPTH * NEXP, D)], axis=1),
    }
    if "nc" not in _NC_CACHE:
        _NC_CACHE["nc"] = build_program()
    nc = _NC_CACHE["nc"]
    in_maps = []
    for b in range(8):
        m = dict(shared)
        m["x"] = x[b]
        in_maps.append(m)
    res = run_bass_kernel_spmd(nc, in_maps, core_ids=list(range(8)))
    return np.stack([np.asarray(r["out"], dtype=np.float32) for r in res.results], axis=0)
```

```python
import numpy as np
import concourse.bass as bass
import concourse.mybir as mybir
from concourse.bass_utils import run_bass_kernel_spmd
from contextlib import ExitStack

F32 = mybir.dt.float32
BF16 = mybir.dt.bfloat16
I32 = mybir.dt.int32
U32 = mybir.dt.uint32
ALU = mybir.AluOpType
AF = mybir.ActivationFunctionType
AX = mybir.AxisListType

D = 1024
SEQ = 2048
NT = 16
DEPTH = 2
INC = 4352
NEXP = 16384
EPS = 1e-6
NEG = -30000.0


class T:
    def __init__(self, t, name):
        self.t = t
        self.name = name
        self.writers = []
        self.readers = []
        self.group = None
        self.sem = None
        self.cnt = 0

    def __getitem__(self, idx):
        return self.t[idx]


class Op:
    __slots__ = ("eng", "fn", "deps", "dma", "sem", "val", "needed")

    def __init__(self, eng, fn, dma):
        self.eng = eng
        self.fn = fn
        self.dma = dma
        self.deps = []
        self.sem = None
        self.val = 0
        self.needed = False


class Sched:
    ENGS = ["pe", "act", "dve", "pool", "sp"]

    def __init__(self, nc, es, same_engine_sync=True):
        self.nc = nc
        self.es = es
        self.prog = {e: [] for e in self.ENGS}
        self.esem = {e: es.enter_context(nc.semaphore("sem_" + e)) for e in ["pe", "act", "dve", "pool"]}
        self.ses = same_engine_sync
        self.pending_dma = []
        self.semcache = {}

    def sb(self, name, shape, dt):
        return T(self.es.enter_context(self.nc.sbuf_tensor(name, list(shape), dt)), name)

    def sb_at(self, name, shape, dt, offset):
        return T(self.nc.alloc_sbuf_tensor_at(name, list(shape), dt, offset=offset), name)

    def ps(self, name, shape, dt):
        return T(self.es.enter_context(self.nc.psum_tensor(name, list(shape), dt)), name)

    def dram(self, name, shape, dt, kind="Internal"):
        return T(self.nc.dram_tensor(name, list(shape), dt, kind=kind), name)

    def token(self, t, name):
        return T(t, name)

    def add(self, eng, fn, reads=(), writes=(), dma=False, group=None, semkey=None, nobar=False):
        op = Op(eng, fn, dma)
        deps = []
        for t in reads:
            deps.extend(t.writers)
        for t in writes:
            if group is not None and t.group == group:
                continue
            deps.extend(t.writers)
            deps.extend(t.readers)
        seen = set()
        for d in deps:
            if d is op or id(d) in seen:
                continue
            seen.add(id(d))
            op.deps.append(d)
        for t in writes:
            if group is not None and t.group == group:
                t.writers.append(op)
            else:
                t.writers = [op]
                t.readers = []
                t.group = group
        wset = set(id(t) for t in writes)
        for t in reads:
            if id(t) not in wset:
                t.readers.append(op)
        if dma:
            t0 = writes[0]
            if t0.sem is None:
                key = semkey if semkey is not None else ("t", id(t0))
                if key not in self.semcache:
                    self.semcache[key] = [self.es.enter_context(self.nc.semaphore("dsem%d" % len(self.semcache))), 0]
                t0.sem = self.semcache[key]
            t0.sem[1] += 16
            op.sem = t0.sem[0]
            op.val = t0.sem[1]
            if not nobar:
                self.pending_dma.append(op)
        self.prog[eng].append(op)
        return op

    def dma(self, out_t, out_ap, in_t, in_ap, eng="sp", group=None, extra_reads=(), semkey=None, nobar=False):
        return self.add(eng, lambda e: e.dma_start(out=out_ap, in_=in_ap), reads=[in_t] + list(extra_reads),
                        writes=[out_t], dma=True, group=group, semkey=semkey, nobar=nobar)

    def barrier(self):
        lasts = []
        for e in ["pe", "act", "dve", "pool"]:
            for op in reversed(self.prog[e]):
                if not op.dma:
                    lasts.append(op)
                    break
        pend = self.pending_dma
        self.pending_dma = []
        tok = T(None, "bar")
        for e in self.ENGS:
            op = Op(e, None, False)
            op.deps = [d for d in lasts if d.eng != e] + list(pend)
            self.prog[e].append(op)

    def emit(self, final_wait_tiles=()):
        nc = self.nc
        for e in self.ENGS:
            for op in self.prog[e]:
                for d in op.deps:
                    if not d.dma:
                        if d.eng == op.eng and (not self.ses or d.eng == "pe"):
                            continue
                        d.needed = True
        for e in ["pe", "act", "dve", "pool"]:
            c = 0
            for op in self.prog[e]:
                if op.dma or op.fn is None:
                    continue
                if op.needed:
                    c += 1
                    op.sem = self.esem[e]
                    op.val = c
        prog = self.prog
        ses = self.ses
        final_ops = []
        for t in final_wait_tiles:
            final_ops.extend(t.writers)

        with nc.Block() as block:

            def make(ename):
                def body(eng):
                    known = {}
                    for op in prog[ename]:
                        need = {}
                        for d in op.deps:
                            if not d.dma and d.eng == ename and (not ses or ename == "pe"):
                                continue
                            if d.sem is None:
                                continue
                            k = id(d.sem)
                            if k not in need or need[k][1] < d.val:
                                need[k] = (d.sem, d.val)
                        for k, (sm, v) in need.items():
                            if known.get(k, 0) >= v:
                                continue
                            eng.wait_ge(sm, v)
                            known[k] = v
                        if op.fn is None:
                            continue
                        ins = op.fn(eng)
                        if op.dma:
                            ins.then_inc(op.sem, 16)
                        elif op.needed:
                            ins.then_inc(op.sem, 1)
                    if ename == "sp":
                        need = {}
                        for d in final_ops:
                            k = id(d.sem)
                            if k not in need or need[k][1] < d.val:
                                need[k] = (d.sem, d.val)
                        for k, (sm, v) in need.items():
                            if known.get(k, 0) >= v:
                                continue
                            eng.wait_ge(sm, v)
                            known[k] = v

                return body

            block.tensor(make("pe"))
            block.scalar(make("act"))
            block.vector(make("dve"))
            block.gpsimd(make("pool"))
            block.sync(make("sp"))


class Ring:
    def __init__(self, tiles):
        self.tiles = tiles
        self.i = 0

    def next(self):
        t = self.tiles[self.i % len(self.tiles)]
        self.i += 1
        return t


def build_program(n_layers=DEPTH, do_attn=True, do_peer=True, dbg=False, peer_idx_only=False, peer_stage=3):
    nc = bass.Bass("TRN2", target_bir_lowering=False)
    es = ExitStack()
    with es:
        S = Sched(nc, es)
        EI = "ExternalInput"
        x_in = S.dram("x", [SEQ, D], F32, EI)
        norm_mix = S.dram("norm_mix", [DEPTH, D], F32, EI)
        norm_ffn = S.dram("norm_ffn", [DEPTH, D], F32, EI)
        gate_bias = S.dram("gate_bias", [DEPTH, 2 * D], F32, EI)
        qk_norm = S.dram("qk_norm", [DEPTH, 256], F32, EI)
        swa_sink = S.dram("swa_sink", [DEPTH, 8], F32, EI)
        w_in = S.dram("w_in", [DEPTH, D, INC], F32, EI)
        w_bna = S.dram("w_bna", [DEPTH, 512, D], F32, EI)
        w_bsw = S.dram("w_bsw", [DEPTH, 512, D], F32, EI)
        w_out = S.dram("w_out", [DEPTH, D, D], F32, EI)
        nab = S.dram("nab", [DEPTH, 5, 8, 128, 640], F32, EI)
        rope_d = S.dram("rope", [SEQ, 64], F32, EI)
        swam_d = S.dram("swam", [128, 384], F32, EI)
        wqT = S.dram("wqT", [DEPTH, 2048, D], F32, EI)
        keysT = S.dram("keysT", [DEPTH, 2, 128, 128], F32, EI)
        ptab = S.dram("ptab", [DEPTH * NEXP, 2 * D], F32, EI)
        ptab16 = S.dram("ptab16", [DEPTH * NEXP, 2 * D], BF16, "Internal")
        out_d = S.dram("out", [SEQ, D], F32, "ExternalOutput")
        gates_d = S.dram("gates_scr", [SEQ, 2 * D], F32, "Internal")
        out_tok = [S.token(out_d.t, "out%d" % i) for i in range(NT)]
        gates_tok = [S.token(gates_d.t, "gts%d" % i) for i in range(NT)]

        ARENA = 126976
        ARENA2 = 36864
        es.enter_context(nc.sbuf_tensor("arena", [128, ARENA // 4], F32))
        es.enter_context(nc.sbuf_tensor("arena2", [128, ARENA2 // 4], F32))
        ARENA_BASE = int(nc.lookup_mloc("arena").addr)
        ARENA2_BASE = int(nc.lookup_mloc("arena2").addr)
        assert ARENA_BASE % 32 == 0 and ARENA2_BASE % 32 == 0

        def AT(name, shape, dt, offset):
            return S.sb_at(name, shape, dt, ARENA_BASE + offset)

        cur2 = [0]

        def A2(name, shape, dt):
            n = int(np.prod(shape[1:])) * (2 if dt == BF16 else 4)
            n = (n + 31) // 32 * 32
            t = S.sb_at(name, shape, dt, ARENA2_BASE + cur2[0])
            cur2[0] += n
            assert cur2[0] <= ARENA2, (name, cur2[0])
            return t

        K = 1024
        qTna = AT("qTna", [128, 4, SEQ], BF16, 0)
        kTna = AT("kTna", [128, 4, SEQ], BF16, 16 * K)
        vna = AT("vna", [128, NT, 512], BF16, 32 * K)
        qTsw = AT("qTsw", [128, 4, SEQ], BF16, 48 * K)
        kTsw = AT("kTsw", [128, 2, SEQ], BF16, 64 * K)
        vsw = AT("vsw", [128, NT, 128], BF16, 72 * K)
        xnT = AT("xnT", [128, 8, SEQ], BF16, 76 * K)
        wst = [AT("wst%d" % i, [128, 8, 512], BF16, 108 * K + i * 8 * K) for i in range(2)]
        Wna = AT("Wna", [128, 4, D], BF16, 76 * K)
        Wsw = AT("Wsw", [128, 4, D], BF16, 84 * K)
        Wout = AT("Wout", [128, 8, D], BF16, 92 * K)
        gt = [AT("gt%d" % i, [128, 2 * D], F32, 108 * K + i * 8 * K) for i in range(2)]
        Wk = AT("Wk", [128, 8, 2048], BF16, 0)
        GG = 4
        uv_r = Ring([AT("UV%d" % i, [128, GG, 2 * D], BF16, 32 * K + i * 16 * K) for i in range(4)])
        s_sb = AT("s_sb", [128, 2048], F32, 96 * K)
        ut_r = Ring([AT("UT%d" % i, [128, D], BF16, 114 * K + i * 2 * K) for i in range(5)])
        POOL_DOTS = False
        PE_DOTS = False
        TT_DOTS = True
        prod_r = Ring([AT("prod%d" % i, [128, D], F32, 104 * K + i * 4 * K) for i in range(5)])
        wq_t = [AT("wq_t%d" % i, [128, D], F32, 104 * K + i * 4 * K) for i in range(2)]
        kT_t = AT("kT_t", [128, 2, 128], F32, 112 * K)

        ident_f = S.sb("ident_f", [128, 128], F32)
        ident = S.sb("ident", [128, 128], BF16)
        epsb = S.sb("epsb", [128, 1], F32)
        iota16i = S.sb("iota16i", [128, 16], I32)
        iota16 = S.sb("iota16", [128, 16], F32)
        gvec = S.sb("gvec", [128, D], F32)
        gq = S.sb("gq", [128, 256], F32)
        gb = S.sb("gb", [128, 2 * D], F32)
        sinkb = S.sb("sinkb", [128, 8], F32)
        nsinkb = S.sb("nsinkb", [128, 8], F32)
        nB = S.sb("nB", [128, 2], F32)
        bnd = S.sb("bnd", [128, 4], F32)
        esc = S.sb("esc", [128, 8], F32)
        onesb = S.sb("onesb", [128, 1], BF16)
        rope = S.sb("rope_sb", [128, NT, 64], F32)
        swam = S.sb("swam_sb", [128, 384], F32)
        xt_r = Ring([S.sb("xt%d" % i, [128, D], F32) for i in range(2)])
        xn_r = Ring([S.sb("xn%d" % i, [128, D], BF16) for i in range(2)])
        junk = S.sb("junk", [128, D], BF16)
        junk2 = S.sb("junk2", [128, D], BF16)
        st_r = Ring([S.sb("st%d" % i, [128, 4], F32) for i in range(4)])
        cur2[0] = 0
        pj_r = Ring([A2("pj%d" % i, [128, 512], F32) for i in range(4)])
        sq_t = A2("sq_t", [128, 512], F32)
        hs_r = Ring([S.sb("hs%d" % i, [128, 16], F32) for i in range(4)])
        qn_r = Ring([A2("qn%d" % i, [128, 512], BF16) for i in range(4)])
        rt_r = Ring([A2("rt%d" % i, [128, 256], F32) for i in range(4)])
        gl_r = Ring([A2("gl%d" % i, [128, 512], F32) for i in range(4)])
        cur2[0] = 0
        bias_r = Ring([A2("nb%d" % i, [128, 640], F32) for i in range(4)])
        lg_r = Ring([A2("lg%d" % i, [128, 640], F32) for i in range(3)])
        P_r = Ring([A2("P%d" % i, [128, 640], BF16) for i in range(3)])
        PT_r = None
        ms_r = Ring([S.sb("ms%d" % i, [128, 4], F32) for i in range(4)])
        rsum_r = Ring([S.sb("rsum%d" % i, [128, 24], F32) for i in range(2)])
        osb = A2("osb", [128, 512], BF16)
        oT_r = Ring([A2("oT%d" % i, [128, 4, 128], BF16) for i in range(4)])
        m1 = A2("m1", [128, 512], F32)
        m2 = A2("m2", [128, 512], F32)
        mg = A2("mg", [128, D], BF16)
        mgT = A2("mgT", [128, 8, 128], BF16)
        xo_r = Ring([S.sb("xo%d" % i, [128, D], F32) for i in range(2)])
        cur2[0] = 0
        hf_b = [A2("hf%d" % i, [128, D], F32) for i in range(2)]
        xp = [xt_r.tiles[0], xt_r.tiles[1], A2("xp2", [128, D], F32)]
        eidi_b = [A2("eidi%d" % i, [128, 128], I32) for i in range(3)]
        esm_b = [A2("esmb%d" % i, [128, 128], F32) for i in range(3)]
        d_ring = Ring([A2("dts%d" % i, [128, 4], F32) for i in range(8)])
        w_ring = Ring([A2("wac%d" % i, [128, 4], F32) for i in range(8)])
        dgg_r = Ring([A2("dgg%d" % i, [128, 8, 128], BF16) for i in range(3)])
        pos = A2("pos", [128, 16], U32)
        pab = A2("pab", [128, 32], U32)
        pabf = A2("pabf", [128, 32], F32)
        i12 = A2("i12", [128, 32], F32)
        hT_b = [A2("hT%d" % i, [128, 8, 128], BF16) for i in range(2)]
        sv_t = [A2("sv%d" % i, [128, 16], F32) for i in range(16)]
        si_t = [A2("si%d" % i, [128, 16], U32) for i in range(16)]
        sif_t = [A2("sif%d" % i, [128, 16], F32) for i in range(16)]
        wk_t = [A2("wk128_%d" % i, [128, 128], F32) for i in range(2)]
        cand = A2("cand", [128, 256], F32)
        cand2 = A2("cand2", [128, 256], F32)
        jk256 = A2("jk256", [128, 256], F32)
        tops_t = [A2("tops%d" % i, [128, 16], F32) for i in range(8)]
        esm = A2("esm", [128, 8, 16], F32)
        zs = A2("zs", [128, 8], F32)
        eidf = A2("eidf", [128, 128], F32)

        PSA = S.ps("PSA", [128, 1024], F32)
        gram_r = None
        PSB = S.ps("PSB", [128, 1024], F32)
        PST = [S.ps("PST%d" % i, [128, 1024], BF16) for i in range(2)]
        PSO = S.ps("PSO", [128, 512], F32)
        PSX = S.ps("PSX", [128, 512], F32)
        pst_r = Ring(PST)
        gram_r = Ring([S.token(PSB.t, "gram0"), S.token(PSB.t, "gram1")])
        acc_r = Ring([PSA, PSB, PSO, PSX])

        S.add("pool", lambda e: e.memset(ident_f[:], 0.0), writes=[ident_f])
        S.add("pool", lambda e: e.affine_select(out=ident_f[:], in_=ident_f[:], pattern=[[-1, 128]],
                                                compare_op=ALU.not_equal, fill=1.0, base=0, channel_multiplier=1),
              reads=[ident_f], writes=[ident_f])
        S.add("dve", lambda e: e.tensor_copy(out=ident[:], in_=ident_f[:]), reads=[ident_f], writes=[ident])
        S.add("pool", lambda e: e.memset(epsb[:], EPS), writes=[epsb])
        S.add("pool", lambda e: e.memset(onesb[:], 1.0), writes=[onesb])
        S.add("pool", lambda e: e.iota(out=iota16i[:], pattern=[[1, 16]], base=0, channel_multiplier=0), writes=[iota16i])
        S.add("dve", lambda e: e.tensor_copy(out=iota16[:], in_=iota16i[:]), reads=[iota16i], writes=[iota16])
        S.dma(rope, rope[:], rope_d, rope_d.t.ap().rearrange("(t p) c -> p t c", p=128))
        S.dma(swam, swam[:], swam_d, swam_d[:, :])

        CONV_ROWS = 256
        NCH = NEXP // CONV_ROWS
        conv_todo = list(range(DEPTH * NCH))
        ptab16_tok = [S.token(ptab16.t, "ptab16_l%d" % i) for i in range(DEPTH)]

        def conv_some(k, after=None):
            for _ in range(k):
                if not conv_todo:
                    return
                c = conv_todo.pop(0)
                S.dma(ptab16_tok[c // NCH], ptab16[c * CONV_ROWS:(c + 1) * CONV_ROWS, :], ptab, ptab[c * CONV_ROWS:(c + 1) * CONV_ROWS, :],
                      eng="pool", group="conv", nobar=True, extra_reads=([after] if after is not None else []))

        def conv_flush(upto_layer):
            while conv_todo and conv_todo[0] // NCH <= upto_layer:
                conv_some(1)

        def transposes(src_t, src_aps, dst_t, dst_ap, n, eng="act"):
            pt = pst_r.next()

            def fn(e):
                for i in range(n):
                    ins = e.transpose(out=pt[:, i * 128:(i + 1) * 128], in_=src_aps[i], identity=ident[:])
                return ins

            S.add("pe", fn, reads=[src_t, ident], writes=[pt])
            src = pt[:, 0:n * 128]
            if len(dst_ap.shape) == 3:
                src = src.rearrange("p (c t) -> p c t", t=128)
            if eng == "act":
                S.add("act", lambda e: e.activation(out=dst_ap, in_=src, func=AF.Copy), reads=[pt], writes=[dst_t])
            else:
                S.add("dve", lambda e: e.tensor_copy(out=dst_ap, in_=src), reads=[pt], writes=[dst_t])

        def rmsnorm_tile(xt, gv, out_t, out_ap, out2_t=None, out2_ap=None):
            st = st_r.next()
            S.add("act", lambda e: e.activation(out=junk[:], in_=xt[:], func=AF.Square, accum_out=st[:, 0:1]),
                  reads=[xt], writes=[junk, st])
            S.add("act", lambda e: e.activation(out=st[:, 1:2], in_=st[:, 0:1], func=AF.Sqrt, bias=epsb[:, 0:1],
                                                scale=1.0 / D), reads=[st, epsb], writes=[st])
            S.add("dve", lambda e: e.reciprocal(out=st[:, 2:3], in_=st[:, 1:2]), reads=[st], writes=[st])
            S.add("dve", lambda e: e.scalar_tensor_tensor(out=out_ap, in0=xt[:], scalar=st[:, 2:3], in1=gv[:],
                                                          op0=ALU.mult, op1=ALU.mult),
                  reads=[xt, st, gv], writes=[out_t])
            if out2_t is not None:
                S.add("dve", lambda e: e.tensor_copy(out=out2_ap, in_=out_ap), reads=[out_t], writes=[out2_t])

        def headnorm(pj, nh, goff, out_t, out_ap3):
            hs = hs_r.next()
            n = nh * 64
            pj3 = pj[:, 0:n].rearrange("p (h d) -> p h d", d=64)
            S.add("dve", lambda e: e.tensor_tensor(out=sq_t[:, 0:n], in0=pj[:, 0:n], in1=pj[:, 0:n], op=ALU.mult),
                  reads=[pj], writes=[sq_t])
            S.add("dve", lambda e: e.tensor_reduce(out=hs[:, 0:nh], in_=sq_t[:, 0:n].rearrange("p (h d) -> p h d", d=64),
                                                   axis=AX.X, op=ALU.add), reads=[sq_t], writes=[hs])
            S.add("act", lambda e: e.activation(out=hs[:, 8:8 + nh], in_=hs[:, 0:nh], func=AF.Sqrt, bias=epsb[:, 0:1],
                                                scale=1.0 / 64), reads=[hs, epsb], writes=[hs])
            S.add("dve", lambda e: e.reciprocal(out=hs[:, 0:nh], in_=hs[:, 8:8 + nh]), reads=[hs], writes=[hs])
            S.add("dve", lambda e: e.tensor_tensor(out=pj3, in0=pj3, in1=hs[:, 0:nh].unsqueeze(2).to_broadcast([128, nh, 64]),
                                                   op=ALU.mult), reads=[pj, hs], writes=[pj])
            S.add("dve", lambda e: e.tensor_tensor(out=out_ap3, in0=pj3,
                                                   in1=gq[:, goff:goff + 64].unsqueeze(1).to_broadcast([128, nh, 64]),
                                                   op=ALU.mult), reads=[pj, gq], writes=[out_t])

        def rotary(pj, nh, tt, out_t, out_ap3):
            n = nh * 64
            pj3 = pj[:, 0:n].rearrange("p (h d) -> p h d", d=64)
            x1 = pj3[:, :, 0:32]
            x2 = pj3[:, :, 32:64]
            cs = rope[:, tt, 0:32].unsqueeze(1).to_broadcast([128, nh, 32])
            sn = rope[:, tt, 32:64].unsqueeze(1).to_broadcast([128, nh, 32])
            r = [rt_r.next() for _ in range(4)]
            rv = [t[:, 0:nh * 32].rearrange("p (h d) -> p h d", d=32) for t in r]
            S.add("dve", lambda e: e.tensor_tensor(out=rv[0], in0=x1, in1=cs, op=ALU.mult), reads=[pj, rope], writes=[r[0]])
            S.add("dve", lambda e: e.tensor_tensor(out=rv[1], in0=x2, in1=sn, op=ALU.mult), reads=[pj, rope], writes=[r[1]])
            S.add("dve", lambda e: e.tensor_tensor(out=rv[2], in0=x1, in1=sn, op=ALU.mult), reads=[pj, rope], writes=[r[2]])
            S.add("dve", lambda e: e.tensor_tensor(out=rv[3], in0=x2, in1=cs, op=ALU.mult), reads=[pj, rope], writes=[r[3]])
            S.add("dve", lambda e: e.tensor_tensor(out=out_ap3[:, :, 0:32], in0=rv[0], in1=rv[1], op=ALU.subtract),
                  reads=[r[0], r[1]], writes=[out_t])
            S.add("dve", lambda e: e.tensor_tensor(out=out_ap3[:, :, 32:64], in0=rv[2], in1=rv[3], op=ALU.add),
                  reads=[r[2], r[3], out_t], writes=[out_t])

        def attn_head(qT_t, qT_ap, kT_t, kT_ap, nk, v_t, v_aps, bias_t, bias_ap, sink_ap, sc, o_ap, rsum, hcol, grp):
            nkk = nk * 128

            def mm(e):
                ins = e.matmul(sc[:, 0:min(512, nkk)], lhsT=qT_ap, rhs=kT_ap[:, 0:min(512, nkk)], start=True, stop=True)
                if nkk > 512:
                    ins = e.matmul(sc[:, 512:nkk], lhsT=qT_ap, rhs=kT_ap[:, 512:nkk], start=True, stop=True)
                return ins

            S.add("pe", mm, reads=[qT_t, kT_t], writes=[sc])
            lg = lg_r.next()
            S.add("dve", lambda e: e.scalar_tensor_tensor(out=lg[:, 0:nkk], in0=sc[:, 0:nkk], scalar=0.125, in1=bias_ap,
                                                          op0=ALU.mult, op1=ALU.add), reads=[sc, bias_t], writes=[lg])
            ms = ms_r.next()
            S.add("dve", lambda e: e.tensor_reduce(out=ms[:, 0:1], in_=lg[:, 0:nkk], axis=AX.X, op=ALU.max),
                  reads=[lg], writes=[ms])
            if sink_ap is not None:
                S.add("dve", lambda e: e.tensor_tensor(out=ms[:, 0:1], in0=ms[:, 0:1], in1=sink_ap, op=ALU.max),
                      reads=[ms, sinkb], writes=[ms])
            S.add("dve", lambda e: e.tensor_scalar(out=ms[:, 1:2], in0=ms[:, 0:1], scalar1=-1.0, scalar2=None, op0=ALU.mult),
                  reads=[ms], writes=[ms])
            P = P_r.next()
            S.add("act", lambda e: e.activation(out=P[:, 0:nkk], in_=lg[:, 0:nkk], func=AF.Exp, bias=ms[:, 1:2], scale=1.0,
                                                accum_out=rsum[:, hcol:hcol + 1]), reads=[lg, ms], writes=[P, rsum])
            if sink_ap is not None:
                S.add("act", lambda e: e.activation(out=ms[:, 2:3], in_=sink_ap, func=AF.Exp, bias=ms[:, 1:2], scale=1.0),
                      reads=[ms, sinkb], writes=[ms])
                S.add("dve", lambda e: e.tensor_tensor(out=rsum[:, hcol:hcol + 1], in0=rsum[:, hcol:hcol + 1], in1=ms[:, 2:3],
                                                       op=ALU.add), reads=[rsum, ms], writes=[rsum])
            PT = PT_r.next()
            transposes(P, [P[:, j * 128:(j + 1) * 128] for j in range(nk)], PT, PT[:, 0:nkk], nk, eng="act")

            def pv(e):
                for j in range(nk):
                    ins = e.matmul(o_ap, lhsT=PT[:, j * 128:(j + 1) * 128], rhs=v_aps[j], start=(j == 0), stop=(j == nk - 1))
                return ins

            S.add("pe", pv, reads=[PT, v_t], writes=[PSO], group=grp)

        for l in range(n_layers):
            xsrc_tok = [x_in] * NT if l == 0 else out_tok
            xsrc = x_in if l == 0 else out_d
            if do_attn:
                S.dma(gvec, gvec[:], norm_mix, norm_mix[l:l + 1, :].to_broadcast([128, D]))
                S.dma(gq, gq[:], qk_norm, qk_norm[l:l + 1, :].to_broadcast([128, 256]))
                S.dma(gb, gb[:], gate_bias, gate_bias[l:l + 1, :].to_broadcast([128, 2 * D]))
                S.dma(sinkb, sinkb[:], swa_sink, swa_sink[l:l + 1, :].to_broadcast([128, 8]))
                S.add("dve", lambda e: e.tensor_reduce(out=bnd[:, 0:4], in_=gq[:, 0:256].rearrange("p (r d) -> p r d", d=64), axis=AX.X,
                                                       op=ALU.max, apply_absolute_value=True), reads=[gq], writes=[bnd])
                S.add("dve", lambda e: e.tensor_tensor(out=nB[:].rearrange("p (a b) -> p a b", b=1),
                                                       in0=bnd[:, 0:4].rearrange("p (a b) -> p a b", b=2)[:, :, 0:1],
                                                       in1=bnd[:, 0:4].rearrange("p (a b) -> p a b", b=2)[:, :, 1:2], op=ALU.mult),
                      reads=[bnd], writes=[nB])
                S.add("dve", lambda e: e.tensor_scalar(out=nB[:], in0=nB[:], scalar1=-8.0, scalar2=None, op0=ALU.mult),
                      reads=[nB], writes=[nB])
                S.add("act", lambda e: e.activation(out=esc[:, 0:8], in_=sinkb[:, 0:8], func=AF.Exp, bias=nB[:, 1:2], scale=1.0),
                      reads=[sinkb, nB], writes=[esc])
                for tt in range(NT):
                    xt = xt_r.next()
                    S.dma(xt, xt[:], xsrc_tok[tt], xsrc[tt * 128:(tt + 1) * 128, :])
                    xn = xn_r.next()
                    rmsnorm_tile(xt, gvec, xn, xn[:])
                    transposes(xn, [xn[:, c * 128:(c + 1) * 128] for c in range(8)], xnT,
                               xnT[:, :, tt * 128:(tt + 1) * 128], 8, eng="act")
                groups = [(0, 512, "naq"), (512, 512, "nak"), (1024, 512, "nav"), (1536, 512, "swq"), (2048, 256, "swkv"),
                          (2304, 512, "g"), (2816, 512, "g"), (3328, 512, "g"), (3840, 512, "g")]
                deferred = []
                ep_q = []

                def wload(gi_):
                    c0_, n_, _k = groups[gi_]
                    w_ = wst[gi_ % 2]
                    S.dma(w_, w_[:, :, 0:n_], w_in, w_in[l, :, c0_:c0_ + n_].rearrange("(c p) n -> p c n", p=128), eng="pool")
                wload(0)
                for gi, (c0, n, kind) in enumerate(groups):
                    w = wst[gi % 2]
                    if gi + 1 < len(groups):
                        wload(gi + 1)

                    for tt in range(NT):
                        acc = acc_r.next()

                        def mm(e, acc=acc, w=w, tt=tt, n=n):
                            for c in range(8):
                                ins = e.matmul(acc[:, 0:n], lhsT=xnT[:, c, tt * 128:(tt + 1) * 128], rhs=w[:, c, 0:n],
                                               start=(c == 0), stop=(c == 7))
                            return ins

                        ptok = T(None, "ptok")
                        S.add("pe", mm, reads=[xnT, w], writes=[acc, ptok])
                        if (gi * NT + tt) % 4 == 0 and conv_todo and conv_todo[0] < NCH:
                            conv_some(1, after=ptok)
                        while deferred:
                            deferred.pop(0)()
                        ep_prev = ep_q[:]
                        del ep_q[:]
                        if kind == "nav":
                            S.add("act", lambda e, acc=acc, tt=tt: e.activation(out=vna[:, tt, :], in_=acc[:, 0:512], func=AF.Copy),
                                  reads=[acc], writes=[vna])
                            for f_ in ep_prev:
                                f_()
                            continue
                        if kind == "g":
                            goff = c0 - 2304
                            gl = gl_r.next()
                            S.add("dve", lambda e, acc=acc, gl=gl, goff=goff: e.tensor_tensor(
                                out=gl[:], in0=acc[:, 0:512], in1=gb[:, goff:goff + 512], op=ALU.add),
                                reads=[acc, gb], writes=[gl])
                            for f_ in ep_prev:
                                f_()
                            S.add("act", lambda e, gl=gl: e.activation(out=gl[:], in_=gl[:], func=AF.Sigmoid),
                                  reads=[gl], writes=[gl])
                            S.dma(gates_tok[tt], gates_d[tt * 128:(tt + 1) * 128, goff:goff + 512], gl, gl[:],
                                  group=("g", l, tt))
                            continue
                        pj = pj_r.next()
                        S.add("act", lambda e, acc=acc, pj=pj, n=n: e.activation(out=pj[:, 0:n], in_=acc[:, 0:n], func=AF.Copy),
                              reads=[acc], writes=[pj])
                        for f_ in ep_prev:
                            f_()

                        def ep(kind=kind, pj=pj, tt=tt):
                            if kind in ("naq", "nak"):
                                qn = qn_r.next()
                                headnorm(pj, 8, 0 if kind == "naq" else 64, qn, qn[:].rearrange("p (h d) -> p h d", d=64))
                                dst = qTna if kind == "naq" else kTna
                                deferred.append(lambda qn=qn, dst=dst, tt=tt: transposes(
                                    qn, [qn[:, c * 128:(c + 1) * 128] for c in range(4)], dst,
                                    dst[:, :, tt * 128:(tt + 1) * 128], 4, eng="act"))
                            elif kind == "swq":
                                qn = qn_r.next()
                                headnorm(pj, 8, 128, pj, pj[:].rearrange("p (h d) -> p h d", d=64))
                                rotary(pj, 8, tt, qn, qn[:].rearrange("p (h d) -> p h d", d=64))
                                deferred.append(lambda qn=qn, tt=tt: transposes(
                                    qn, [qn[:, c * 128:(c + 1) * 128] for c in range(4)], qTsw,
                                    qTsw[:, :, tt * 128:(tt + 1) * 128], 4, eng="act"))
                            elif kind == "swkv":
                                qn = qn_r.next()
                                S.add("act", lambda e, pj=pj, tt=tt: e.activation(out=vsw[:, tt, :], in_=pj[:, 128:256], func=AF.Copy),
                                      reads=[pj], writes=[vsw])
                                headnorm(pj, 2, 192, pj, pj[:, 0:128].rearrange("p (h d) -> p h d", d=64))
                                rotary(pj, 2, tt, qn, qn[:, 0:128].rearrange("p (h d) -> p h d", d=64))
                                S.add("act", lambda e, qn=qn: e.activation(out=qn[:, 128:192], in_=qn[:, 64:128], func=AF.Copy),
                                      reads=[qn], writes=[qn])
                                S.add("act", lambda e, qn=qn: e.activation(out=qn[:, 192:256], in_=qn[:, 0:64], func=AF.Copy),
                                      reads=[qn], writes=[qn])
                                deferred.append(lambda qn=qn, tt=tt: transposes(
                                    qn, [qn[:, 0:128], qn[:, 128:256]], kTsw, kTsw[:, :, tt * 128:(tt + 1) * 128], 2, eng="act"))
                        ep_q.append(ep)
                for f_ in ep_q[:]:
                    f_()
                del ep_q[:]
                while deferred:
                    deferred.pop(0)()
                S.barrier()
                if dbg and l == 0:
                    for nm, tl in [("xnT", xnT), ("qTna", qTna), ("kTna", kTna), ("vna", vna), ("qTsw", qTsw), ("kTsw", kTsw), ("vsw", vsw)]:
                        shp = list(tl.t.shape)
                        dd = S.dram("dbg_" + nm, shp, F32, "ExternalOutput")
                        S.dma(dd, dd.t.ap(), tl, tl.t.ap(), eng="pool")
                    S.barrier()
                S.dma(Wna, Wna[:], w_bna, w_bna[l].rearrange("(c p) n -> p c n", p=128), eng="pool")
                S.dma(Wsw, Wsw[:], w_bsw, w_bsw[l].rearrange("(c p) n -> p c n", p=128), eng="pool")
                S.dma(Wout, Wout[:, 0:4, :], w_out, w_out[l, 0:512, :].rearrange("(c p) n -> p c n", p=128), eng="pool",
                      group=("wout", l))
                S.dma(Wout, Wout[:, 4:8, :], w_out, w_out[l, 512:1024, :].rearrange("(c p) n -> p c n", p=128), eng="pool",
                      group=("wout", l))
                sc_r = Ring([PSA, PSB])
                sums_tok = [S.token(PSA.t, "sums0"), S.token(PSB.t, "sums1")]
                PSY = PST[1]
                psy = PST[1][:, :].bitcast(F32)
                pst_r.tiles = [PST[0]]
                jobs = [(tt, br, h) for tt in range(NT) for br in range(2) for h in range(8)]
                NJ = len(jobs)
                jst = {}
                tst = {}
                later = []

                def stA(j, l=l):
                    tt, br, h = jobs[j]
                    if br == 0 and h == 0:
                        g_t = gt[tt % 2]
                        S.dma(g_t, g_t[:], gates_tok[tt], gates_d[tt * 128:(tt + 1) * 128, :])
                        xt = xt_r.next()
                        S.dma(xt, xt[:], xsrc_tok[tt], xsrc[tt * 128:(tt + 1) * 128, :])
                        tst[tt] = dict(g_t=g_t, xt=xt, oT=[None, None])
                    if h == 0:
                        tst[tt]["rsum%d" % br] = rsum_r.next()
                    sc = sc_r.next()
                    pr, hp = h // 2, (h % 2) * 64
                    d = dict(sc=sc, rsum=tst[tt]["rsum%d" % br])
                    if br == 0:
                        ws = min(max(tt - 2, 0), 11)
                        ty = {0: 0, 1: 1, 14: 3, 15: 4}.get(tt, 2)
                        nk = 5
                        bt = bias_r.next()
                        S.dma(bt, bt[:], nab, nab[l, ty, h, :, :])
                        qT_t, qT_ap = qTna, qTna[hp:hp + 64, pr, tt * 128:(tt + 1) * 128]
                        kT_t, kT_ap = kTna, kTna[hp:hp + 64, pr, ws * 128:(ws + 5) * 128]
                        d.update(nk=5, v_t=vna, v_aps=[vna[:, ws + jj, h * 64:(h + 1) * 64] for jj in range(5)],
                                 bias_t=bt, bias_ap=bt[:, 0:640], sink=None, negb=nB[:, 0:1])
                    else:
                        lo = max(tt - 1, 0)
                        hi = min(tt + 1, 15)
                        nk = hi - lo + 1
                        kvh = h // 4
                        slot = 0 if (hp // 64) == kvh else 1
                        moff = 128 if tt == 0 else 0
                        qT_t, qT_ap = qTsw, qTsw[hp:hp + 64, pr, tt * 128:(tt + 1) * 128]
                        kT_t, kT_ap = kTsw, kTsw[hp:hp + 64, slot, lo * 128:(hi + 1) * 128]
                        d.update(nk=nk, v_t=vsw, v_aps=[vsw[:, lo + jj, kvh * 64:(kvh + 1) * 64] for jj in range(nk)],
                                 bias_t=swam, bias_ap=swam[:, moff:moff + nk * 128], sink=sinkb[:, h:h + 1], negb=nB[:, 1:2])
                    nkk = nk * 128

                    def mm(e):
                        for kt in range(nk):
                            ins = e.matmul(sc[:, kt * 128:(kt + 1) * 128], lhsT=kT_ap[:, kt * 128:(kt + 1) * 128], rhs=qT_ap,
                                           start=True, stop=True)
                        return ins
                    ptok = T(None, "ptok")
                    S.add("pe", mm, reads=[qT_t, kT_t], writes=[sc, ptok])
                    if h == 0 and conv_todo and conv_todo[0] < NCH:
                        conv_some(1, after=ptok)
                    jst[j] = d

                def stB(j):
                    tt, br, h = jobs[j]
                    d = jst[j]
                    sc, rsum, sink_ap = d["sc"], d["rsum"], d["sink"]
                    nkk = d["nk"] * 128
                    lg = lg_r.next()
                    S.add("dve", lambda e: e.scalar_tensor_tensor(out=lg[:, 0:nkk], in0=sc[:, 0:nkk], scalar=0.125, in1=d["bias_ap"],
                                                                  op0=ALU.mult, op1=ALU.add), reads=[sc, d["bias_t"]], writes=[lg])
                    P = P_r.next()
                    S.add("act", lambda e: e.activation(out=P[:, 0:nkk], in_=lg[:, 0:nkk], func=AF.Exp, bias=d["negb"], scale=1.0),
                          reads=[lg, nB], writes=[P])
                    d["PT"] = P

                def stD(j, l=l):
                    tt, br, h = jobs[j]
                    d = jst.pop(j)
                    PT, nk, v_aps = d["PT"], d["nk"], d["v_aps"]

                    sc = d["sc"]
                    stok = sums_tok[0] if sc is PSA else sums_tok[1]

                    def pv(e):
                        for jj in range(nk):
                            e.matmul(PSO[:, h * 64:(h + 1) * 64], lhsT=PT[:, jj * 128:(jj + 1) * 128], rhs=v_aps[jj],
                                     start=(jj == 0), stop=(jj == nk - 1))
                        for jj in range(nk):
                            ins = e.matmul(sc[:, 640 + h:641 + h], lhsT=PT[:, jj * 128:(jj + 1) * 128], rhs=onesb[:, 0:1],
                                           start=(jj == 0), stop=(jj == nk - 1))
                        return ins
                    S.add("pe", pv, reads=[PT, d["v_t"], onesb], writes=[PSO, stok], group=("pso", l, tt, br))
                    assert (sc is PSA) == (h % 2 == 0)
                    if h != 7:
                        return
                    rsum = d["rsum"]
                    ts_ = tst[tt]
                    steps = []

                    def e1():
                        r2 = rsum[:, 0:8].rearrange("p (a b) -> p a b", b=2)
                        S.add("dve", lambda e: e.tensor_copy(out=r2[:, :, 0], in_=PSA[:, 640:648].rearrange("p (a b) -> p a b", b=2)[:, :, 0]),
                              reads=[sums_tok[0]], writes=[rsum])
                        S.add("dve", lambda e: e.tensor_copy(out=r2[:, :, 1], in_=PSB[:, 640:648].rearrange("p (a b) -> p a b", b=2)[:, :, 1]),
                              reads=[sums_tok[1], rsum], writes=[rsum])
                        if br == 1:
                            S.add("dve", lambda e: e.tensor_tensor(out=rsum[:, 0:8], in0=rsum[:, 0:8], in1=esc[:, 0:8], op=ALU.add),
                                  reads=[rsum, esc], writes=[rsum])
                        S.add("dve", lambda e: e.reciprocal(out=rsum[:, 8:16], in_=rsum[:, 0:8]), reads=[rsum], writes=[rsum])
                        S.add("dve", lambda e: e.tensor_tensor(
                            out=osb[:].rearrange("p (h d) -> p h d", d=64), in0=PSO[:].rearrange("p (h d) -> p h d", d=64),
                            in1=rsum[:, 8:16].unsqueeze(2).to_broadcast([128, 8, 64]), op=ALU.mult),
                            reads=[PSO, rsum], writes=[osb])

                    def e2():
                        oT = oT_r.next()
                        transposes(osb, [osb[:, c * 128:(c + 1) * 128] for c in range(4)], oT, oT[:], 4, eng="act")
                        ts_["oT"][br] = oT
                    e1()
                    steps += [e2]
                    if br == 1:
                        g_t, xt = ts_["g_t"], ts_["xt"]

                        def m_mm(n):
                            def mmb(e):
                                for (W, oTb, acc) in ((Wna, ts_["oT"][0], PSX[:, :]), (Wsw, ts_["oT"][1], psy)):
                                    for c in range(4):
                                        ins = e.matmul(acc, lhsT=oTb[:, c, :], rhs=W[:, c, n * 512:(n + 1) * 512],
                                                       start=(c == 0), stop=(c == 3))
                                return ins
                            S.add("pe", mmb, reads=[Wna, Wsw, ts_["oT"][0], ts_["oT"][1]], writes=[PSX, PSY])

                        def m_dve(n):
                            S.add("dve", lambda e: e.tensor_tensor(out=m1[:], in0=PSX[:], in1=g_t[:, n * 512:(n + 1) * 512], op=ALU.mult),
                                  reads=[PSX, g_t], writes=[m1])
                            S.add("dve", lambda e: e.tensor_tensor(out=m2[:], in0=psy, in1=g_t[:, D + n * 512:D + (n + 1) * 512], op=ALU.mult),
                                  reads=[PSY, g_t], writes=[m2])
                            S.add("dve", lambda e: e.tensor_tensor(out=mg[:, n * 512:(n + 1) * 512], in0=m1[:], in1=m2[:], op=ALU.add),
                                  reads=[m1, m2, mg], writes=[mg])

                        def m_tr():
                            transposes(mg, [mg[:, c * 128:(c + 1) * 128] for c in range(8)], mgT, mgT[:], 8, eng="act")
                            ts_["xo"] = xo_r.next()

                        def o_mm(n):
                            acc_t, acc = (PSX, PSX[:, :]) if n == 0 else (PSY, psy)

                            def mmo(e):
                                for c in range(8):
                                    ins = e.matmul(acc, lhsT=mgT[:, c, :], rhs=Wout[:, c, n * 512:(n + 1) * 512],
                                                   start=(c == 0), stop=(c == 7))
                                return ins
                            S.add("pe", mmo, reads=[Wout, mgT], writes=[acc_t])

                        def o_dve(n):
                            acc_t, acc = (PSX, PSX[:, :]) if n == 0 else (PSY, psy)
                            xo = ts_["xo"]
                            S.add("dve", lambda e: e.tensor_tensor(out=xo[:, n * 512:(n + 1) * 512], in0=acc,
                                                                   in1=xt[:, n * 512:(n + 1) * 512], op=ALU.add),
                                  reads=[acc_t, xt, xo], writes=[xo])

                        def fin():
                            S.dma(out_tok[tt], out_d[tt * 128:(tt + 1) * 128, :], ts_["xo"], ts_["xo"][:])
                            del tst[tt]
                        nop = None
                        steps += [lambda: m_mm(0), lambda: m_dve(0), lambda: m_mm(1), lambda: m_dve(1), m_tr,
                                  lambda: o_mm(0), lambda: o_mm(1), lambda: o_dve(0), lambda: (o_dve(1), fin())]
                    for i_, fn_ in enumerate(steps):
                        if fn_ is not None:
                            later.append((j + 1 + i_, fn_))
                    later.sort(key=lambda t_: t_[0])

                stA(0)
                stA(1)
                stB(0)
                for j in range(NJ):
                    if j + 2 < NJ:
                        stA(j + 2)
                    if j + 1 < NJ:
                        stB(j + 1)
                    stD(j)
                    while later and later[0][0] <= j:
                        later.pop(0)[1]()
                while later:
                    later.pop(0)[1]()
                pst_r.tiles = PST
                S.barrier()
            elif l == 0:
                for tt in range(NT):
                    xt = xt_r.next()
                    S.dma(xt, xt[:], x_in, x_in[tt * 128:(tt + 1) * 128, :])
                    S.dma(out_tok[tt], out_d[tt * 128:(tt + 1) * 128, :], xt, xt[:])
                S.barrier()
            if do_peer:
                S.dma(gvec, gvec[:], norm_ffn, norm_ffn[l:l + 1, :].to_broadcast([128, D]))
                S.dma(kT_t, kT_t[:], keysT, keysT[l].rearrange("p c n -> c p n"))
                for hp in range(16):
                    wq = wq_t[hp % 2]
                    S.dma(wq, wq[:], wqT, wqT[l, hp * 128:(hp + 1) * 128, :])
                    for half in range(2):
                        acc = acc_r.next()

                        def mmk(e, acc=acc, wq=wq, hp=hp, half=half):
                            for j in range(4):
                                dch = half * 4 + j
                                ins = e.matmul(acc[:, j * 128:(j + 1) * 128], lhsT=wq[:, dch * 128:(dch + 1) * 128],
                                               rhs=kT_t[:, hp % 2, :], start=True, stop=True)
                            return ins
                        S.add("pe", mmk, reads=[wq, kT_t], writes=[acc])
                        S.add("act", lambda e, acc=acc, hp=hp, half=half: e.activation(
                            out=Wk[:, half * 4:half * 4 + 4, hp * 128:(hp + 1) * 128],
                            in_=acc[:, 0:512].rearrange("p (c n) -> p c n", n=128), func=AF.Copy), reads=[acc], writes=[Wk])
                NG = 128 // GG
                hb_of = {}

                def s1_chunks(tt, l=l):
                    ch = []
                    xt = xp[tt % 3]
                    hfb = hf_b[tt % 2]

                    def a():
                        S.dma(xt, xt[:], out_tok[tt], out_d[tt * 128:(tt + 1) * 128, :])
                        xn = xn_r.next()
                        hb_of[tt] = xn
                        rmsnorm_tile(xt, gvec, hfb, hfb[:], xn, xn[:])
                        hT = hT_b[tt % 2]
                        transposes(xn, [xn[:, c * 128:(c + 1) * 128] for c in range(8)], hT, hT[:], 8, eng="act")
                        for half in range(2):
                            for q2 in range(2):
                                q4 = half * 2 + q2
                                col = q2 * 512

                                def mms(e, col=col, q4=q4, hT=hT):
                                    for c in range(8):
                                        ins = e.matmul(PSA[:, col:col + 512], lhsT=hT[:, c, :], rhs=Wk[:, c, q4 * 512:(q4 + 1) * 512],
                                                       start=(c == 0), stop=(c == 7))
                                    return ins
                                S.add("pe", mms, reads=[hT, Wk], writes=[PSA], group=("s", l, tt, half))
                            S.add("act", lambda e, half=half: e.activation(out=s_sb[:, half * 1024:(half + 1) * 1024], in_=PSA[:], func=AF.Copy),
                                  reads=[PSA], writes=[s_sb], group=("ssb", l, tt))
                    ch.append(a)

                    def oplist():
                        ops = []

                        def A_(*args, **kw):
                            ops.append(lambda: S.add(*args, **kw))
                        return ops, A_

                    def b(hp):
                        ops, A_ = oplist()
                        sl = s_sb[:, hp * 128:(hp + 1) * 128]
                        svt, sit, sft, wk = sv_t[hp], si_t[hp], sif_t[hp], wk_t[hp % 2]
                        A_("dve", lambda e: e.max(out=svt[:, 0:8], in_=sl), reads=[s_sb], writes=[svt])
                        A_("dve", lambda e: e.max_index(out=sit[:, 0:8], in_max=svt[:, 0:8], in_values=sl),
                           reads=[s_sb, svt], writes=[sit])
                        A_("dve", lambda e: e.match_replace(out=wk[:], in_to_replace=svt[:, 0:8], in_values=sl,
                                                            imm_value=-1e30), reads=[s_sb, svt], writes=[wk])
                        A_("dve", lambda e: e.max(out=svt[:, 8:16], in_=wk[:]), reads=[wk, svt], writes=[svt])
                        A_("dve", lambda e: e.max_index(out=sit[:, 8:16], in_max=svt[:, 8:16], in_values=wk[:]),
                           reads=[wk, svt, sit], writes=[sit])
                        A_("dve", lambda e: e.tensor_copy(out=sft[:], in_=sit[:]), reads=[sit], writes=[sft])
                        return ops

                    def c(h):
                        ops, A_ = oplist()
                        tp = tops_t[h]
                        c3 = cand[:].rearrange("p (a b) -> p a b", b=16)
                        A_("dve", lambda e: e.tensor_tensor(
                            out=c3, in0=sv_t[2 * h][:, :].unsqueeze(2).to_broadcast([128, 16, 16]),
                            in1=sv_t[2 * h + 1][:, :].unsqueeze(1).to_broadcast([128, 16, 16]), op=ALU.add),
                            reads=[sv_t[2 * h], sv_t[2 * h + 1]], writes=[cand])
                        A_("dve", lambda e: e.max(out=tp[:, 0:8], in_=cand[:]), reads=[cand], writes=[tp])
                        A_("dve", lambda e: e.max_index(out=pos[:, 0:8], in_max=tp[:, 0:8], in_values=cand[:]),
                           reads=[cand, tp], writes=[pos])
                        A_("dve", lambda e: e.match_replace(out=cand2[:], in_to_replace=tp[:, 0:8], in_values=cand[:],
                                                            imm_value=-1e30), reads=[cand, tp], writes=[cand2])
                        A_("dve", lambda e: e.max(out=tp[:, 8:16], in_=cand2[:]), reads=[cand2, tp], writes=[tp])
                        A_("dve", lambda e: e.max_index(out=pos[:, 8:16], in_max=tp[:, 8:16], in_values=cand2[:]),
                           reads=[cand2, tp, pos], writes=[pos])
                        A_("dve", lambda e: e.tensor_single_scalar(out=pab[:, 0:16], in_=pos[:], scalar=4, op=ALU.logical_shift_right),
                           reads=[pos], writes=[pab])
                        A_("dve", lambda e: e.tensor_single_scalar(out=pab[:, 16:32], in_=pos[:], scalar=15, op=ALU.bitwise_and),
                           reads=[pos, pab], writes=[pab])
                        A_("dve", lambda e: e.tensor_copy(out=pabf[:], in_=pab[:]), reads=[pab], writes=[pabf])
                        for w_, (srcrow, col0) in enumerate(((2 * h, 0), (2 * h + 1, 16))):
                            oh = jk256[:].rearrange("p (k a) -> p k a", a=16)
                            A_("dve", lambda e, col0=col0, oh=oh: e.tensor_tensor(
                                out=oh, in0=iota16[:].unsqueeze(1).to_broadcast([128, 16, 16]),
                                in1=pabf[:, col0:col0 + 16].unsqueeze(2).to_broadcast([128, 16, 16]), op=ALU.is_equal),
                                reads=[iota16, pabf], writes=[jk256])
                            A_("dve", lambda e, srcrow=srcrow, oh=oh: e.tensor_tensor(
                                out=oh, in0=oh, in1=sif_t[srcrow][:, :].unsqueeze(1).to_broadcast([128, 16, 16]), op=ALU.mult),
                                reads=[jk256, sif_t[srcrow]], writes=[jk256])
                            A_("dve", lambda e, w_=w_, oh=oh: e.tensor_reduce(out=i12[:, w_ * 16:(w_ + 1) * 16], in_=oh, axis=AX.X, op=ALU.add),
                               reads=[jk256, i12], writes=[i12])
                        A_("dve", lambda e: e.scalar_tensor_tensor(out=eidf[:, h * 16:(h + 1) * 16], in0=i12[:, 0:16], scalar=128.0,
                                                                   in1=i12[:, 16:32], op0=ALU.mult, op1=ALU.add),
                           reads=[i12, eidf], writes=[eidf])
                        ms = ms_r.next()
                        A_("dve", lambda e: e.tensor_scalar(out=ms[:, 0:1], in0=tp[:, 0:1], scalar1=-1.0, scalar2=None,
                                                            op0=ALU.mult), reads=[tp], writes=[ms])
                        A_("act", lambda e: e.activation(out=esm[:, h, :], in_=tp[:, :], func=AF.Exp, bias=ms[:, 0:1],
                                                         scale=1.0, accum_out=zs[:, h:h + 1]),
                           reads=[tp, ms], writes=[esm, zs])
                        return ops

                    def rr(*lists):
                        out_, lists = [], [list(x) for x in lists]
                        while any(lists):
                            for x in lists:
                                if x:
                                    out_.append(x.pop(0))
                        return out_

                    ch += rr(b(0), b(1))
                    for h in range(8):
                        if h < 7:
                            ch += rr(c(h), b(2 * h + 2), b(2 * h + 3))
                        else:
                            ch += c(h)

                    def d():
                        eb = esm_b[tt % 3]
                        ei = eidi_b[tt % 3]
                        S.add("dve", lambda e: e.reciprocal(out=zs[:], in_=zs[:]), reads=[zs], writes=[zs])
                        S.add("dve", lambda e: e.tensor_tensor(out=eb[:].rearrange("p (h k) -> p h k", k=16), in0=esm[:],
                                                               in1=zs[:].unsqueeze(2).to_broadcast([128, 8, 16]),
                                                               op=ALU.mult), reads=[esm, zs], writes=[eb])
                        S.add("dve", lambda e: e.tensor_scalar(out=eidf[:], in0=eidf[:], scalar1=float(NEXP - 1), scalar2=float(l * NEXP),
                                                               op0=ALU.min, op1=ALU.add), reads=[eidf], writes=[eidf])
                        S.add("dve", lambda e: e.tensor_copy(out=ei[:], in_=eidf[:]), reads=[eidf], writes=[ei])
                    ch.append(d)
                    return ch

                for cfn in s1_chunks(0):
                    cfn()
                conv_flush(l)
                NGRP = 128 // GG
                NG_ALL = NT * NGRP
                uv_of = {}

                def stG(G, l=l):
                    tt, g = divmod(G, NGRP)
                    UV = uv_r.next()
                    uv_of[G] = UV
                    ei = eidi_b[tt % 3]
                    for j in range(GG):
                        hk = g * GG + j
                        S.add("pool", lambda e, UV=UV, j=j, hk=hk, ei=ei: e.indirect_dma_start(
                            out=UV[:, j, :], out_offset=None, in_=ptab16[:, :],
                            in_offset=bass.IndirectOffsetOnAxis(ap=ei[:, hk:hk + 1], axis=0)),
                            reads=[ei, ptab16_tok[l]], writes=[UV], dma=True, group=("UV", l, G))

                def stDots(G, l=l):
                    tt, g = divmod(G, NGRP)
                    UV = uv_of[G]
                    hfb = hf_b[tt % 2]
                    dt_ = d_ring.next()
                    wa = w_ring.next()
                    uv_of[G] = (UV, wa)
                    if PE_DOTS:
                        hT = hT_b[tt % 2]
                        gram = gram_r.next()
                        gcol = 0 if gram.name == "gram0" else 512
                        UTs = [ut_r.next() for _ in range(GG)]

                        def Tj(j):
                            transposes(UV, [UV[:, j, c * 128:(c + 1) * 128] for c in range(8)], UTs[j], UTs[j][:], 8, eng="act")

                        def MMj(j):
                            def mmg(e):
                                for c in range(8):
                                    ins = e.matmul(PSB[:, gcol + j * 128:gcol + (j + 1) * 128], lhsT=hT[:, c, :],
                                                   rhs=UTs[j][:, c * 128:(c + 1) * 128], start=(c == 0), stop=(c == 7))
                                return ins
                            S.add("pe", mmg, reads=[hT, UTs[j]], writes=[gram], group=("gram", l, G))
                        Tj(0); Tj(1); MMj(0); Tj(2); MMj(1); Tj(3); MMj(2); MMj(3)
                        g3 = PSB[:, gcol:gcol + 512].rearrange("p (j t) -> p j t", t=128)
                        S.add("dve", lambda e: e.tensor_tensor(out=gmul[:].rearrange("p (j t) -> p j t", t=128), in0=g3,
                                                               in1=ident_f[:].unsqueeze(1).to_broadcast([128, GG, 128]), op=ALU.mult),
                              reads=[gram, ident_f], writes=[gmul])
                        S.add("dve", lambda e: e.tensor_reduce(out=dt_[:, 0:GG], in_=gmul[:].rearrange("p (j t) -> p j t", t=128),
                                                               axis=AX.X, op=ALU.add), reads=[gmul], writes=[dt_])
                        S.add("act", lambda e: e.activation(out=wa[:, 0:GG], in_=dt_[:, 0:GG], func=AF.Gelu),
                              reads=[dt_], writes=[wa])
                        return
                    if TT_DOTS:
                        hb = hb_of[tt]
                        for j in range(GG):
                            pr_ = prod_r.next()
                            S.add("dve", lambda e, UV=UV, j=j, pr_=pr_: e.tensor_tensor(out=pr_[:], in0=UV[:, j, 0:D], in1=hb[:], op=ALU.mult),
                                  reads=[UV, hb], writes=[pr_])
                            S.add("act", lambda e, j=j, pr_=pr_: e.activation(out=pr_[:], in_=pr_[:], func=AF.Copy, accum_out=dt_[:, j:j + 1]),
                                  reads=[pr_], writes=[dt_], group=("dt", l, G))
                            for _ in range(2):
                                if pend_ops:
                                    pend_ops.pop(0)()
                        S.add("act", lambda e: e.activation(out=wa[:, 0:GG], in_=dt_[:, 0:GG], func=AF.Gelu),
                              reads=[dt_], writes=[wa])
                        return
                    for j in range(GG):
                        if j == GG - 1 and POOL_DOTS:
                            pr_ = prod_r.next()
                            S.add("pool", lambda e, UV=UV, j=j, pr_=pr_: e.tensor_tensor(out=pr_[:], in0=UV[:, j, 0:D], in1=hfb[:], op=ALU.mult),
                                  reads=[UV, hfb], writes=[pr_])
                            S.add("act", lambda e, j=j, pr_=pr_: e.activation(out=junk2[:], in_=pr_[:], func=AF.Copy, accum_out=dt_[:, j:j + 1]),
                                  reads=[pr_], writes=[dt_], group=("dt", l, G))
                            continue
                        S.add("dve", lambda e, UV=UV, j=j: e.scalar_tensor_tensor(
                            out=junk[:], in0=UV[:, j, 0:D], scalar=1.0, in1=hfb[:], op0=ALU.mult, op1=ALU.mult,
                            accum_out=dt_[:, j:j + 1]), reads=[UV, hfb], writes=[dt_], group=("dt", l, G))
                    S.add("act", lambda e: e.activation(out=wa[:, 0:GG], in_=dt_[:, 0:GG], func=AF.Gelu),
                          reads=[dt_], writes=[wa])

                def stMM(G, l=l):
                    tt, g = divmod(G, NGRP)
                    UV, wa = uv_of.pop(G)
                    eb = esm_b[tt % 3]
                    dg = dgg_r.next()
                    S.add("dve", lambda e: e.tensor_tensor(out=wa[:, 0:GG], in0=wa[:, 0:GG],
                                                           in1=eb[:, g * GG:(g + 1) * GG], op=ALU.mult), reads=[wa, eb], writes=[wa])
                    S.add("dve", lambda e: e.tensor_tensor(
                        out=dg[:, 0:GG, :], in0=ident_f[:].unsqueeze(1).to_broadcast([128, GG, 128]),
                        in1=wa[:, 0:GG].unsqueeze(2).to_broadcast([128, GG, 128]), op=ALU.mult),
                        reads=[ident_f, wa], writes=[dg])

                    def mmv(e):
                        for j in range(GG):
                            hk = g * GG + j
                            e.matmul(PSO[:, 0:512], lhsT=dg[:, j, :], rhs=UV[:, j, D:D + 512], start=(hk == 0), stop=(hk == 127))
                            ins = e.matmul(PSX[:, 0:512], lhsT=dg[:, j, :], rhs=UV[:, j, D + 512:2 * D], start=(hk == 0), stop=(hk == 127))
                        return ins
                    gtok = T(None, "gtok")
                    S.add("pe", mmv, reads=[dg, UV], writes=[PSO, PSX, gtok], group=("pv", l, tt))
                    if G % 6 == 0 and conv_todo:
                        conv_some(1, after=gtok)
                    if g == NGRP - 1:
                        xt = xp[tt % 3]
                        xo = xo_r.next()
                        S.add("dve", lambda e: e.tensor_tensor(out=xo[:, 0:512], in0=PSO[:], in1=xt[:, 0:512], op=ALU.add),
                              reads=[PSO, xt], writes=[xo])
                        S.add("dve", lambda e: e.tensor_tensor(out=xo[:, 512:1024], in0=PSX[:], in1=xt[:, 512:1024], op=ALU.add),
                              reads=[PSX, xt, xo], writes=[xo])
                        S.dma(out_tok[tt], out_d[tt * 128:(tt + 1) * 128, :], xo, xo[:])

                LOOK = 3
                pend_ops = []
                for G in range(min(LOOK, NG_ALL)):
                    stG(G)
                stDots(0)
                for G in range(NG_ALL):
                    tt, g = divmod(G, NGRP)
                    if g == 0 and tt + 1 < NT:
                        assert not pend_ops
                        pend_ops.extend(s1_chunks(tt + 1))
                    if G + LOOK < NG_ALL:
                        stG(G + LOOK)
                    if G + 1 < NG_ALL:
                        stDots(G + 1)
                    stMM(G)
                    for _ in range(2):
                        if pend_ops:
                            pend_ops.pop(0)()
                    if g == 27:
                        while pend_ops:
                            pend_ops.pop(0)()
                S.barrier()
        S.emit(final_wait_tiles=out_tok)
    return nc


def _na_bias_tables(rpb):
    L = rpb.shape[0]
    out = np.empty((L, 5, 8, 128, 640), np.float32)
    q = np.arange(128)
    kk = np.arange(640)
    for ty, (tt, ws) in enumerate([(0, 0), (1, 0), (2, 0), (14, 11), (15, 11)]):
        qr = 2 * tt + q // 64
        qc = q % 64
        kr = 2 * ws + kk // 64
        kc = kk % 64
        rs = np.clip(qr - 4, 0, 24)
        cs = np.clip(qc - 8, 0, 48)
        vr = (kr[None, :] >= rs[:, None]) & (kr[None, :] < rs[:, None] + 8)
        vc = (kc[None, :] >= cs[:, None]) & (kc[None, :] < cs[:, None] + 16)
        valid = vr & vc
        dr = np.clip(kr[None, :] - qr[:, None] + 7, 0, 14)
        dc = np.clip(kc[None, :] - qc[:, None] + 15, 0, 30)
        g = rpb[:, :, dr, dc]
        out[:, ty] = np.where(valid[None, None], g, np.float32(NEG))
    out = out.reshape(L, 5, 8, 128, 5, 128).transpose(0, 1, 2, 5, 4, 3)
    return np.ascontiguousarray(out).reshape(L, 5, 8, 128, 640)


def _const_tables():
    half = 32
    inv = (10000.0 ** (-np.arange(half, dtype=np.float32) / half)).astype(np.float32)
    ang = np.arange(SEQ, dtype=np.float32)[:, None] * inv[None, :]
    rope = np.concatenate([np.cos(ang), np.sin(ang)], axis=1).astype(np.float32)
    q = np.arange(128)[:, None]
    c = np.arange(384)[None, :]
    swam = np.where(np.abs(q + 128 - c) <= 128, 0.0, NEG).astype(np.float32)
    swam = np.ascontiguousarray(swam.reshape(128, 3, 128).transpose(2, 1, 0)).reshape(128, 384)
    return rope, swam


_NC_CACHE = {}


def kernel(x, norm_mix, w_in, gate_bias, qk_norm, na_rpb, swa_sink, w_branch_na, w_branch_swa,
           w_out, norm_ffn, peer_query, peer_sub_keys, peer_down, peer_up):
    f = lambda a: np.ascontiguousarray(np.asarray(a, dtype=np.float32))
    x = f(x)
    rope, swam = _const_tables()
    shared = {
        "norm_mix": f(norm_mix), "norm_ffn": f(norm_ffn),
        "gate_bias": f(gate_bias).reshape(DEPTH, 2 * D),
        "qk_norm": f(qk_norm).reshape(DEPTH, 256),
        "swa_sink": f(swa_sink),
        "w_in": f(w_in), "w_bna": f(w_branch_na), "w_bsw": f(w_branch_swa), "w_out": f(w_out),
        "nab": _na_bias_tables(f(na_rpb)),
        "rope": rope, "swam": swam,
        "wqT": np.ascontiguousarray(np.transpose(f(peer_query), (0, 2, 1))),
        "keysT": np.ascontiguousarray(np.transpose(f(peer_sub_keys), (0, 1, 3, 2))),
        "ptab": np.concatenate([f(peer_down).reshape(DEPTH * NEXP, D), f(peer_up).reshape(DEPTH * NEXP, D)], axis=1),
    }
    if "nc" not in _NC_CACHE:
        _NC_CACHE["nc"] = build_program()
    nc = _NC_CACHE["nc"]
    in_maps = []
    for b in range(8):
        m = dict(shared)
        m["x"] = x[b]
        in_maps.append(m)
    res = run_bass_kernel_spmd(nc, in_maps, core_ids=list(range(8)))
    return np.stack([np.asarray(r["out"], dtype=np.float32) for r in res.results], axis=0)
```

```python
import numpy as np
import concourse.bass as bass
import concourse.mybir as mybir
from concourse.bass_utils import run_bass_kernel_spmd
from contextlib import ExitStack

F32 = mybir.dt.float32
BF16 = mybir.dt.bfloat16
I32 = mybir.dt.int32
U32 = mybir.dt.uint32
ALU = mybir.AluOpType
AF = mybir.ActivationFunctionType
AX = mybir.AxisListType

D = 1024
SEQ = 2048
NT = 16
DEPTH = 2
INC = 4352
NEXP = 16384
EPS = 1e-6
NEG = -30000.0


class T:
    def __init__(self, t, name):
        self.t = t
        self.name = name
        self.writers = []
        self.readers = []
        self.group = None
        self.sem = None
        self.cnt = 0

    def __getitem__(self, idx):
        return self.t[idx]


class Op:
    __slots__ = ("eng", "fn", "deps", "dma", "sem", "val", "needed")

    def __init__(self, eng, fn, dma):
        self.eng = eng
        self.fn = fn
        self.dma = dma
        self.deps = []
        self.sem = None
        self.val = 0
        self.needed = False


class Sched:
    ENGS = ["pe", "act", "dve", "pool", "sp"]

    def __init__(self, nc, es, same_engine_sync=True):
        self.nc = nc
        self.es = es
        self.prog = {e: [] for e in self.ENGS}
        self.esem = {e: es.enter_context(nc.semaphore("sem_" + e)) for e in ["pe", "act", "dve", "pool"]}
        self.ses = same_engine_sync
        self.pending_dma = []
        self.semcache = {}

    def sb(self, name, shape, dt):
        return T(self.es.enter_context(self.nc.sbuf_tensor(name, list(shape), dt)), name)

    def sb_at(self, name, shape, dt, offset):
        return T(self.nc.alloc_sbuf_tensor_at(name, list(shape), dt, offset=offset), name)

    def ps(self, name, shape, dt):
        return T(self.es.enter_context(self.nc.psum_tensor(name, list(shape), dt)), name)

    def dram(self, name, shape, dt, kind="Internal"):
        return T(self.nc.dram_tensor(name, list(shape), dt, kind=kind), name)

    def token(self, t, name):
        return T(t, name)

    def add(self, eng, fn, reads=(), writes=(), dma=False, group=None, semkey=None, nobar=False):
        op = Op(eng, fn, dma)
        deps = []
        for t in reads:
            deps.extend(t.writers)
        for t in writes:
            if group is not None and t.group == group:
                continue
            deps.extend(t.writers)
            deps.extend(t.readers)
        seen = set()
        for d in deps:
            if d is op or id(d) in seen:
                continue
            seen.add(id(d))
            op.deps.append(d)
        for t in writes:
            if group is not None and t.group == group:
                t.writers.append(op)
            else:
                t.writers = [op]
                t.readers = []
                t.group = group
        wset = set(id(t) for t in writes)
        for t in reads:
            if id(t) not in wset:
                t.readers.append(op)
        if dma:
            t0 = writes[0]
            if t0.sem is None:
                key = semkey if semkey is not None else ("t", id(t0))
                if key not in self.semcache:
                    self.semcache[key] = [self.es.enter_context(self.nc.semaphore("dsem%d" % len(self.semcache))), 0]
                t0.sem = self.semcache[key]
            t0.sem[1] += 16
            op.sem = t0.sem[0]
            op.val = t0.sem[1]
            if not nobar:
                self.pending_dma.append(op)
        self.prog[eng].append(op)
        return op

    def dma(self, out_t, out_ap, in_t, in_ap, eng="sp", group=None, extra_reads=(), semkey=None, nobar=False):
        return self.add(eng, lambda e: e.dma_start(out=out_ap, in_=in_ap), reads=[in_t] + list(extra_reads),
                        writes=[out_t], dma=True, group=group, semkey=semkey, nobar=nobar)

    def barrier(self):
        lasts = []
        for e in ["pe", "act", "dve", "pool"]:
            for op in reversed(self.prog[e]):
                if not op.dma:
                    lasts.append(op)
                    break
        pend = self.pending_dma
        self.pending_dma = []
        tok = T(None, "bar")
        for e in self.ENGS:
            op = Op(e, None, False)
            op.deps = [d for d in lasts if d.eng != e] + list(pend)
            self.prog[e].append(op)

    def emit(self, final_wait_tiles=()):
        nc = self.nc
        for e in self.ENGS:
            for op in self.prog[e]:
                for d in op.deps:
                    if not d.dma:
                        if d.eng == op.eng and (not self.ses or d.eng == "pe"):
                            continue
                        d.needed = True
        for e in ["pe", "act", "dve", "pool"]:
            c = 0
            for op in self.prog[e]:
                if op.dma or op.fn is None:
                    continue
                if op.needed:
                    c += 1
                    op.sem = self.esem[e]
                    op.val = c
        prog = self.prog
        ses = self.ses
        final_ops = []
        for t in final_wait_tiles:
            final_ops.extend(t.writers)

        with nc.Block() as block:

            def make(ename):
                def body(eng):
                    known = {}
                    for op in prog[ename]:
                        need = {}
                        for d in op.deps:
                            if not d.dma and d.eng == ename and (not ses or ename == "pe"):
                                continue
                            if d.sem is None:
                                continue
                            k = id(d.sem)
                            if k not in need or need[k][1] < d.val:
                                need[k] = (d.sem, d.val)
                        for k, (sm, v) in need.items():
                            if known.get(k, 0) >= v:
                                continue
                            eng.wait_ge(sm, v)
                            known[k] = v
                        if op.fn is None:
                            continue
                        ins = op.fn(eng)
                        if op.dma:
                            ins.then_inc(op.sem, 16)
                        elif op.needed:
                            ins.then_inc(op.sem, 1)
                    if ename == "sp":
                        need = {}
                        for d in final_ops:
                            k = id(d.sem)
                            if k not in need or need[k][1] < d.val:
                                need[k] = (d.sem, d.val)
                        for k, (sm, v) in need.items():
                            if known.get(k, 0) >= v:
                                continue
                            eng.wait_ge(sm, v)
                            known[k] = v

                return body

            block.tensor(make("pe"))
            block.scalar(make("act"))
            block.vector(make("dve"))
            block.gpsimd(make("pool"))
            block.sync(make("sp"))


class Ring:
    def __init__(self, tiles):
        self.tiles = tiles
        self.i = 0

    def next(self):
        t = self.tiles[self.i % len(self.tiles)]
        self.i += 1
        return t


def build_program(n_layers=DEPTH, do_attn=True, do_peer=True, dbg=False, peer_idx_only=False, peer_stage=3):
    nc = bass.Bass("TRN2", target_bir_lowering=False)
    es = ExitStack()
    with es:
        S = Sched(nc, es)
        EI = "ExternalInput"
        x_in = S.dram("x", [SEQ, D], F32, EI)
        norm_mix = S.dram("norm_mix", [DEPTH, D], F32, EI)
        norm_ffn = S.dram("norm_ffn", [DEPTH, D], F32, EI)
        gate_bias = S.dram("gate_bias", [DEPTH, 2 * D], F32, EI)
        qk_norm = S.dram("qk_norm", [DEPTH, 256], F32, EI)
        swa_sink = S.dram("swa_sink", [DEPTH, 8], F32, EI)
        w_in = S.dram("w_in", [DEPTH, D, INC], F32, EI)
        w_bna = S.dram("w_bna", [DEPTH, 512, D], F32, EI)
        w_bsw = S.dram("w_bsw", [DEPTH, 512, D], F32, EI)
        w_out = S.dram("w_out", [DEPTH, D, D], F32, EI)
        nab = S.dram("nab", [DEPTH, 5, 8, 128, 640], F32, EI)
        rope_d = S.dram("rope", [SEQ, 64], F32, EI)
        swam_d = S.dram("swam", [128, 384], F32, EI)
        wqT = S.dram("wqT", [DEPTH, 2048, D], F32, EI)
        keysT = S.dram("keysT", [DEPTH, 2, 128, 128], F32, EI)
        ptab = S.dram("ptab", [DEPTH * NEXP, 2 * D], F32, EI)
        ptab16 = S.dram("ptab16", [DEPTH * NEXP, 2 * D], BF16, "Internal")
        out_d = S.dram("out", [SEQ, D], F32, "ExternalOutput")
        gates_d = S.dram("gates_scr", [SEQ, 2 * D], F32, "Internal")
        out_tok = [S.token(out_d.t, "out%d" % i) for i in range(NT)]
        gates_tok = [S.token(gates_d.t, "gts%d" % i) for i in range(NT)]

        ARENA = 126976
        ARENA2 = 36864
        es.enter_context(nc.sbuf_tensor("arena", [128, ARENA // 4], F32))
        es.enter_context(nc.sbuf_tensor("arena2", [128, ARENA2 // 4], F32))
        ARENA_BASE = int(nc.lookup_mloc("arena").addr)
        ARENA2_BASE = int(nc.lookup_mloc("arena2").addr)
        assert ARENA_BASE % 32 == 0 and ARENA2_BASE % 32 == 0

        def AT(name, shape, dt, offset):
            return S.sb_at(name, shape, dt, ARENA_BASE + offset)

        cur2 = [0]

        def A2(name, shape, dt):
            n = int(np.prod(shape[1:])) * (2 if dt == BF16 else 4)
            n = (n + 31) // 32 * 32
            t = S.sb_at(name, shape, dt, ARENA2_BASE + cur2[0])
            cur2[0] += n
            assert cur2[0] <= ARENA2, (name, cur2[0])
            return t

        K = 1024
        qTna = AT("qTna", [128, 4, SEQ], BF16, 0)
        kTna = AT("kTna", [128, 4, SEQ], BF16, 16 * K)
        vna = AT("vna", [128, NT, 512], BF16, 32 * K)
        qTsw = AT("qTsw", [128, 4, SEQ], BF16, 48 * K)
        kTsw = AT("kTsw", [128, 2, SEQ], BF16, 64 * K)
        vsw = AT("vsw", [128, NT, 128], BF16, 72 * K)
        xnT = AT("xnT", [128, 8, SEQ], BF16, 76 * K)
        wst = [AT("wst%d" % i, [128, 8, 512], BF16, 108 * K + i * 8 * K) for i in range(2)]
        Wna = AT("Wna", [128, 4, D], BF16, 76 * K)
        Wsw = AT("Wsw", [128, 4, D], BF16, 84 * K)
        Wout = AT("Wout", [128, 8, D], BF16, 92 * K)
        gt = [AT("gt%d" % i, [128, 2 * D], F32, 108 * K + i * 8 * K) for i in range(2)]
        Wk = AT("Wk", [128, 8, 2048], BF16, 0)
        GG = 4
        uv_r = Ring([AT("UV%d" % i, [128, GG, 2 * D], BF16, 32 * K + i * 16 * K) for i in range(4)])
        s_sb = AT("s_sb", [128, 2048], F32, 96 * K)
        ut_r = Ring([AT("UT%d" % i, [128, D], BF16, 114 * K + i * 2 * K) for i in range(5)])
        POOL_DOTS = False
        PE_DOTS = False
        TT_DOTS = True
        prod_r = Ring([AT("prod%d" % i, [128, D], F32, 104 * K + i * 4 * K) for i in range(5)])
        wq_t = [AT("wq_t%d" % i, [128, D], F32, 104 * K + i * 4 * K) for i in range(2)]
        kT_t = AT("kT_t", [128, 2, 128], F32, 112 * K)

        ident_f = S.sb("ident_f", [128, 128], F32)
        ident = S.sb("ident", [128, 128], BF16)
        epsb = S.sb("epsb", [128, 1], F32)
        iota16i = S.sb("iota16i", [128, 16], I32)
        iota16 = S.sb("iota16", [128, 16], F32)
        gvec = S.sb("gvec", [128, D], F32)
        gq = S.sb("gq", [128, 256], F32)
        gb = S.sb("gb", [128, 2 * D], F32)
        sinkb = S.sb("sinkb", [128, 8], F32)
        nsinkb = S.sb("nsinkb", [128, 8], F32)
        nB = S.sb("nB", [128, 2], F32)
        bnd = S.sb("bnd", [128, 4], F32)
        esc = S.sb("esc", [128, 8], F32)
        onesb = S.sb("onesb", [128, 1], BF16)
        rope = S.sb("rope_sb", [128, NT, 64], F32)
        swam = S.sb("swam_sb", [128, 384], F32)
        xt_r = Ring([S.sb("xt%d" % i, [128, D], F32) for i in range(2)])
        xn_r = Ring([S.sb("xn%d" % i, [128, D], BF16) for i in range(2)])
        junk = S.sb("junk", [128, D], BF16)
        junk2 = S.sb("junk2", [128, D], BF16)
        st_r = Ring([S.sb("st%d" % i, [128, 4], F32) for i in range(4)])
        cur2[0] = 0
        pj_r = Ring([A2("pj%d" % i, [128, 512], F32) for i in range(4)])
        sq_t = A2("sq_t", [128, 512], F32)
        hs_r = Ring([S.sb("hs%d" % i, [128, 16], F32) for i in range(4)])
        qn_r = Ring([A2("qn%d" % i, [128, 512], BF16) for i in range(4)])
        rt_r = Ring([A2("rt%d" % i, [128, 256], F32) for i in range(4)])
        gl_r = Ring([A2("gl%d" % i, [128, 512], F32) for i in range(4)])
        cur2[0] = 0
        bias_r = Ring([A2("nb%d" % i, [128, 640], F32) for i in range(4)])
        lg_r = Ring([A2("lg%d" % i, [128, 640], F32) for i in range(2)])
        P_r = Ring([A2("P%d" % i, [128, 640], BF16) for i in range(2)])
        PT_r = Ring([A2("PT%d" % i, [128, 640], BF16) for i in range(2)])
        ms_r = Ring([S.sb("ms%d" % i, [128, 4], F32) for i in range(4)])
        rsum_r = Ring([S.sb("rsum%d" % i, [128, 24], F32) for i in range(2)])
        osb = A2("osb", [128, 512], BF16)
        oT_r = Ring([A2("oT%d" % i, [128, 4, 128], BF16) for i in range(4)])
        m1 = A2("m1", [128, 512], F32)
        m2 = A2("m2", [128, 512], F32)
        mg = A2("mg", [128, D], BF16)
        mgT = A2("mgT", [128, 8, 128], BF16)
        xo_r = Ring([S.sb("xo%d" % i, [128, D], F32) for i in range(2)])
        cur2[0] = 0
        hf_b = [A2("hf%d" % i, [128, D], F32) for i in range(2)]
        xp = [xt_r.tiles[0], xt_r.tiles[1], A2("xp2", [128, D], F32)]
        eidi_b = [A2("eidi%d" % i, [128, 128], I32) for i in range(3)]
        esm_b = [A2("esmb%d" % i, [128, 128], F32) for i in range(3)]
        d_ring = Ring([A2("dts%d" % i, [128, 4], F32) for i in range(8)])
        w_ring = Ring([A2("wac%d" % i, [128, 4], F32) for i in range(8)])
        dgg_r = Ring([A2("dgg%d" % i, [128, 8, 128], BF16) for i in range(3)])
        pos = A2("pos", [128, 16], U32)
        pab = A2("pab", [128, 32], U32)
        pabf = A2("pabf", [128, 32], F32)
        i12 = A2("i12", [128, 32], F32)
        hT_b = [A2("hT%d" % i, [128, 8, 128], BF16) for i in range(2)]
        sv_t = [A2("sv%d" % i, [128, 16], F32) for i in range(16)]
        si_t = [A2("si%d" % i, [128, 16], U32) for i in range(16)]
        sif_t = [A2("sif%d" % i, [128, 16], F32) for i in range(16)]
        wk_t = [A2("wk128_%d" % i, [128, 128], F32) for i in range(2)]
        cand = A2("cand", [128, 256], F32)
        cand2 = A2("cand2", [128, 256], F32)
        jk256 = A2("jk256", [128, 256], F32)
        tops_t = [A2("tops%d" % i, [128, 16], F32) for i in range(8)]
        esm = A2("esm", [128, 8, 16], F32)
        zs = A2("zs", [128, 8], F32)
        eidf = A2("eidf", [128, 128], F32)

        PSA = S.ps("PSA", [128, 1024], F32)
        gram_r = None
        PSB = S.ps("PSB", [128, 1024], F32)
        PST = [S.ps("PST%d" % i, [128, 1024], BF16) for i in range(2)]
        PSO = S.ps("PSO", [128, 512], F32)
        PSX = S.ps("PSX", [128, 512], F32)
        pst_r = Ring(PST)
        gram_r = Ring([S.token(PSB.t, "gram0"), S.token(PSB.t, "gram1")])
        acc_r = Ring([PSA, PSB, PSO, PSX])

        S.add("pool", lambda e: e.memset(ident_f[:], 0.0), writes=[ident_f])
        S.add("pool", lambda e: e.affine_select(out=ident_f[:], in_=ident_f[:], pattern=[[-1, 128]],
                                                compare_op=ALU.not_equal, fill=1.0, base=0, channel_multiplier=1),
              reads=[ident_f], writes=[ident_f])
        S.add("dve", lambda e: e.tensor_copy(out=ident[:], in_=ident_f[:]), reads=[ident_f], writes=[ident])
        S.add("pool", lambda e: e.memset(epsb[:], EPS), writes=[epsb])
        S.add("pool", lambda e: e.memset(onesb[:], 1.0), writes=[onesb])
        S.add("pool", lambda e: e.iota(out=iota16i[:], pattern=[[1, 16]], base=0, channel_multiplier=0), writes=[iota16i])
        S.add("dve", lambda e: e.tensor_copy(out=iota16[:], in_=iota16i[:]), reads=[iota16i], writes=[iota16])
        S.dma(rope, rope[:], rope_d, rope_d.t.ap().rearrange("(t p) c -> p t c", p=128))
        S.dma(swam, swam[:], swam_d, swam_d[:, :])

        CONV_ROWS = 256
        NCH = NEXP // CONV_ROWS
        conv_todo = list(range(DEPTH * NCH))
        ptab16_tok = [S.token(ptab16.t, "ptab16_l%d" % i) for i in range(DEPTH)]

        def conv_some(k, after=None):
            for _ in range(k):
                if not conv_todo:
                    return
                c = conv_todo.pop(0)
                S.dma(ptab16_tok[c // NCH], ptab16[c * CONV_ROWS:(c + 1) * CONV_ROWS, :], ptab, ptab[c * CONV_ROWS:(c + 1) * CONV_ROWS, :],
                      eng="pool", group="conv", nobar=True, extra_reads=([after] if after is not None else []))

        def conv_flush(upto_layer):
            while conv_todo and conv_todo[0] // NCH <= upto_layer:
                conv_some(1)

        def transposes(src_t, src_aps, dst_t, dst_ap, n, eng="act"):
            pt = pst_r.next()

            def fn(e):
                for i in range(n):
                    ins = e.transpose(out=pt[:, i * 128:(i + 1) * 128], in_=src_aps[i], identity=ident[:])
                return ins

            S.add("pe", fn, reads=[src_t, ident], writes=[pt])
            src = pt[:, 0:n * 128]
            if len(dst_ap.shape) == 3:
                src = src.rearrange("p (c t) -> p c t", t=128)
            if eng == "act":
                S.add("act", lambda e: e.activation(out=dst_ap, in_=src, func=AF.Copy), reads=[pt], writes=[dst_t])
            else:
                S.add("dve", lambda e: e.tensor_copy(out=dst_ap, in_=src), reads=[pt], writes=[dst_t])

        def rmsnorm_tile(xt, gv, out_t, out_ap, out2_t=None, out2_ap=None):
            st = st_r.next()
            S.add("act", lambda e: e.activation(out=junk[:], in_=xt[:], func=AF.Square, accum_out=st[:, 0:1]),
                  reads=[xt], writes=[junk, st])
            S.add("act", lambda e: e.activation(out=st[:, 1:2], in_=st[:, 0:1], func=AF.Sqrt, bias=epsb[:, 0:1],
                                                scale=1.0 / D), reads=[st, epsb], writes=[st])
            S.add("dve", lambda e: e.reciprocal(out=st[:, 2:3], in_=st[:, 1:2]), reads=[st], writes=[st])
            S.add("dve", lambda e: e.scalar_tensor_tensor(out=out_ap, in0=xt[:], scalar=st[:, 2:3], in1=gv[:],
                                                          op0=ALU.mult, op1=ALU.mult),
                  reads=[xt, st, gv], writes=[out_t])
            if out2_t is not None:
                S.add("dve", lambda e: e.tensor_copy(out=out2_ap, in_=out_ap), reads=[out_t], writes=[out2_t])

        def headnorm(pj, nh, goff, out_t, out_ap3):
            hs = hs_r.next()
            n = nh * 64
            pj3 = pj[:, 0:n].rearrange("p (h d) -> p h d", d=64)
            S.add("dve", lambda e: e.tensor_tensor(out=sq_t[:, 0:n], in0=pj[:, 0:n], in1=pj[:, 0:n], op=ALU.mult),
                  reads=[pj], writes=[sq_t])
            S.add("dve", lambda e: e.tensor_reduce(out=hs[:, 0:nh], in_=sq_t[:, 0:n].rearrange("p (h d) -> p h d", d=64),
                                                   axis=AX.X, op=ALU.add), reads=[sq_t], writes=[hs])
            S.add("act", lambda e: e.activation(out=hs[:, 8:8 + nh], in_=hs[:, 0:nh], func=AF.Sqrt, bias=epsb[:, 0:1],
                                                scale=1.0 / 64), reads=[hs, epsb], writes=[hs])
            S.add("dve", lambda e: e.reciprocal(out=hs[:, 0:nh], in_=hs[:, 8:8 + nh]), reads=[hs], writes=[hs])
            S.add("dve", lambda e: e.tensor_tensor(out=pj3, in0=pj3, in1=hs[:, 0:nh].unsqueeze(2).to_broadcast([128, nh, 64]),
                                                   op=ALU.mult), reads=[pj, hs], writes=[pj])
            S.add("dve", lambda e: e.tensor_tensor(out=out_ap3, in0=pj3,
                                                   in1=gq[:, goff:goff + 64].unsqueeze(1).to_broadcast([128, nh, 64]),
                                                   op=ALU.mult), reads=[pj, gq], writes=[out_t])

        def rotary(pj, nh, tt, out_t, out_ap3):
            n = nh * 64
            pj3 = pj[:, 0:n].rearrange("p (h d) -> p h d", d=64)
            x1 = pj3[:, :, 0:32]
            x2 = pj3[:, :, 32:64]
            cs = rope[:, tt, 0:32].unsqueeze(1).to_broadcast([128, nh, 32])
            sn = rope[:, tt, 32:64].unsqueeze(1).to_broadcast([128, nh, 32])
            r = [rt_r.next() for _ in range(4)]
            rv = [t[:, 0:nh * 32].rearrange("p (h d) -> p h d", d=32) for t in r]
            S.add("dve", lambda e: e.tensor_tensor(out=rv[0], in0=x1, in1=cs, op=ALU.mult), reads=[pj, rope], writes=[r[0]])
            S.add("dve", lambda e: e.tensor_tensor(out=rv[1], in0=x2, in1=sn, op=ALU.mult), reads=[pj, rope], writes=[r[1]])
            S.add("dve", lambda e: e.tensor_tensor(out=rv[2], in0=x1, in1=sn, op=ALU.mult), reads=[pj, rope], writes=[r[2]])
            S.add("dve", lambda e: e.tensor_tensor(out=rv[3], in0=x2, in1=cs, op=ALU.mult), reads=[pj, rope], writes=[r[3]])
            S.add("dve", lambda e: e.tensor_tensor(out=out_ap3[:, :, 0:32], in0=rv[0], in1=rv[1], op=ALU.subtract),
                  reads=[r[0], r[1]], writes=[out_t])
            S.add("dve", lambda e: e.tensor_tensor(out=out_ap3[:, :, 32:64], in0=rv[2], in1=rv[3], op=ALU.add),
                  reads=[r[2], r[3], out_t], writes=[out_t])

        def attn_head(qT_t, qT_ap, kT_t, kT_ap, nk, v_t, v_aps, bias_t, bias_ap, sink_ap, sc, o_ap, rsum, hcol, grp):
            nkk = nk * 128

            def mm(e):
                ins = e.matmul(sc[:, 0:min(512, nkk)], lhsT=qT_ap, rhs=kT_ap[:, 0:min(512, nkk)], start=True, stop=True)
                if nkk > 512:
                    ins = e.matmul(sc[:, 512:nkk], lhsT=qT_ap, rhs=kT_ap[:, 512:nkk], start=True, stop=True)
                return ins

            S.add("pe", mm, reads=[qT_t, kT_t], writes=[sc])
            lg = lg_r.next()
            S.add("dve", lambda e: e.scalar_tensor_tensor(out=lg[:, 0:nkk], in0=sc[:, 0:nkk], scalar=0.125, in1=bias_ap,
                                                          op0=ALU.mult, op1=ALU.add), reads=[sc, bias_t], writes=[lg])
            ms = ms_r.next()
            S.add("dve", lambda e: e.tensor_reduce(out=ms[:, 0:1], in_=lg[:, 0:nkk], axis=AX.X, op=ALU.max),
                  reads=[lg], writes=[ms])
            if sink_ap is not None:
                S.add("dve", lambda e: e.tensor_tensor(out=ms[:, 0:1], in0=ms[:, 0:1], in1=sink_ap, op=ALU.max),
                      reads=[ms, sinkb], writes=[ms])
            S.add("dve", lambda e: e.tensor_scalar(out=ms[:, 1:2], in0=ms[:, 0:1], scalar1=-1.0, scalar2=None, op0=ALU.mult),
                  reads=[ms], writes=[ms])
            P = P_r.next()
            S.add("act", lambda e: e.activation(out=P[:, 0:nkk], in_=lg[:, 0:nkk], func=AF.Exp, bias=ms[:, 1:2], scale=1.0,
                                                accum_out=rsum[:, hcol:hcol + 1]), reads=[lg, ms], writes=[P, rsum])
            if sink_ap is not None:
                S.add("act", lambda e: e.activation(out=ms[:, 2:3], in_=sink_ap, func=AF.Exp, bias=ms[:, 1:2], scale=1.0),
                      reads=[ms, sinkb], writes=[ms])
                S.add("dve", lambda e: e.tensor_tensor(out=rsum[:, hcol:hcol + 1], in0=rsum[:, hcol:hcol + 1], in1=ms[:, 2:3],
                                                       op=ALU.add), reads=[rsum, ms], writes=[rsum])
            PT = PT_r.next()
            transposes(P, [P[:, j * 128:(j + 1) * 128] for j in range(nk)], PT, PT[:, 0:nkk], nk, eng="act")

            def pv(e):
                for j in range(nk):
                    ins = e.matmul(o_ap, lhsT=PT[:, j * 128:(j + 1) * 128], rhs=v_aps[j], start=(j == 0), stop=(j == nk - 1))
                return ins

            S.add("pe", pv, reads=[PT, v_t], writes=[PSO], group=grp)

        for l in range(n_layers):
            xsrc_tok = [x_in] * NT if l == 0 else out_tok
            xsrc = x_in if l == 0 else out_d
            if do_attn:
                S.dma(gvec, gvec[:], norm_mix, norm_mix[l:l + 1, :].to_broadcast([128, D]))
                S.dma(gq, gq[:], qk_norm, qk_norm[l:l + 1, :].to_broadcast([128, 256]))
                S.dma(gb, gb[:], gate_bias, gate_bias[l:l + 1, :].to_broadcast([128, 2 * D]))
                S.dma(sinkb, sinkb[:], swa_sink, swa_sink[l:l + 1, :].to_broadcast([128, 8]))
                S.add("dve", lambda e: e.tensor_reduce(out=bnd[:, 0:4], in_=gq[:, 0:256].rearrange("p (r d) -> p r d", d=64), axis=AX.X,
                                                       op=ALU.max, apply_absolute_value=True), reads=[gq], writes=[bnd])
                S.add("dve", lambda e: e.tensor_tensor(out=nB[:].rearrange("p (a b) -> p a b", b=1),
                                                       in0=bnd[:, 0:4].rearrange("p (a b) -> p a b", b=2)[:, :, 0:1],
                                                       in1=bnd[:, 0:4].rearrange("p (a b) -> p a b", b=2)[:, :, 1:2], op=ALU.mult),
                      reads=[bnd], writes=[nB])
                S.add("dve", lambda e: e.tensor_scalar(out=nB[:], in0=nB[:], scalar1=-8.0, scalar2=None, op0=ALU.mult),
                      reads=[nB], writes=[nB])
                S.add("act", lambda e: e.activation(out=esc[:, 0:8], in_=sinkb[:, 0:8], func=AF.Exp, bias=nB[:, 1:2], scale=1.0),
                      reads=[sinkb, nB], writes=[esc])
                for tt in range(NT):
                    xt = xt_r.next()
                    S.dma(xt, xt[:], xsrc_tok[tt], xsrc[tt * 128:(tt + 1) * 128, :])
                    xn = xn_r.next()
                    rmsnorm_tile(xt, gvec, xn, xn[:])
                    transposes(xn, [xn[:, c * 128:(c + 1) * 128] for c in range(8)], xnT,
                               xnT[:, :, tt * 128:(tt + 1) * 128], 8, eng="act")
                groups = [(0, 512, "naq"), (512, 512, "nak"), (1024, 512, "nav"), (1536, 512, "swq"), (2048, 256, "swkv"),
                          (2304, 512, "g"), (2816, 512, "g"), (3328, 512, "g"), (3840, 512, "g")]
                deferred = []
                ep_q = []

                def wload(gi_):
                    c0_, n_, _k = groups[gi_]
                    w_ = wst[gi_ % 2]
                    S.dma(w_, w_[:, :, 0:n_], w_in, w_in[l, :, c0_:c0_ + n_].rearrange("(c p) n -> p c n", p=128), eng="pool")
                wload(0)
                for gi, (c0, n, kind) in enumerate(groups):
                    w = wst[gi % 2]
                    if gi + 1 < len(groups):
                        wload(gi + 1)

                    for tt in range(NT):
                        acc = acc_r.next()

                        def mm(e, acc=acc, w=w, tt=tt, n=n):
                            for c in range(8):
                                ins = e.matmul(acc[:, 0:n], lhsT=xnT[:, c, tt * 128:(tt + 1) * 128], rhs=w[:, c, 0:n],
                                               start=(c == 0), stop=(c == 7))
                            return ins

                        ptok = T(None, "ptok")
                        S.add("pe", mm, reads=[xnT, w], writes=[acc, ptok])
                        if (gi * NT + tt) % 4 == 0 and conv_todo and conv_todo[0] < NCH:
                            conv_some(1, after=ptok)
                        while deferred:
                            deferred.pop(0)()
                        ep_prev = ep_q[:]
                        del ep_q[:]
                        if kind == "nav":
                            S.add("act", lambda e, acc=acc, tt=tt: e.activation(out=vna[:, tt, :], in_=acc[:, 0:512], func=AF.Copy),
                                  reads=[acc], writes=[vna])
                            for f_ in ep_prev:
                                f_()
                            continue
                        if kind == "g":
                            goff = c0 - 2304
                            gl = gl_r.next()
                            S.add("dve", lambda e, acc=acc, gl=gl, goff=goff: e.tensor_tensor(
                                out=gl[:], in0=acc[:, 0:512], in1=gb[:, goff:goff + 512], op=ALU.add),
                                reads=[acc, gb], writes=[gl])
                            for f_ in ep_prev:
                                f_()
                            S.add("act", lambda e, gl=gl: e.activation(out=gl[:], in_=gl[:], func=AF.Sigmoid),
                                  reads=[gl], writes=[gl])
                            S.dma(gates_tok[tt], gates_d[tt * 128:(tt + 1) * 128, goff:goff + 512], gl, gl[:],
                                  group=("g", l, tt))
                            continue
                        pj = pj_r.next()
                        S.add("act", lambda e, acc=acc, pj=pj, n=n: e.activation(out=pj[:, 0:n], in_=acc[:, 0:n], func=AF.Copy),
                              reads=[acc], writes=[pj])
                        for f_ in ep_prev:
                            f_()

                        def ep(kind=kind, pj=pj, tt=tt):
                            if kind in ("naq", "nak"):
                                qn = qn_r.next()
                                headnorm(pj, 8, 0 if kind == "naq" else 64, qn, qn[:].rearrange("p (h d) -> p h d", d=64))
                                dst = qTna if kind == "naq" else kTna
                                deferred.append(lambda qn=qn, dst=dst, tt=tt: transposes(
                                    qn, [qn[:, c * 128:(c + 1) * 128] for c in range(4)], dst,
                                    dst[:, :, tt * 128:(tt + 1) * 128], 4, eng="act"))
                            elif kind == "swq":
                                qn = qn_r.next()
                                headnorm(pj, 8, 128, pj, pj[:].rearrange("p (h d) -> p h d", d=64))
                                rotary(pj, 8, tt, qn, qn[:].rearrange("p (h d) -> p h d", d=64))
                                deferred.append(lambda qn=qn, tt=tt: transposes(
                                    qn, [qn[:, c * 128:(c + 1) * 128] for c in range(4)], qTsw,
                                    qTsw[:, :, tt * 128:(tt + 1) * 128], 4, eng="act"))
                            elif kind == "swkv":
                                qn = qn_r.next()
                                S.add("act", lambda e, pj=pj, tt=tt: e.activation(out=vsw[:, tt, :], in_=pj[:, 128:256], func=AF.Copy),
                                      reads=[pj], writes=[vsw])
                                headnorm(pj, 2, 192, pj, pj[:, 0:128].rearrange("p (h d) -> p h d", d=64))
                                rotary(pj, 2, tt, qn, qn[:, 0:128].rearrange("p (h d) -> p h d", d=64))
                                S.add("act", lambda e, qn=qn: e.activation(out=qn[:, 128:192], in_=qn[:, 64:128], func=AF.Copy),
                                      reads=[qn], writes=[qn])
                                S.add("act", lambda e, qn=qn: e.activation(out=qn[:, 192:256], in_=qn[:, 0:64], func=AF.Copy),
                                      reads=[qn], writes=[qn])
                                deferred.append(lambda qn=qn, tt=tt: transposes(
                                    qn, [qn[:, 0:128], qn[:, 128:256]], kTsw, kTsw[:, :, tt * 128:(tt + 1) * 128], 2, eng="act"))
                        ep_q.append(ep)
                for f_ in ep_q[:]:
                    f_()
                del ep_q[:]
                while deferred:
                    deferred.pop(0)()
                S.barrier()
                if dbg and l == 0:
                    for nm, tl in [("xnT", xnT), ("qTna", qTna), ("kTna", kTna), ("vna", vna), ("qTsw", qTsw), ("kTsw", kTsw), ("vsw", vsw)]:
                        shp = list(tl.t.shape)
                        dd = S.dram("dbg_" + nm, shp, F32, "ExternalOutput")
                        S.dma(dd, dd.t.ap(), tl, tl.t.ap(), eng="pool")
                    S.barrier()
                S.dma(Wna, Wna[:], w_bna, w_bna[l].rearrange("(c p) n -> p c n", p=128), eng="pool")
                S.dma(Wsw, Wsw[:], w_bsw, w_bsw[l].rearrange("(c p) n -> p c n", p=128), eng="pool")
                S.dma(Wout, Wout[:, 0:4, :], w_out, w_out[l, 0:512, :].rearrange("(c p) n -> p c n", p=128), eng="pool",
                      group=("wout", l))
                S.dma(Wout, Wout[:, 4:8, :], w_out, w_out[l, 512:1024, :].rearrange("(c p) n -> p c n", p=128), eng="pool",
                      group=("wout", l))
                sc_r = Ring([PSA, PSB])
                sums_tok = [S.token(PSA.t, "sums0"), S.token(PSB.t, "sums1")]
                PSY = PST[1]
                psy = PST[1][:, :].bitcast(F32)
                pst_r.tiles = [PST[0]]
                jobs = [(tt, br, h) for tt in range(NT) for br in range(2) for h in range(8)]
                NJ = len(jobs)
                jst = {}
                tst = {}
                later = []

                def stA(j, l=l):
                    tt, br, h = jobs[j]
                    if br == 0 and h == 0:
                        g_t = gt[tt % 2]
                        S.dma(g_t, g_t[:], gates_tok[tt], gates_d[tt * 128:(tt + 1) * 128, :])
                        xt = xt_r.next()
                        S.dma(xt, xt[:], xsrc_tok[tt], xsrc[tt * 128:(tt + 1) * 128, :])
                        tst[tt] = dict(g_t=g_t, xt=xt, oT=[None, None])
                    if h == 0:
                        tst[tt]["rsum%d" % br] = rsum_r.next()
                    sc = sc_r.next()
                    pr, hp = h // 2, (h % 2) * 64
                    d = dict(sc=sc, rsum=tst[tt]["rsum%d" % br])
                    if br == 0:
                        ws = min(max(tt - 2, 0), 11)
                        ty = {0: 0, 1: 1, 14: 3, 15: 4}.get(tt, 2)
                        nk = 5
                        bt = bias_r.next()
                        S.dma(bt, bt[:], nab, nab[l, ty, h, :, :])
                        qT_t, qT_ap = qTna, qTna[hp:hp + 64, pr, tt * 128:(tt + 1) * 128]
                        kT_t, kT_ap = kTna, kTna[hp:hp + 64, pr, ws * 128:(ws + 5) * 128]
                        d.update(nk=5, v_t=vna, v_aps=[vna[:, ws + jj, h * 64:(h + 1) * 64] for jj in range(5)],
                                 bias_t=bt, bias_ap=bt[:, 0:640], sink=None, negb=nB[:, 0:1])
                    else:
                        lo = max(tt - 1, 0)
                        hi = min(tt + 1, 15)
                        nk = hi - lo + 1
                        kvh = h // 4
                        slot = 0 if (hp // 64) == kvh else 1
                        moff = 128 if tt == 0 else 0
                        qT_t, qT_ap = qTsw, qTsw[hp:hp + 64, pr, tt * 128:(tt + 1) * 128]
                        kT_t, kT_ap = kTsw, kTsw[hp:hp + 64, slot, lo * 128:(hi + 1) * 128]
                        d.update(nk=nk, v_t=vsw, v_aps=[vsw[:, lo + jj, kvh * 64:(kvh + 1) * 64] for jj in range(nk)],
                                 bias_t=swam, bias_ap=swam[:, moff:moff + nk * 128], sink=sinkb[:, h:h + 1], negb=nB[:, 1:2])
                    nkk = nk * 128

                    def mm(e):
                        for kt in range(nk):
                            ins = e.matmul(sc[:, kt * 128:(kt + 1) * 128], lhsT=kT_ap[:, kt * 128:(kt + 1) * 128], rhs=qT_ap,
                                           start=True, stop=True)
                        return ins
                    ptok = T(None, "ptok")
                    S.add("pe", mm, reads=[qT_t, kT_t], writes=[sc, ptok])
                    if h == 0 and conv_todo and conv_todo[0] < NCH:
                        conv_some(1, after=ptok)
                    jst[j] = d

                def stB(j):
                    tt, br, h = jobs[j]
                    d = jst[j]
                    sc, rsum, sink_ap = d["sc"], d["rsum"], d["sink"]
                    nkk = d["nk"] * 128
                    lg = lg_r.next()
                    S.add("dve", lambda e: e.scalar_tensor_tensor(out=lg[:, 0:nkk], in0=sc[:, 0:nkk], scalar=0.125, in1=d["bias_ap"],
                                                                  op0=ALU.mult, op1=ALU.add), reads=[sc, d["bias_t"]], writes=[lg])
                    P = P_r.next()
                    S.add("act", lambda e: e.activation(out=P[:, 0:nkk], in_=lg[:, 0:nkk], func=AF.Exp, bias=d["negb"], scale=1.0),
                          reads=[lg, nB], writes=[P])
                    d["PT"] = P

                def stD(j, l=l):
                    tt, br, h = jobs[j]
                    d = jst.pop(j)
                    PT, nk, v_aps = d["PT"], d["nk"], d["v_aps"]

                    sc = d["sc"]
                    stok = sums_tok[0] if sc is PSA else sums_tok[1]

                    def pv(e):
                        for jj in range(nk):
                            e.matmul(PSO[:, h * 64:(h + 1) * 64], lhsT=PT[:, jj * 128:(jj + 1) * 128], rhs=v_aps[jj],
                                     start=(jj == 0), stop=(jj == nk - 1))
                        for jj in range(nk):
                            ins = e.matmul(sc[:, 640 + h:641 + h], lhsT=PT[:, jj * 128:(jj + 1) * 128], rhs=onesb[:, 0:1],
                                           start=(jj == 0), stop=(jj == nk - 1))
                        return ins
                    S.add("pe", pv, reads=[PT, d["v_t"], onesb], writes=[PSO, stok], group=("pso", l, tt, br))
                    assert (sc is PSA) == (h % 2 == 0)
                    if h != 7:
                        return
                    rsum = d["rsum"]
                    ts_ = tst[tt]
                    steps = []

                    def e1():
                        r2 = rsum[:, 0:8].rearrange("p (a b) -> p a b", b=2)
                        S.add("dve", lambda e: e.tensor_copy(out=r2[:, :, 0], in_=PSA[:, 640:648].rearrange("p (a b) -> p a b", b=2)[:, :, 0]),
                              reads=[sums_tok[0]], writes=[rsum])
                        S.add("dve", lambda e: e.tensor_copy(out=r2[:, :, 1], in_=PSB[:, 640:648].rearrange("p (a b) -> p a b", b=2)[:, :, 1]),
                              reads=[sums_tok[1], rsum], writes=[rsum])
                        if br == 1:
                            S.add("dve", lambda e: e.tensor_tensor(out=rsum[:, 0:8], in0=rsum[:, 0:8], in1=esc[:, 0:8], op=ALU.add),
                                  reads=[rsum, esc], writes=[rsum])
                        S.add("dve", lambda e: e.reciprocal(out=rsum[:, 8:16], in_=rsum[:, 0:8]), reads=[rsum], writes=[rsum])
                        S.add("dve", lambda e: e.tensor_tensor(
                            out=osb[:].rearrange("p (h d) -> p h d", d=64), in0=PSO[:].rearrange("p (h d) -> p h d", d=64),
                            in1=rsum[:, 8:16].unsqueeze(2).to_broadcast([128, 8, 64]), op=ALU.mult),
                            reads=[PSO, rsum], writes=[osb])

                    def e2():
                        oT = oT_r.next()
                        transposes(osb, [osb[:, c * 128:(c + 1) * 128] for c in range(4)], oT, oT[:], 4, eng="act")
                        ts_["oT"][br] = oT
                    e1()
                    steps += [e2]
                    if br == 1:
                        g_t, xt = ts_["g_t"], ts_["xt"]

                        def m_mm(n):
                            def mmb(e):
                                for (W, oTb, acc) in ((Wna, ts_["oT"][0], PSX[:, :]), (Wsw, ts_["oT"][1], psy)):
                                    for c in range(4):
                                        ins = e.matmul(acc, lhsT=oTb[:, c, :], rhs=W[:, c, n * 512:(n + 1) * 512],
                                                       start=(c == 0), stop=(c == 3))
                                return ins
                            S.add("pe", mmb, reads=[Wna, Wsw, ts_["oT"][0], ts_["oT"][1]], writes=[PSX, PSY])

                        def m_dve(n):
                            S.add("dve", lambda e: e.tensor_tensor(out=m1[:], in0=PSX[:], in1=g_t[:, n * 512:(n + 1) * 512], op=ALU.mult),
                                  reads=[PSX, g_t], writes=[m1])
                            S.add("dve", lambda e: e.tensor_tensor(out=m2[:], in0=psy, in1=g_t[:, D + n * 512:D + (n + 1) * 512], op=ALU.mult),
                                  reads=[PSY, g_t], writes=[m2])
                            S.add("dve", lambda e: e.tensor_tensor(out=mg[:, n * 512:(n + 1) * 512], in0=m1[:], in1=m2[:], op=ALU.add),
                                  reads=[m1, m2, mg], writes=[mg])

                        def m_tr():
                            transposes(mg, [mg[:, c * 128:(c + 1) * 128] for c in range(8)], mgT, mgT[:], 8, eng="act")
                            ts_["xo"] = xo_r.next()

                        def o_mm(n):
                            acc_t, acc = (PSX, PSX[:, :]) if n == 0 else (PSY, psy)

                            def mmo(e):
                                for c in range(8):
                                    ins = e.matmul(acc, lhsT=mgT[:, c, :], rhs=Wout[:, c, n * 512:(n + 1) * 512],
                                                   start=(c == 0), stop=(c == 7))
                                return ins
                            S.add("pe", mmo, reads=[Wout, mgT], writes=[acc_t])

                        def o_dve(n):
                            acc_t, acc = (PSX, PSX[:, :]) if n == 0 else (PSY, psy)
                            xo = ts_["xo"]
                            S.add("dve", lambda e: e.tensor_tensor(out=xo[:, n * 512:(n + 1) * 512], in0=acc,
                                                                   in1=xt[:, n * 512:(n + 1) * 512], op=ALU.add),
                                  reads=[acc_t, xt, xo], writes=[xo])

                        def fin():
                            S.dma(out_tok[tt], out_d[tt * 128:(tt + 1) * 128, :], ts_["xo"], ts_["xo"][:])
                            del tst[tt]
                        nop = None
                        steps += [lambda: m_mm(0), lambda: m_dve(0), lambda: m_mm(1), lambda: m_dve(1), m_tr,
                                  lambda: o_mm(0), lambda: o_mm(1), lambda: o_dve(0), lambda: (o_dve(1), fin())]
                    for i_, fn_ in enumerate(steps):
                        if fn_ is not None:
                            later.append((j + 1 + i_, fn_))
                    later.sort(key=lambda t_: t_[0])

                stA(0)
                stA(1)
                stB(0)
                for j in range(NJ):
                    if j + 2 < NJ:
                        stA(j + 2)
                    if j + 1 < NJ:
                        stB(j + 1)
                    stD(j)
                    while later and later[0][0] <= j:
                        later.pop(0)[1]()
                while later:
                    later.pop(0)[1]()
                pst_r.tiles = PST
                S.barrier()
            elif l == 0:
                for tt in range(NT):
                    xt = xt_r.next()
                    S.dma(xt, xt[:], x_in, x_in[tt * 128:(tt + 1) * 128, :])
                    S.dma(out_tok[tt], out_d[tt * 128:(tt + 1) * 128, :], xt, xt[:])
                S.barrier()
            if do_peer:
                S.dma(gvec, gvec[:], norm_ffn, norm_ffn[l:l + 1, :].to_broadcast([128, D]))
                S.dma(kT_t, kT_t[:], keysT, keysT[l].rearrange("p c n -> c p n"))
                for hp in range(16):
                    wq = wq_t[hp % 2]
                    S.dma(wq, wq[:], wqT, wqT[l, hp * 128:(hp + 1) * 128, :])
                    for half in range(2):
                        acc = acc_r.next()

                        def mmk(e, acc=acc, wq=wq, hp=hp, half=half):
                            for j in range(4):
                                dch = half * 4 + j
                                ins = e.matmul(acc[:, j * 128:(j + 1) * 128], lhsT=wq[:, dch * 128:(dch + 1) * 128],
                                               rhs=kT_t[:, hp % 2, :], start=True, stop=True)
                            return ins
                        S.add("pe", mmk, reads=[wq, kT_t], writes=[acc])
                        S.add("act", lambda e, acc=acc, hp=hp, half=half: e.activation(
                            out=Wk[:, half * 4:half * 4 + 4, hp * 128:(hp + 1) * 128],
                            in_=acc[:, 0:512].rearrange("p (c n) -> p c n", n=128), func=AF.Copy), reads=[acc], writes=[Wk])
                NG = 128 // GG
                hb_of = {}

                def s1_chunks(tt, l=l):
                    ch = []
                    xt = xp[tt % 3]
                    hfb = hf_b[tt % 2]

                    def a():
                        S.dma(xt, xt[:], out_tok[tt], out_d[tt * 128:(tt + 1) * 128, :])
                        xn = xn_r.next()
                        hb_of[tt] = xn
                        rmsnorm_tile(xt, gvec, hfb, hfb[:], xn, xn[:])
                        hT = hT_b[tt % 2]
                        transposes(xn, [xn[:, c * 128:(c + 1) * 128] for c in range(8)], hT, hT[:], 8, eng="act")
                        for half in range(2):
                            for q2 in range(2):
                                q4 = half * 2 + q2
                                col = q2 * 512

                                def mms(e, col=col, q4=q4, hT=hT):
                                    for c in range(8):
                                        ins = e.matmul(PSA[:, col:col + 512], lhsT=hT[:, c, :], rhs=Wk[:, c, q4 * 512:(q4 + 1) * 512],
                                                       start=(c == 0), stop=(c == 7))
                                    return ins
                                S.add("pe", mms, reads=[hT, Wk], writes=[PSA], group=("s", l, tt, half))
                            S.add("act", lambda e, half=half: e.activation(out=s_sb[:, half * 1024:(half + 1) * 1024], in_=PSA[:], func=AF.Copy),
                                  reads=[PSA], writes=[s_sb], group=("ssb", l, tt))
                    ch.append(a)

                    def oplist():
                        ops = []

                        def A_(*args, **kw):
                            ops.append(lambda: S.add(*args, **kw))
                        return ops, A_

                    def b(hp):
                        ops, A_ = oplist()
                        sl = s_sb[:, hp * 128:(hp + 1) * 128]
                        svt, sit, sft, wk = sv_t[hp], si_t[hp], sif_t[hp], wk_t[hp % 2]
                        A_("dve", lambda e: e.max(out=svt[:, 0:8], in_=sl), reads=[s_sb], writes=[svt])
                        A_("dve", lambda e: e.max_index(out=sit[:, 0:8], in_max=svt[:, 0:8], in_values=sl),
                           reads=[s_sb, svt], writes=[sit])
                        A_("dve", lambda e: e.match_replace(out=wk[:], in_to_replace=svt[:, 0:8], in_values=sl,
                                                            imm_value=-1e30), reads=[s_sb, svt], writes=[wk])
                        A_("dve", lambda e: e.max(out=svt[:, 8:16], in_=wk[:]), reads=[wk, svt], writes=[svt])
                        A_("dve", lambda e: e.max_index(out=sit[:, 8:16], in_max=svt[:, 8:16], in_values=wk[:]),
                           reads=[wk, svt, sit], writes=[sit])
                        A_("dve", lambda e: e.tensor_copy(out=sft[:], in_=sit[:]), reads=[sit], writes=[sft])
                        return ops

                    def c(h):
                        ops, A_ = oplist()
                        tp = tops_t[h]
                        c3 = cand[:].rearrange("p (a b) -> p a b", b=16)
                        A_("dve", lambda e: e.tensor_tensor(
                            out=c3, in0=sv_t[2 * h][:, :].unsqueeze(2).to_broadcast([128, 16, 16]),
                            in1=sv_t[2 * h + 1][:, :].unsqueeze(1).to_broadcast([128, 16, 16]), op=ALU.add),
                            reads=[sv_t[2 * h], sv_t[2 * h + 1]], writes=[cand])
                        A_("dve", lambda e: e.max(out=tp[:, 0:8], in_=cand[:]), reads=[cand], writes=[tp])
                        A_("dve", lambda e: e.max_index(out=pos[:, 0:8], in_max=tp[:, 0:8], in_values=cand[:]),
                           reads=[cand, tp], writes=[pos])
                        A_("dve", lambda e: e.match_replace(out=cand2[:], in_to_replace=tp[:, 0:8], in_values=cand[:],
                                                            imm_value=-1e30), reads=[cand, tp], writes=[cand2])
                        A_("dve", lambda e: e.max(out=tp[:, 8:16], in_=cand2[:]), reads=[cand2, tp], writes=[tp])
                        A_("dve", lambda e: e.max_index(out=pos[:, 8:16], in_max=tp[:, 8:16], in_values=cand2[:]),
                           reads=[cand2, tp, pos], writes=[pos])
                        A_("dve", lambda e: e.tensor_single_scalar(out=pab[:, 0:16], in_=pos[:], scalar=4, op=ALU.logical_shift_right),
                           reads=[pos], writes=[pab])
                        A_("dve", lambda e: e.tensor_single_scalar(out=pab[:, 16:32], in_=pos[:], scalar=15, op=ALU.bitwise_and),
                           reads=[pos, pab], writes=[pab])
                        A_("dve", lambda e: e.tensor_copy(out=pabf[:], in_=pab[:]), reads=[pab], writes=[pabf])
                        for w_, (srcrow, col0) in enumerate(((2 * h, 0), (2 * h + 1, 16))):
                            oh = jk256[:].rearrange("p (k a) -> p k a", a=16)
                            A_("dve", lambda e, col0=col0, oh=oh: e.tensor_tensor(
                                out=oh, in0=iota16[:].unsqueeze(1).to_broadcast([128, 16, 16]),
                                in1=pabf[:, col0:col0 + 16].unsqueeze(2).to_broadcast([128, 16, 16]), op=ALU.is_equal),
                                reads=[iota16, pabf], writes=[jk256])
                            A_("dve", lambda e, srcrow=srcrow, oh=oh: e.tensor_tensor(
                                out=oh, in0=oh, in1=sif_t[srcrow][:, :].unsqueeze(1).to_broadcast([128, 16, 16]), op=ALU.mult),
                                reads=[jk256, sif_t[srcrow]], writes=[jk256])
                            A_("dve", lambda e, w_=w_, oh=oh: e.tensor_reduce(out=i12[:, w_ * 16:(w_ + 1) * 16], in_=oh, axis=AX.X, op=ALU.add),
                               reads=[jk256, i12], writes=[i12])
                        A_("dve", lambda e: e.scalar_tensor_tensor(out=eidf[:, h * 16:(h + 1) * 16], in0=i12[:, 0:16], scalar=128.0,
                                                                   in1=i12[:, 16:32], op0=ALU.mult, op1=ALU.add),
                           reads=[i12, eidf], writes=[eidf])
                        ms = ms_r.next()
                        A_("dve", lambda e: e.tensor_scalar(out=ms[:, 0:1], in0=tp[:, 0:1], scalar1=-1.0, scalar2=None,
                                                            op0=ALU.mult), reads=[tp], writes=[ms])
                        A_("act", lambda e: e.activation(out=esm[:, h, :], in_=tp[:, :], func=AF.Exp, bias=ms[:, 0:1],
                                                         scale=1.0, accum_out=zs[:, h:h + 1]),
                           reads=[tp, ms], writes=[esm, zs])
                        return ops

                    def rr(*lists):
                        out_, lists = [], [list(x) for x in lists]
                        while any(lists):
                            for x in lists:
                                if x:
                                    out_.append(x.pop(0))
                        return out_

                    ch += rr(b(0), b(1))
                    for h in range(8):
                        if h < 7:
                            ch += rr(c(h), b(2 * h + 2), b(2 * h + 3))
                        else:
                            ch += c(h)

                    def d():
                        eb = esm_b[tt % 3]
                        ei = eidi_b[tt % 3]
                        S.add("dve", lambda e: e.reciprocal(out=zs[:], in_=zs[:]), reads=[zs], writes=[zs])
                        S.add("dve", lambda e: e.tensor_tensor(out=eb[:].rearrange("p (h k) -> p h k", k=16), in0=esm[:],
                                                               in1=zs[:].unsqueeze(2).to_broadcast([128, 8, 16]),
                                                               op=ALU.mult), reads=[esm, zs], writes=[eb])
                        S.add("dve", lambda e: e.tensor_scalar(out=eidf[:], in0=eidf[:], scalar1=float(NEXP - 1), scalar2=float(l * NEXP),
                                                               op0=ALU.min, op1=ALU.add), reads=[eidf], writes=[eidf])
                        S.add("dve", lambda e: e.tensor_copy(out=ei[:], in_=eidf[:]), reads=[eidf], writes=[ei])
                    ch.append(d)
                    return ch

                for cfn in s1_chunks(0):
                    cfn()
                conv_flush(l)
                NGRP = 128 // GG
                NG_ALL = NT * NGRP
                uv_of = {}

                def stG(G, l=l):
                    tt, g = divmod(G, NGRP)
                    UV = uv_r.next()
                    uv_of[G] = UV
                    ei = eidi_b[tt % 3]
                    for j in range(GG):
                        hk = g * GG + j
                        S.add("pool", lambda e, UV=UV, j=j, hk=hk, ei=ei: e.indirect_dma_start(
                            out=UV[:, j, :], out_offset=None, in_=ptab16[:, :],
                            in_offset=bass.IndirectOffsetOnAxis(ap=ei[:, hk:hk + 1], axis=0)),
                            reads=[ei, ptab16_tok[l]], writes=[UV], dma=True, group=("UV", l, G))

                def stDots(G, l=l):
                    tt, g = divmod(G, NGRP)
                    UV = uv_of[G]
                    hfb = hf_b[tt % 2]
                    dt_ = d_ring.next()
                    wa = w_ring.next()
                    uv_of[G] = (UV, wa)
                    if PE_DOTS:
                        hT = hT_b[tt % 2]
                        gram = gram_r.next()
                        gcol = 0 if gram.name == "gram0" else 512
                        UTs = [ut_r.next() for _ in range(GG)]

                        def Tj(j):
                            transposes(UV, [UV[:, j, c * 128:(c + 1) * 128] for c in range(8)], UTs[j], UTs[j][:], 8, eng="act")

                        def MMj(j):
                            def mmg(e):
                                for c in range(8):
                                    ins = e.matmul(PSB[:, gcol + j * 128:gcol + (j + 1) * 128], lhsT=hT[:, c, :],
                                                   rhs=UTs[j][:, c * 128:(c + 1) * 128], start=(c == 0), stop=(c == 7))
                                return ins
                            S.add("pe", mmg, reads=[hT, UTs[j]], writes=[gram], group=("gram", l, G))
                        Tj(0); Tj(1); MMj(0); Tj(2); MMj(1); Tj(3); MMj(2); MMj(3)
                        g3 = PSB[:, gcol:gcol + 512].rearrange("p (j t) -> p j t", t=128)
                        S.add("dve", lambda e: e.tensor_tensor(out=gmul[:].rearrange("p (j t) -> p j t", t=128), in0=g3,
                                                               in1=ident_f[:].unsqueeze(1).to_broadcast([128, GG, 128]), op=ALU.mult),
                              reads=[gram, ident_f], writes=[gmul])
                        S.add("dve", lambda e: e.tensor_reduce(out=dt_[:, 0:GG], in_=gmul[:].rearrange("p (j t) -> p j t", t=128),
                                                               axis=AX.X, op=ALU.add), reads=[gmul], writes=[dt_])
                        S.add("act", lambda e: e.activation(out=wa[:, 0:GG], in_=dt_[:, 0:GG], func=AF.Gelu),
                              reads=[dt_], writes=[wa])
                        return
                    if TT_DOTS:
                        hb = hb_of[tt]
                        for j in range(GG):
                            pr_ = prod_r.next()
                            S.add("dve", lambda e, UV=UV, j=j, pr_=pr_: e.tensor_tensor(out=pr_[:], in0=UV[:, j, 0:D], in1=hb[:], op=ALU.mult),
                                  reads=[UV, hb], writes=[pr_])
                            S.add("act", lambda e, j=j, pr_=pr_: e.activation(out=pr_[:], in_=pr_[:], func=AF.Copy, accum_out=dt_[:, j:j + 1]),
                                  reads=[pr_], writes=[dt_], group=("dt", l, G))
                            for _ in range(2):
                                if pend_ops:
                                    pend_ops.pop(0)()
                        S.add("act", lambda e: e.activation(out=wa[:, 0:GG], in_=dt_[:, 0:GG], func=AF.Gelu),
                              reads=[dt_], writes=[wa])
                        return
                    for j in range(GG):
                        if j == GG - 1 and POOL_DOTS:
                            pr_ = prod_r.next()
                            S.add("pool", lambda e, UV=UV, j=j, pr_=pr_: e.tensor_tensor(out=pr_[:], in0=UV[:, j, 0:D], in1=hfb[:], op=ALU.mult),
                                  reads=[UV, hfb], writes=[pr_])
                            S.add("act", lambda e, j=j, pr_=pr_: e.activation(out=junk2[:], in_=pr_[:], func=AF.Copy, accum_out=dt_[:, j:j + 1]),
                                  reads=[pr_], writes=[dt_], group=("dt", l, G))
                            continue
                        S.add("dve", lambda e, UV=UV, j=j: e.scalar_tensor_tensor(
                            out=junk[:], in0=UV[:, j, 0:D], scalar=1.0, in1=hfb[:], op0=ALU.mult, op1=ALU.mult,
                            accum_out=dt_[:, j:j + 1]), reads=[UV, hfb], writes=[dt_], group=("dt", l, G))
                    S.add("act", lambda e: e.activation(out=wa[:, 0:GG], in_=dt_[:, 0:GG], func=AF.Gelu),
                          reads=[dt_], writes=[wa])

                def stMM(G, l=l):
                    tt, g = divmod(G, NGRP)
                    UV, wa = uv_of.pop(G)
                    eb = esm_b[tt % 3]
                    dg = dgg_r.next()
                    S.add("dve", lambda e: e.tensor_tensor(out=wa[:, 0:GG], in0=wa[:, 0:GG],
                                                           in1=eb[:, g * GG:(g + 1) * GG], op=ALU.mult), reads=[wa, eb], writes=[wa])
                    S.add("dve", lambda e: e.tensor_tensor(
                        out=dg[:, 0:GG, :], in0=ident_f[:].unsqueeze(1).to_broadcast([128, GG, 128]),
                        in1=wa[:, 0:GG].unsqueeze(2).to_broadcast([128, GG, 128]), op=ALU.mult),
                        reads=[ident_f, wa], writes=[dg])

                    def mmv(e):
                        for j in range(GG):
                            hk = g * GG + j
                            e.matmul(PSO[:, 0:512], lhsT=dg[:, j, :], rhs=UV[:, j, D:D + 512], start=(hk == 0), stop=(hk == 127))
                            ins = e.matmul(PSX[:, 0:512], lhsT=dg[:, j, :], rhs=UV[:, j, D + 512:2 * D], start=(hk == 0), stop=(hk == 127))
                        return ins
                    gtok = T(None, "gtok")
                    S.add("pe", mmv, reads=[dg, UV], writes=[PSO, PSX, gtok], group=("pv", l, tt))
                    gtoks.append(gtok)
                    if G % 6 == 0 and conv_todo and len(gtoks) > 3:
                        conv_some(1, after=gtoks[-4])
                    if g == NGRP - 1:
                        xt = xp[tt % 3]
                        xo = xo_r.next()
                        S.add("dve", lambda e: e.tensor_tensor(out=xo[:, 0:512], in0=PSO[:], in1=xt[:, 0:512], op=ALU.add),
                              reads=[PSO, xt], writes=[xo])
                        S.add("dve", lambda e: e.tensor_tensor(out=xo[:, 512:1024], in0=PSX[:], in1=xt[:, 512:1024], op=ALU.add),
                              reads=[PSX, xt, xo], writes=[xo])
                        S.dma(out_tok[tt], out_d[tt * 128:(tt + 1) * 128, :], xo, xo[:])

                LOOK = 3
                gtoks = []
                pend_ops = []
                for G in range(min(LOOK, NG_ALL)):
                    stG(G)
                stDots(0)
                for G in range(NG_ALL):
                    tt, g = divmod(G, NGRP)
                    if g == 0 and tt + 1 < NT:
                        assert not pend_ops
                        pend_ops.extend(s1_chunks(tt + 1))
                    if G + LOOK < NG_ALL:
                        stG(G + LOOK)
                    if G + 1 < NG_ALL:
                        stDots(G + 1)
                    stMM(G)
                    for _ in range(2):
                        if pend_ops:
                            pend_ops.pop(0)()
                    if g == 27:
                        while pend_ops:
                            pend_ops.pop(0)()
                S.barrier()
        S.emit(final_wait_tiles=out_tok)
    return nc


def _na_bias_tables(rpb):
    L = rpb.shape[0]
    out = np.empty((L, 5, 8, 128, 640), np.float32)
    q = np.arange(128)
    kk = np.arange(640)
    for ty, (tt, ws) in enumerate([(0, 0), (1, 0), (2, 0), (14, 11), (15, 11)]):
        qr = 2 * tt + q // 64
        qc = q % 64
        kr = 2 * ws + kk // 64
        kc = kk % 64
        rs = np.clip(qr - 4, 0, 24)
        cs = np.clip(qc - 8, 0, 48)
        vr = (kr[None, :] >= rs[:, None]) & (kr[None, :] < rs[:, None] + 8)
        vc = (kc[None, :] >= cs[:, None]) & (kc[None, :] < cs[:, None] + 16)
        valid = vr & vc
        dr = np.clip(kr[None, :] - qr[:, None] + 7, 0, 14)
        dc = np.clip(kc[None, :] - qc[:, None] + 15, 0, 30)
        g = rpb[:, :, dr, dc]
        out[:, ty] = np.where(valid[None, None], g, np.float32(NEG))
    out = out.reshape(L, 5, 8, 128, 5, 128).transpose(0, 1, 2, 5, 4, 3)
    return np.ascontiguousarray(out).reshape(L, 5, 8, 128, 640)


def _const_tables():
    half = 32
    inv = (10000.0 ** (-np.arange(half, dtype=np.float32) / half)).astype(np.float32)
    ang = np.arange(SEQ, dtype=np.float32)[:, None] * inv[None, :]
    rope = np.concatenate([np.cos(ang), np.sin(ang)], axis=1).astype(np.float32)
    q = np.arange(128)[:, None]
    c = np.arange(384)[None, :]
    swam = np.where(np.abs(q + 128 - c) <= 128, 0.0, NEG).astype(np.float32)
    swam = np.ascontiguousarray(swam.reshape(128, 3, 128).transpose(2, 1, 0)).reshape(128, 384)
    return rope, swam


_NC_CACHE = {}


def kernel(x, norm_mix, w_in, gate_bias, qk_norm, na_rpb, swa_sink, w_branch_na, w_branch_swa,
           w_out, norm_ffn, peer_query, peer_sub_keys, peer_down, peer_up):
    f = lambda a: np.ascontiguousarray(np.asarray(a, dtype=np.float32))
    x = f(x)
    rope, swam = _const_tables()
    shared = {
        "norm_mix": f(norm_mix), "norm_ffn": f(norm_ffn),
        "gate_bias": f(gate_bias).reshape(DEPTH, 2 * D),
        "qk_norm": f(qk_norm).reshape(DEPTH, 256),
        "swa_sink": f(swa_sink),
        "w_in": f(w_in), "w_bna": f(w_branch_na), "w_bsw": f(w_branch_swa), "w_out": f(w_out),
        "nab": _na_bias_tables(f(na_rpb)),
        "rope": rope, "swam": swam,
        "wqT": np.ascontiguousarray(np.transpose(f(peer_query), (0, 2, 1))),
        "keysT": np.ascontiguousarray(np.transpose(f(peer_sub_keys), (0, 1, 3, 2))),
        "ptab": np.concatenate([f(peer_down).reshape(DEPTH * NEXP, D), f(peer_up).reshape(DEPTH * NEXP, D)], axis=1),
    }
    if "nc" not in _NC_CACHE:
        _NC_CACHE["nc"] = build_program()
    nc = _NC_CACHE["nc"]
    in_maps = []
    for b in range(8):
        m = dict(shared)
        m["x"] = x[b]
        in_maps.append(m)
    res = run_bass_kernel_spmd(nc, in_maps, core_ids=list(range(8)))
    return np.stack([np.asarray(r["out"], dtype=np.float32) for r in res.results], axis=0)
```
